# Optimizing a Trainium2 kernel written in Bass

```python
import math
import jax
import jax.numpy as jnp
from jax import lax
import numpy as np

D_MODEL = 1024
BATCH = 16
SEQ = 256
DEPTH = 2
DEC_BATCH = 4
DEC_SEQ = 4096
PAST_LEN = 512

GRID_W = 64
N_HEADS = 16
HEAD_DIM = D_MODEL // N_HEADS
D_RNN = D_MODEL // 2
RNN_HEADS = 8
RNN_BLOCK = D_RNN // RNN_HEADS
CONV_W = 4
D_FOURIER = D_MODEL // 2
FOURIER_GROUPS = 4
FOURIER_CH = D_FOURIER // FOURIER_GROUPS
D_IN_EVEN = 2 * D_RNN + D_FOURIER
D_OUT_EVEN = D_RNN + D_FOURIER
D_FF = 4 * D_MODEL
NA_ROWS = 8
NA_COLS = 16
C_SCALE = 8.0
N_EVEN = (DEPTH + 1) // 2
N_ODD = DEPTH // 2
ALPHA = (2 * DEPTH) ** 0.25
BETA = (8 * DEPTH) ** -0.25
LN_EPS = 1e-5

kernel_name = "hybrid_diffusion_rglru_fnet_natten_step"


def layer_norm(x, g=None, b=None):
    xf = x.astype(jnp.float32)
    mu = jnp.mean(xf, axis=-1, keepdims=True)
    var = jnp.mean(jnp.square(xf - mu), axis=-1, keepdims=True)
    y = (xf - mu) * lax.rsqrt(var + LN_EPS)
    if g is not None:
        y = y * g.astype(jnp.float32) + b.astype(jnp.float32)
    return y.astype(x.dtype)


def ada_params(cond, w, b):
    m = jax.nn.silu(cond) @ w + b
    return jnp.split(m[..., None, :], 6, axis=-1)


def modulate(x, shift, scale):
    return layer_norm(x) * (1.0 + scale) + shift


def depthwise_conv(x, w, b):
    S = x.shape[1]
    left = CONV_W // 2
    xp = jnp.pad(x, ((0, 0), (left, CONV_W - 1 - left), (0, 0)))
    out = b
    for k in range(CONV_W):
        out = out + xp[:, k:k + S] * w[k]
    return out


def block_diag(x, w, b):
    xb = x.reshape(x.shape[:-1] + (RNN_HEADS, RNN_BLOCK))
    y = jnp.einsum('bshi,hij->bshj', xb, w.astype(x.dtype))
    return y.reshape(x.shape) + b.astype(x.dtype)


def _lin_combine(e1, e2):
    a1, b1 = e1
    a2, b2 = e2
    return a1 * a2, a2 * b1 + b2


def rglru(xc, w_r, b_r, w_i, b_i, lam, h0, reverse):
    xf = xc.astype(jnp.float32)
    r = jax.nn.sigmoid(block_diag(xf, w_r, b_r))
    i = jax.nn.sigmoid(block_diag(xf, w_i, b_i))
    log_a = -C_SCALE * r * jax.nn.softplus(-lam.astype(jnp.float32))
    a = jnp.exp(log_a)
    bterm = jnp.sqrt(-jnp.expm1(2.0 * log_a)) * (i * xf)
    h0f = h0.astype(jnp.float32)
    if reverse:
        bterm = bterm.at[:, -1].add(a[:, -1] * h0f)
    else:
        bterm = bterm.at[:, 0].add(a[:, 0] * h0f)
    _, h = lax.associative_scan(_lin_combine, (a, bterm), reverse=reverse, axis=1)
    return h.astype(xc.dtype)


def fourier_mix(xf):
    B, S, _ = xf.shape
    z = xf.reshape(B, S, FOURIER_GROUPS, FOURIER_CH).astype(jnp.float32)
    z = jnp.fft.fft2(z, axes=(1, 3), norm='ortho').real
    return z.reshape(B, S, D_FOURIER).astype(xf.dtype)


def mix_even(h, h0, w_in, b_in, conv_w, conv_b, w_r, b_r, w_i, b_i, lam, w_out, b_out):
    u = h @ w_in + b_in
    x_rnn = u[..., :D_RNN]
    x_gate = u[..., D_RNN:2 * D_RNN]
    x_four = u[..., 2 * D_RNN:]
    xc = depthwise_conv(x_rnn, conv_w, conv_b)
    h_fwd = rglru(xc, w_r[0], b_r[0], w_i[0], b_i[0], lam[0], h0[:, 0], reverse=False)
    h_bwd = rglru(xc, w_r[1], b_r[1], w_i[1], b_i[1], lam[1], h0[:, 1], reverse=True)
    y_a = (h_fwd + h_bwd) * jax.nn.gelu(x_gate)
    y_b = fourier_mix(x_four)
    y = jnp.concatenate([y_a, y_b], axis=-1) @ w_out + b_out
    final_state = jnp.stack([h_fwd[:, -1], h_bwd[:, 0]], axis=1)
    return y, final_state


def split_heads(t):
    B, S, _ = t.shape
    return t.reshape(B, S, N_HEADS, HEAD_DIM).transpose(0, 2, 1, 3)


def attn_context(h, w_qkv, b_qkv, w_out, b_out):
    B, S, _ = h.shape
    qkv = h @ w_qkv + b_qkv
    q, k, v = (split_heads(t) for t in jnp.split(qkv, 3, axis=-1))
    s = jnp.einsum('bhqd,bhkd->bhqk', q, k).astype(jnp.float32) * (HEAD_DIM ** -0.5)
    p = jax.nn.softmax(s, axis=-1).astype(v.dtype)
    o = jnp.einsum('bhqk,bhkd->bhqd', p, v)
    o = o.transpose(0, 2, 1, 3).reshape(B, S, D_MODEL)
    return o @ w_out + b_out, k, v


def attn_latent(h, k_ctx, v_ctx, w_qkv, b_qkv, rpb, w_out, b_out):
    B, N, _ = h.shape
    rows = N // GRID_W
    kr = min(NA_ROWS, rows)
    qkv = h @ w_qkv + b_qkv
    q, k, v = (split_heads(t).reshape(B, N_HEADS, rows, GRID_W, HEAD_DIM)
               for t in jnp.split(qkv, 3, axis=-1))
    col = jnp.arange(GRID_W)
    col_start = jnp.clip(col - NA_COLS // 2, 0, GRID_W - NA_COLS)
    col_mask = (col[None, :] >= col_start[:, None]) & (col[None, :] < col_start[:, None] + NA_COLS)
    col_off = jnp.clip(col[None, :] - col[:, None] + NA_COLS - 1, 0, 2 * NA_COLS - 2)
    scale = HEAD_DIM ** -0.5
    s_ctx_all = None

    def row_block(i):
        rs = jnp.clip(i - kr // 2, 0, rows - kr)
        q_i = lax.dynamic_index_in_dim(q, i, axis=2, keepdims=False)
        k_blk = lax.dynamic_slice_in_dim(k, rs, kr, axis=2)
        v_blk = lax.dynamic_slice_in_dim(v, rs, kr, axis=2)
        s_loc = jnp.einsum('bhqd,bhakd->bhqak', q_i, k_blk).astype(jnp.float32) * scale
        row_off = rs + jnp.arange(kr) - i + NA_ROWS - 1
        bias = rpb[:, row_off][:, :, col_off].astype(jnp.float32)
        bias = bias.transpose(0, 2, 1, 3)
        s_loc = jnp.where(col_mask[:, None, :], s_loc + bias, -jnp.inf)
        s_ctx = jnp.einsum('bhqd,bhcd->bhqc', q_i, k_ctx).astype(jnp.float32) * scale
        logits = jnp.concatenate([s_loc.reshape(B, N_HEADS, GRID_W, kr * GRID_W), s_ctx], axis=-1)
        p = jax.nn.softmax(logits, axis=-1)
        p_loc = p[..., :kr * GRID_W].reshape(B, N_HEADS, GRID_W, kr, GRID_W).astype(v.dtype)
        p_ctx = p[..., kr * GRID_W:].astype(v.dtype)
        return (jnp.einsum('bhqak,bhakd->bhqd', p_loc, v_blk)
                + jnp.einsum('bhqc,bhcd->bhqd', p_ctx, v_ctx))

    o = lax.map(row_block, jnp.arange(rows))
    o = o.transpose(1, 0, 3, 2, 4).reshape(B, N, D_MODEL)
    return o @ w_out + b_out


def mlp(h, w1, b1, w2, b2):
    return jnp.square(jax.nn.relu(h @ w1 + b1)) @ w2 + b2


def setup_inputs(seed: int = 0) -> dict:
    key = jax.random.key(seed)
    ks = iter(jax.random.split(key, 40))
    f32 = jnp.float32

    def nrm(shape, s):
        return jax.random.normal(next(ks), shape, f32) * s

    inp = {}
    inp['x_prompt'] = nrm((BATCH, SEQ, D_MODEL), 1.0)
    inp['x_sample'] = nrm((DEC_BATCH, DEC_SEQ, D_MODEL), 1.0)
    inp['c'] = nrm((DEC_BATCH, D_MODEL), 1.0)
    inp['state_lru'] = nrm((DEC_BATCH, N_EVEN, 2, D_RNN), 0.5)
    inp['cache_k'] = nrm((DEC_BATCH, N_ODD, N_HEADS, PAST_LEN, HEAD_DIM), 1.0)
    inp['cache_v'] = nrm((DEC_BATCH, N_ODD, N_HEADS, PAST_LEN, HEAD_DIM), 1.0)
    inp['c_ctx'] = nrm((D_MODEL,), 1.0)
    inp['ada_w'] = nrm((DEPTH, D_MODEL, 6 * D_MODEL), 0.5 * D_MODEL ** -0.5)
    inp['ada_b'] = nrm((DEPTH, 6 * D_MODEL), 0.02)
    inp['ln1_g'] = 1.0 + nrm((DEPTH, D_MODEL), 0.02)
    inp['ln1_b'] = nrm((DEPTH, D_MODEL), 0.02)
    inp['ln2_g'] = 1.0 + nrm((DEPTH, D_MODEL), 0.02)
    inp['ln2_b'] = nrm((DEPTH, D_MODEL), 0.02)
    inp['w1'] = nrm((DEPTH, D_MODEL, D_FF), D_MODEL ** -0.5)
    inp['b1'] = nrm((DEPTH, D_FF), 0.02)
    inp['w2'] = nrm((DEPTH, D_FF, D_MODEL), BETA * D_FF ** -0.5)
    inp['b2'] = nrm((DEPTH, D_MODEL), 0.02)
    inp['e_w_in'] = nrm((N_EVEN, D_MODEL, D_IN_EVEN), D_MODEL ** -0.5)
    inp['e_b_in'] = nrm((N_EVEN, D_IN_EVEN), 0.02)
    inp['e_conv_w'] = nrm((N_EVEN, CONV_W, D_RNN), CONV_W ** -0.5)
    inp['e_conv_b'] = nrm((N_EVEN, D_RNN), 0.02)
    inp['e_w_r'] = nrm((N_EVEN, 2, RNN_HEADS, RNN_BLOCK, RNN_BLOCK), RNN_BLOCK ** -0.5)
    inp['e_b_r'] = nrm((N_EVEN, 2, D_RNN), 0.02)
    inp['e_w_i'] = nrm((N_EVEN, 2, RNN_HEADS, RNN_BLOCK, RNN_BLOCK), RNN_BLOCK ** -0.5)
    inp['e_b_i'] = nrm((N_EVEN, 2, D_RNN), 0.02)
    a0 = jax.random.uniform(next(ks), (N_EVEN, 2, D_RNN), f32, 0.9, 0.999)
    a_root = a0 ** (1.0 / C_SCALE)
    inp['e_lam'] = jnp.log(a_root) - jnp.log1p(-a_root)
    inp['e_w_out'] = nrm((N_EVEN, D_OUT_EVEN, D_MODEL), BETA * D_OUT_EVEN ** -0.5)
    inp['e_b_out'] = nrm((N_EVEN, D_MODEL), 0.02)
    inp['o_w_qkv'] = nrm((N_ODD, D_MODEL, 3 * D_MODEL), D_MODEL ** -0.5)
    inp['o_b_qkv'] = nrm((N_ODD, 3 * D_MODEL), 0.02)
    inp['o_rpb'] = nrm((N_ODD, N_HEADS, 2 * NA_ROWS - 1, 2 * NA_COLS - 1), 0.1)
    inp['o_w_out'] = nrm((N_ODD, D_MODEL, D_MODEL), BETA * D_MODEL ** -0.5)
    inp['o_b_out'] = nrm((N_ODD, D_MODEL), 0.02)
    return inp


def reference(x_prompt, x_sample, c, state_lru, cache_k, cache_v, c_ctx,
              ada_w, ada_b, ln1_g, ln1_b, ln2_g, ln2_b, w1, b1, w2, b2,
              e_w_in, e_b_in, e_conv_w, e_conv_b, e_w_r, e_b_r, e_w_i, e_b_i, e_lam,
              e_w_out, e_b_out, o_w_qkv, o_b_qkv, o_rpb, o_w_out, o_b_out):
    xp = x_prompt
    xs = x_sample
    new_lru = []
    new_k = []
    new_v = []
    for layer in range(DEPTH):
        mp = ada_params(c_ctx, ada_w[layer], ada_b[layer])
        ms = ada_params(c, ada_w[layer], ada_b[layer])
        hp = modulate(xp, mp[0], mp[1])
        hs = modulate(xs, ms[0], ms[1])
        if layer % 2 == 0:
            j = layer // 2
            prm = (e_w_in[j], e_b_in[j], e_conv_w[j], e_conv_b[j], e_w_r[j], e_b_r[j],
                   e_w_i[j], e_b_i[j], e_lam[j], e_w_out[j], e_b_out[j])
            h0_ctx = jnp.zeros((xp.shape[0], 2, D_RNN), xp.dtype)
            yp, st_ctx = mix_even(hp, h0_ctx, *prm)
            ys, _ = mix_even(hs, state_lru[:, j], *prm)
            new_lru.append(st_ctx)
        else:
            j = layer // 2
            yp, k_ctx, v_ctx = attn_context(hp, o_w_qkv[j], o_b_qkv[j], o_w_out[j], o_b_out[j])
            ys = attn_latent(hs, cache_k[:, j], cache_v[:, j], o_w_qkv[j], o_b_qkv[j],
                             o_rpb[j], o_w_out[j], o_b_out[j])
            new_k.append(k_ctx)
            new_v.append(v_ctx)
        xp = layer_norm(ALPHA * xp + mp[2] * yp, ln1_g[layer], ln1_b[layer])
        xs = layer_norm(ALPHA * xs + ms[2] * ys, ln1_g[layer], ln1_b[layer])
        hp = modulate(xp, mp[3], mp[4])
        hs = modulate(xs, ms[3], ms[4])
        xp = layer_norm(ALPHA * xp + mp[5] * mlp(hp, w1[layer], b1[layer], w2[layer], b2[layer]),
                        ln2_g[layer], ln2_b[layer])
        xs = layer_norm(ALPHA * xs + ms[5] * mlp(hs, w1[layer], b1[layer], w2[layer], b2[layer]),
                        ln2_g[layer], ln2_b[layer])
    new_state_lru = jnp.stack(new_lru, axis=1)
    new_cache_k = jnp.stack(new_k, axis=1)
    new_cache_v = jnp.stack(new_v, axis=1)
    return (xp, xs, new_state_lru, new_cache_k, new_cache_v)
```

```python
import contextlib
import numpy as np
import ml_dtypes
import concourse.bass as bass
import concourse.mybir as mybir
from concourse.bass_utils import run_bass_kernel_spmd

F32 = mybir.dt.float32
BF16 = mybir.dt.bfloat16
AF = mybir.ActivationFunctionType
ALU = mybir.AluOpType
ENGS = ("pe", "act", "dve", "pool", "sp")

D_MODEL = 1024
ALPHA = 4.0 ** 0.25
LN_EPS = 1e-5
NS = 4096
SLAB = 2304
OWN = 2048
NPR = 512
NTOK0 = NS + NPR
NSL = SLAB + NPR
NOUT = OWN + NPR
GELU_C = 0.7978845608028654


import os
MAXSTAGE = int(os.environ.get("MAXSTAGE", "99"))


class _Stop(Exception):
    pass


class Tok:
    __slots__ = ("name", "w", "r")

    def __init__(self, name=""):
        self.name = name
        self.w = None
        self.r = []


class Prog:
    def __init__(self, nc, es, ndma=48):
        self.nc = nc
        self.ops = {e: [] for e in ENGS}
        self.emitted = {e: 0 for e in ENGS}
        self.esem = {e: es.enter_context(nc.semaphore("s_" + e)) for e in ENGS}
        self.dpool = [es.enter_context(nc.semaphore("d_%d" % i)) for i in range(ndma)]
        self.dsem = {}
        self.dma_cnt = {}
        self.free = [(s_, 0) for s_ in self.dpool]
        self.stage = 0
        self.seen = {e: {} for e in ENGS}

    def _deps(self, eng, reads, writes):
        deps = []
        for t in reads:
            if t.w is not None:
                deps.append(t.w)
        for t in writes:
            if t.w is not None and not (t.w[0] == 'e' and t.w[1] == eng and eng != 'pool'):
                deps.append(t.w)
            for r in t.r:
                if not (r[0] == 'e' and r[1] == eng and eng != 'pool'):
                    deps.append(r)
        return deps

    def op(self, eng, fn, reads=(), writes=(), sig=True):
        deps = self._deps(eng, reads, writes)
        idx = len(self.ops[eng])
        self.ops[eng].append(dict(fn=fn, deps=deps, sig=sig, dma=None))
        me = ('e', eng, idx)
        for t in reads:
            t.r.append(me)
        for t in writes:
            t.w = me
            t.r = []
        return me

    def dma(self, eng, fn, key, reads=(), writes=()):
        deps = self._deps(None, reads, writes)
        key = (self.stage, key)
        if key not in self.dsem:
            sem_, cnt_ = self.free.pop(0)
            self.dsem[key] = sem_
            self.dma_cnt[key] = cnt_
        self.dma_cnt[key] += 16
        me = ('d', key, self.dma_cnt[key])
        self.ops[eng].append(dict(fn=fn, deps=deps, sig=False, dma=key))
        for t in reads:
            t.r.append(me)
        for t in writes:
            t.w = me
            t.r = []
        return me

    def emit(self):
        nc = self.nc
        need = {}
        for e in ENGS:
            ops = self.ops[e]
            for o in reversed(ops[self.emitted[e]:]):
                if o["dma"] is None:
                    o["sig"] = True
                    break
            c = 0
            arr = []
            for o in ops:
                if o["sig"]:
                    c += 1
                arr.append(c)
            nd = [None] * len(ops)
            nxt = None
            for i in range(len(ops) - 1, -1, -1):
                if ops[i]["sig"]:
                    nxt = arr[i]
                nd[i] = nxt
            need[e] = nd
        with nc.Block() as block:
            engobj = {"pe": block.tensor, "act": block.scalar, "dve": block.vector,
                      "pool": block.gpsimd, "sp": block.sync}

            def make(e):
                def body(eng):
                    seen = self.seen[e]
                    for o in self.ops[e][self.emitted[e]:]:
                        w = {}
                        for d in o["deps"]:
                            if d[0] == 'e':
                                k = ('e', d[1])
                                v = need[d[1]][d[2]]
                                assert v is not None
                            else:
                                if d[1] not in self.dsem:
                                    continue
                                k = ('d', d[1])
                                v = d[2]
                            if v > w.get(k, 0):
                                w[k] = v
                        for k, v in w.items():
                            if seen.get(k, 0) >= v:
                                continue
                            seen[k] = v
                            s = self.esem[k[1]] if k[0] == 'e' else self.dsem[k[1]]
                            eng.wait_ge(s, v)
                        ins = o["fn"](eng)
                        if o["dma"] is not None:
                            ins.then_inc(self.dsem[o["dma"]], 16)
                        elif o["sig"]:
                            ins.then_inc(self.esem[e], 1)
                    if e == "sp":
                        for k, s in self.dsem.items():
                            if seen.get(('d', k), 0) < self.dma_cnt[k]:
                                seen[('d', k)] = self.dma_cnt[k]
                                eng.wait_ge(s, self.dma_cnt[k])
                    self.emitted[e] = len(self.ops[e])
                return body

            for e in ENGS:
                engobj[e](make(e))
        for k in list(self.dsem.keys()):
            self.free.append((self.dsem[k], self.dma_cnt[k]))
            for e in ENGS:
                self.seen[e].pop(('d', k), None)
        self.dsem = {}
        self.dma_cnt = {}
        self.stage += 1
        if self.stage >= MAXSTAGE:
            raise _Stop()


class Ring:
    def __init__(self, items):
        self.items = items
        self.i = 0

    def next(self):
        it = self.items[self.i % len(self.items)]
        self.i += 1
        return it


def build():
    nc = bass.Bass("TRN2", target_bir_lowering=False)
    D = {}

    def din(name, shape, dt=F32):
        D[name] = nc.dram_tensor(name, list(shape), dt, kind="ExternalInput").ap()

    def dout(name, shape, dt=F32):
        D[name] = nc.dram_tensor(name, list(shape), dt, kind="ExternalOutput").ap()

    def dscr(name, shape, dt=F32):
        D[name] = nc.dram_tensor(name, list(shape), dt).ap()

    din("xs", [NS, 1024]); din("xp", [NPR, 1024])
    din("condT", [128, 8, 2]); din("st0", [128, 8])
    din("ckT", [1024, 512]); din("cv", [512, 1024])
    din("ada_w", [2, 1024, 6144]); din("adabF", [128, 2, 48]); din("adabB", [2, 2, 128, 1024])
    din("lnB", [2, 4, 128, 1024])
    din("w1", [2, 1024, 4096]); din("b1F", [128, 2, 32]); din("w2", [2, 4096, 1024])
    din("e_w_in", [1024, 1536]); din("e_b_inF", [128, 12]); din("taps", [128, 4, 5]); din("convbF", [128, 4])
    din("wgate", [128, 16, 128]); din("bgateF", [128, 2, 2, 4]); din("lamF", [128, 2, 4])
    din("e_w_out", [1024, 1024]); din("ebrow", [1, 1024]); din("obrow", [1, 1024]); din("b2row", [2, 1024])
    din("o_w_qkv", [1024, 3072]); din("bqkF", [128, 16]); din("bvB", [128, 1024])
    din("nabias", [16, 128, 7, 128]); din("o_w_out", [1024, 1024])
    din("dftS", [NS, 2, SLAB], BF16); din("dftP", [256, 2, 256], BF16); din("dftC", [128, 256], BF16)
    din("ident", [128, 128], BF16)
    dout("ys", [OWN, 1024]); dout("yp", [NPR, 1024]); dout("nst", [128, 4, 2, 2])
    dout("nkT", [1024, NPR]); dout("nv", [NPR, 1024])
    dscr("w1b", [2, 1024, 4096], BF16); dscr("w2b", [2, 4096, 1024], BF16)
    dscr("xrT", [512, NTOK0]); dscr("gT", [512, NTOK0]); dscr("yT", [1024, NSL], BF16)
    dscr("x1", [NSL, 1024]); dscr("gsc", [2, 4, 128, 1024])
    dscr("qA", [1024, NOUT], BF16); dscr("qB", [1024, NOUT], BF16); dscr("kT", [1024, NSL], BF16)
    dscr("ve", [NSL, 2048], BF16); dscr("oT", [1024, NOUT], BF16)

    try:
      with contextlib.ExitStack() as ges:
        P = Prog(nc, ges)
        gsb = lambda n, s, d: ges.enter_context(nc.sbuf_tensor("g_" + n, list(s), d))
        ident = gsb("ident", [128, 128], BF16); T_ident = Tok()
        modF = [gsb("modF%d" % l, [128, 48, 2], F32) for l in range(2)]
        T_modF = [Tok(), Tok()]
        cst = gsb("cst", [128, 4], F32); T_cst = Tok()
        pb = [ges.enter_context(nc.psum_tensor("pb%d" % i, [128, 512], F32)) for i in range(8)]
        T_pb = [Tok() for _ in range(8)]

        P.dma("sp", lambda e: e.dma_start(out=ident[:], in_=D["ident"]), "ident", writes=[T_ident])
        P.op("dve", lambda e: e.memset(cst[:, 0:1], 1.0), writes=[T_cst])
        P.op("dve", lambda e: e.memset(cst[:, 1:2], LN_EPS), writes=[T_cst])

        def ln_stats(S, xt, T_x):
            st, T_st = S["stat"].next()
            mv, T_mv = S["mv"].next()
            P.op("dve", lambda e: e.bn_stats(out=st[:, 0:6], in_=xt[:, 0:512]), reads=[T_x], writes=[T_st])
            P.op("dve", lambda e: e.bn_stats(out=st[:, 6:12], in_=xt[:, 512:1024]), reads=[T_x], writes=[T_st])
            P.op("dve", lambda e: e.bn_aggr(out=mv[:, 0:2], in_=st[:]), reads=[T_st], writes=[T_mv])
            P.op("act", lambda e: e.activation(out=mv[:, 1:2], in_=mv[:, 1:2], func=AF.Sqrt, bias=cst[:, 1:2], scale=1.0),
                 reads=[T_mv, T_cst], writes=[T_mv])
            P.op("dve", lambda e: e.reciprocal(out=mv[:, 1:2], in_=mv[:, 1:2]), reads=[T_mv], writes=[T_mv])
            return mv, T_mv

        def ln_mod_T(S, xt, T_x, l, vshift, r, hT, T_hT, col0):
            mv, T_mv = ln_stats(S, xt, T_x)
            xh, T_xh = S["xh"].next()
            P.op("dve", lambda e: e.tensor_scalar(out=xh[:], in0=xt[:], scalar1=mv[:, 0:1], scalar2=mv[:, 1:2],
                                                  op0=ALU.subtract, op1=ALU.mult), reads=[T_x, T_mv], writes=[T_xh])
            (pt, T_pt) = S["ptr"].next()
            ptb = pt[:].bitcast(BF16)
            for c in range(8):
                P.op("pe", lambda e, c=c: e.transpose(out=ptb[:, c * 128:(c + 1) * 128], in_=xh[:, c * 128:(c + 1) * 128],
                                                      identity=ident[:]),
                     reads=[T_xh, T_ident], writes=[T_pt], sig=(c == 7))
            for c in range(8):
                P.op("act", lambda e, c=c: e.activation(out=hT[:, c, col0:col0 + 128], in_=ptb[:, c * 128:(c + 1) * 128],
                                                        func=AF.Identity,
                                                        bias=modF[l][:, vshift * 8 + c, r:r + 1],
                                                        scale=modF[l][:, (vshift + 1) * 8 + c, r:r + 1]),
                     reads=[T_pt, T_modF[l]], writes=[T_hT])

        def ln_affine(S, xt, T_x, gB, bB, T_gb, out, T_out):
            mv, T_mv = ln_stats(S, xt, T_x)
            P.op("dve", lambda e: e.tensor_scalar(out=xt[:], in0=xt[:], scalar1=mv[:, 0:1], scalar2=mv[:, 1:2],
                                                  op0=ALU.subtract, op1=ALU.mult), reads=[T_x, T_mv], writes=[T_x])
            P.op("pool", lambda e: e.tensor_tensor(out=xt[:], in0=xt[:], in1=gB, op=ALU.mult), reads=[T_x, T_gb], writes=[T_x])
            P.op("pool", lambda e: e.tensor_tensor(out=out[:], in0=xt[:], in1=bB, op=ALU.add), reads=[T_x, T_gb], writes=[T_out])

        def ln_stats_multi(S, xs):
            sts = [S["stat"].next() for _ in xs]
            mvs = [S["mv"].next() for _ in xs]
            for (xt, T_x), (st, T_st) in zip(xs, sts):
                P.op("dve", lambda e, st=st, xt=xt: e.bn_stats(out=st[:, 0:6], in_=xt[:, 0:512]), reads=[T_x], writes=[T_st])
                P.op("dve", lambda e, st=st, xt=xt: e.bn_stats(out=st[:, 6:12], in_=xt[:, 512:1024]), reads=[T_x], writes=[T_st])
            for (st, T_st), (mv, T_mv) in zip(sts, mvs):
                P.op("dve", lambda e, st=st, mv=mv: e.bn_aggr(out=mv[:, 0:2], in_=st[:]), reads=[T_st], writes=[T_mv])
            for (mv, T_mv) in mvs:
                P.op("act", lambda e, mv=mv: e.activation(out=mv[:, 1:2], in_=mv[:, 1:2], func=AF.Sqrt, bias=cst[:, 1:2], scale=1.0),
                     reads=[T_mv, T_cst], writes=[T_mv])
            for (mv, T_mv) in mvs:
                P.op("dve", lambda e, mv=mv: e.reciprocal(out=mv[:, 1:2], in_=mv[:, 1:2]), reads=[T_mv], writes=[T_mv])
                P.op("dve", lambda e, mv=mv: e.scalar_tensor_tensor(out=mv[:, 2:3], in0=mv[:, 0:1], scalar=-1.0, in1=mv[:, 1:2],
                                                                     op0=ALU.mult, op1=ALU.mult), reads=[T_mv], writes=[T_mv])
            return mvs

        def ln_affine_multi(S, xs, gB, bB, T_gb, outs):
            mvs = ln_stats_multi(S, xs)
            for (xt, T_x), (mv, T_mv) in zip(xs, mvs):
                P.op("act", lambda e, xt=xt, mv=mv: e.activation(out=xt[:], in_=xt[:], func=AF.Identity,
                                                                 bias=mv[:, 2:3], scale=mv[:, 1:2]), reads=[T_x, T_mv], writes=[T_x])
            for (xt, T_x) in xs:
                P.op("dve", lambda e, xt=xt: e.tensor_tensor(out=xt[:], in0=xt[:], in1=gB, op=ALU.mult), reads=[T_x, T_gb], writes=[T_x])
            for (xt, T_x), (out, T_out) in zip(xs, outs):
                P.op("pool", lambda e, xt=xt, out=out: e.tensor_tensor(out=out[:], in0=xt[:], in1=bB, op=ALU.add),
                     reads=[T_x, T_gb], writes=[T_out])

        def ln_group_1(S, srcs, xin, keyp):
            xs = []
            for src in srcs:
                xt, T_x = xin.next()
                P.dma("sp", lambda e, xt=xt, src=src: e.dma_start(out=xt[:], in_=src),
                      keyp + "%d" % ((xin.i - 1) % len(xin.items)), writes=[T_x])
                xs.append((xt, T_x))
            mvs = ln_stats_multi(S, xs)
            xhl = []
            for (xt, T_x), (mv, T_mv) in zip(xs, mvs):
                xh, T_xh = S["xh"].next()
                P.op("act", lambda e, xh=xh, xt=xt, mv=mv: e.activation(out=xh[:], in_=xt[:], func=AF.Identity,
                                                                        bias=mv[:, 2:3], scale=mv[:, 1:2]),
                     reads=[T_x, T_mv], writes=[T_xh])
                xhl.append((xh, T_xh))
            return xhl

        def ln_group_2(S, xhl, l, vshift, rs, hT, T_hT, all_act=False):
            tb = S["tbanks"]
            nt = len(xhl)
            for tt, (xh, T_xh) in enumerate(xhl):
                for c in range(8):
                    bk, T_bk = tb[c // 2]
                    col = ((c % 2) * 4 + tt) * 128
                    P.op("pe", lambda e, c=c, xh=xh, bk=bk, col=col: e.transpose(
                        out=bk[:].bitcast(BF16)[:, col:col + 128], in_=xh[:, c * 128:(c + 1) * 128], identity=ident[:]),
                        reads=[T_xh, T_ident], writes=[T_bk], sig=(tt == nt - 1 or c == 7))
            runs = []
            t0 = 0
            for tt in range(1, nt + 1):
                if tt == nt or rs[tt] != rs[t0]:
                    runs.append((t0, tt, rs[t0]))
                    t0 = tt
            for c in range(8):
                bk, T_bk = tb[c // 2]
                for (a, b, r) in runs:
                    col = ((c % 2) * 4 + a) * 128
                    P.op("act", lambda e, c=c, bk=bk, col=col, a=a, b=b, r=r: e.activation(
                        out=hT[:, c, a * 128:b * 128], in_=bk[:].bitcast(BF16)[:, col:col + (b - a) * 128], func=AF.Identity,
                        bias=modF[l][:, vshift * 8 + c, r:r + 1], scale=modF[l][:, (vshift + 1) * 8 + c, r:r + 1]),
                        reads=[T_bk, T_modF[l]], writes=[T_hT])

        def mk_ln_scratch(es, tag, nx=2):
            sb = lambda n, s, d: es.enter_context(nc.sbuf_tensor(tag + n, list(s), d))
            S = {}
            S["stat"] = Ring([(sb("st%d" % i, [128, 12], F32), Tok()) for i in range(4)])
            S["mv"] = Ring([(sb("mv%d" % i, [128, 4], F32), Tok()) for i in range(8)])
            S["xh"] = Ring([(sb("xh%d" % i, [128, 1024], BF16), Tok()) for i in range(nx)])
            return S

        with contextlib.ExitStack() as es:
            sb = lambda n, s, d: es.enter_context(nc.sbuf_tensor("s0" + n, list(s), d))
            condT = sb("condT", [128, 8, 2], F32); T_condT = Tok()
            sT = sb("sT", [128, 8, 2], BF16); T_sT = Tok()
            sB = [sb("sB%d" % r, [128, 8, 128], BF16) for r in range(2)]; T_sB = [Tok(), Tok()]
            adabF = sb("adabF", [128, 2, 48], F32); T_adabF = Tok()
            adabB = Ring([(sb("adabB%d" % i, [128, 1024], F32), Tok()) for i in range(2)])
            gstage = Ring([(sb("gst%d" % i, [128, 1024], F32), Tok()) for i in range(2)])
            slots = Ring([(sb("aw%d" % i, [128, 8, 1024], BF16), Tok()) for i in range(3)])
            P.dma("sp", lambda e: e.dma_start(out=condT[:], in_=D["condT"]), "condT", writes=[T_condT])
            P.dma("sp", lambda e: e.dma_start(out=adabF[:], in_=D["adabF"]), "adabF", writes=[T_adabF])
            P.op("act", lambda e: e.activation(out=sT[:], in_=condT[:], func=AF.Silu), reads=[T_condT], writes=[T_sT])
            for r in range(2):
                for c in range(8):
                    P.op("dve", lambda e, r=r, c=c: e.tensor_copy(out=sB[r][:, c, :],
                                                                  in_=sT[:, c, r:r + 1].to_broadcast([128, 128])),
                         reads=[T_sT], writes=[T_sB[r]])
            bank = Ring(list(zip(pb, T_pb)))
            for l in range(2):
                for v in range(6):
                    slot, T_slot = slots.next()
                    key = "aw%d" % ((slots.i - 1) % 3)
                    P.dma("pool", lambda e, slot=slot, l=l, v=v: e.dma_start(
                        out=slot[:], in_=D["ada_w"][l, :, v * 1024:(v + 1) * 1024].rearrange("(c p) n -> p c n", p=128)),
                        key, writes=[T_slot])
                    if v in (2, 5):
                        g = 0 if v == 2 else 1
                        bt, T_bt = adabB.next()
                        P.dma("sp", lambda e, bt=bt, l=l, g=g: e.dma_start(out=bt[:], in_=D["adabB"][l, g]),
                              "adabB%d" % ((adabB.i - 1) % 2), writes=[T_bt])
                        for r in range(2):
                            for half in range(2):
                                ps, T_ps = bank.next()
                                for c in range(8):
                                    P.op("pe", lambda e, ps=ps, slot=slot, r=r, c=c, half=half: e.matmul(
                                        ps[:], lhsT=sB[r][:, c, :], rhs=slot[:, c, half * 512:(half + 1) * 512],
                                        start=(c == 0), stop=(c == 7)),
                                        reads=[T_sB[r], T_slot], writes=[T_ps], sig=(c == 7))
                                if half == 0:
                                    gst, T_gst = gstage.next()
                                P.op("dve", lambda e, ps=ps, bt=bt, gst=gst, half=half: e.tensor_tensor(
                                    out=gst[:, half * 512:(half + 1) * 512], in0=ps[:],
                                    in1=bt[:, half * 512:(half + 1) * 512], op=ALU.add),
                                    reads=[T_ps, T_bt], writes=[T_gst])
                                if half == 1:
                                    P.dma("sp", lambda e, gst=gst, l=l, r=r, g=g: e.dma_start(out=D["gsc"][l, r * 2 + g], in_=gst[:]),
                                          "gst%d" % ((gstage.i - 1) % 2), reads=[T_gst])
                    else:
                        ps, T_ps = bank.next()
                        for co in range(8):
                            for c in range(8):
                                P.op("pe", lambda e, ps=ps, slot=slot, co=co, c=c: e.matmul(
                                    ps[:, co * 2:co * 2 + 2], lhsT=slot[:, c, co * 128:(co + 1) * 128], rhs=sT[:, c, :],
                                    start=(c == 0), stop=(c == 7)),
                                    reads=[T_sT, T_slot], writes=[T_ps], sig=(co == 7 and c == 7))
                        P.op("dve", lambda e, ps=ps, l=l, v=v: e.tensor_tensor(
                            out=modF[l][:, v * 8:(v + 1) * 8, :],
                            in0=ps[:, 0:16].rearrange("p (c r) -> p c r", r=2),
                            in1=adabF[:, l, v * 8:(v + 1) * 8].unsqueeze(2).to_broadcast([128, 8, 2]), op=ALU.add),
                            reads=[T_ps, T_adabF], writes=[T_modF[l]])
                        if v in (1, 4):
                            P.op("dve", lambda e, l=l, v=v: e.tensor_scalar(
                                out=modF[l][:, v * 8:(v + 1) * 8, :], in0=modF[l][:, v * 8:(v + 1) * 8, :],
                                scalar1=1.0, scalar2=None, op0=ALU.add), reads=[T_modF[l]], writes=[T_modF[l]])
            P.emit()

        with contextlib.ExitStack() as esAB:
            AB = esAB.enter_context(nc.sbuf_tensor("AB", [128, 36, 1024], BF16)); T_AB = Tok()
            with contextlib.ExitStack() as es:
                sb = lambda n, s, d: es.enter_context(nc.sbuf_tensor("a0" + n, list(s), d))
                S = mk_ln_scratch(es, "a0", nx=5)
                S["tbanks"] = [(pb[i], T_pb[i]) for i in (4, 5, 6, 7)]
                win = sb("win", [128, 8, 1536], BF16); T_win = Tok()
                binF = sb("binF", [128, 12], F32); T_binF = Tok()
                dftC = sb("dftC", [128, 256], BF16); T_dftC = Tok()
                xin = Ring([(sb("xin%d" % i, [128, 1024], F32), Tok()) for i in range(6)])
                hTr = Ring([(sb("hT%d" % i, [128, 8, 512], BF16), Tok()) for i in range(2)])
                ev = Ring([(sb("ev%d" % i, [128, 512], F32), Tok()) for i in range(3)])
                gl = Ring([(sb("gl%d" % i, [128, 512], F32), Tok()) for i in range(4)])
                zT = Ring([(sb("zT%d" % i, [128, 512], BF16), Tok()) for i in range(5)])
                T_winp = [Tok(), Tok(), Tok()]
                for part in (2, 0, 1):
                    P.dma("pool", lambda e, part=part: e.dma_start(
                        out=win[:, :, part * 512:(part + 1) * 512],
                        in_=D["e_w_in"][:, part * 512:(part + 1) * 512].rearrange("(c p) n -> p c n", p=128)),
                        "win%d" % part, writes=[T_winp[part]])
                P.dma("sp", lambda e: e.dma_start(out=binF[:], in_=D["e_b_inF"]), "binF", writes=[T_binF])
                P.dma("sp", lambda e: e.dma_start(out=dftC[:], in_=D["dftC"]), "dftC", writes=[T_dftC])
                bank = Ring([(pb[i], T_pb[i]) for i in range(4)])
                S["mt"] = Ring([(sb("mt%d" % i, [128, 8, 128], F32), Tok()) for i in range(2)])

                def LNG0a(grp):
                    srcs = [(D["xs"][(grp * 4 + tt) * 128:(grp * 4 + tt + 1) * 128, :] if grp < 8
                             else D["xp"][tt * 128:(tt + 1) * 128, :]) for tt in range(4)]
                    return ln_group_1(S, srcs, xin, "xin")

                def LNG0b(grp, xhl):
                    r = 0 if grp < 8 else 1
                    hT, T_hT = hTr.next()
                    ln_group_2(S, xhl, 0, 0, [r] * 4, hT, T_hT, all_act=True)
                    return hT, T_hT

                nxt = LNG0b(0, LNG0a(0))
                for grp in range(9):
                    hT, T_hT = nxt
                    if grp + 1 < 9:
                        xhl_n = LNG0a(grp + 1)
                    dft_q = []
                    for co in (8, 9, 10, 11, 0, 1, 2, 3, 4, 5, 6, 7):
                        if 4 <= co < 8 and grp in (5, 6, 7):
                            continue
                        ps, T_ps = bank.next()
                        for c in range(8):
                            P.op("pe", lambda e, ps=ps, co=co, c=c, hT=hT: e.matmul(
                                ps[:], lhsT=win[:, c, co * 128:(co + 1) * 128], rhs=hT[:, c, :],
                                start=(c == 0), stop=(c == 7)), reads=[T_winp[co // 4], T_hT], writes=[T_ps], sig=(c == 7))
                        cols = slice(grp * 512, (grp + 1) * 512)
                        if co < 4:
                            et, T_et = ev.next()
                            P.op("act", lambda e, et=et, ps=ps, co=co: e.activation(
                                out=et[:], in_=ps[:], func=AF.Identity, bias=binF[:, co:co + 1], scale=1.0),
                                reads=[T_ps, T_binF], writes=[T_et])
                            P.dma("act", lambda e, et=et, co=co, cols=cols: e.dma_start(
                                out=D["xrT"][co * 128:(co + 1) * 128, cols], in_=et[:]),
                                "ev%d" % ((ev.i - 1) % 3), reads=[T_et])
                        elif co < 8:
                            j = co - 4
                            x0, T_x0 = gl.next(); u0, T_u0 = gl.next()
                            P.op("act", lambda e, x0=x0, ps=ps, co=co: e.activation(
                                out=x0[:], in_=ps[:], func=AF.Identity, bias=binF[:, co:co + 1], scale=1.0),
                                reads=[T_ps, T_binF], writes=[T_x0])
                            P.op("dve", lambda e, x0=x0, u0=u0: e.tensor_tensor(out=u0[:], in0=x0[:], in1=x0[:], op=ALU.mult),
                                 reads=[T_x0], writes=[T_u0])
                            P.op("dve", lambda e, u0=u0: e.tensor_scalar(out=u0[:], in0=u0[:], scalar1=0.044715, scalar2=1.0,
                                                                         op0=ALU.mult, op1=ALU.add), reads=[T_u0], writes=[T_u0])
                            P.op("dve", lambda e, x0=x0, u0=u0: e.tensor_tensor(out=u0[:], in0=u0[:], in1=x0[:], op=ALU.mult),
                                 reads=[T_x0, T_u0], writes=[T_u0])
                            P.op("act", lambda e, u0=u0: e.activation(out=u0[:], in_=u0[:], func=AF.Sigmoid, scale=2.0 * GELU_C),
                                 reads=[T_u0], writes=[T_u0])
                            P.op("pool", lambda e, x0=x0, u0=u0: e.tensor_tensor(out=u0[:], in0=u0[:], in1=x0[:], op=ALU.mult),
                                 reads=[T_x0, T_u0], writes=[T_u0])
                            P.dma("pool", lambda e, u0=u0, j=j, cols=cols: e.dma_start(
                                out=D["gT"][j * 128:(j + 1) * 128, cols], in_=u0[:]),
                                "gl%d" % ((gl.i - 1) % 4), reads=[T_u0])
                        else:
                            g = co - 8
                            zt, T_zt = zT.next()
                            P.op("act", lambda e, zt=zt, ps=ps, co=co: e.activation(
                                out=zt[:], in_=ps[:], func=AF.Identity, bias=binF[:, co:co + 1], scale=1.0),
                                reads=[T_ps, T_binF], writes=[T_zt])
                            dft_q.append((g, zt, T_zt))
                            continue
                    for (g, zt, T_zt) in dft_q:
                        if True:
                            ps2, T_ps2 = bank.next()
                            for tt in range(4):
                                P.op("pe", lambda e, ps2=ps2, zt=zt, tt=tt: e.matmul(
                                    ps2[:, tt * 128:(tt + 1) * 128],
                                    lhsT=zt[:, tt * 128:(tt + 1) * 128], rhs=dftC[:, 0:128], start=True, stop=True),
                                    reads=[T_zt, T_dftC], writes=[T_ps2], sig=False)
                            ps3, T_ps3 = bank.next()
                            for tt in range(4):
                                P.op("pe", lambda e, ps3=ps3, zt=zt, tt=tt: e.matmul(
                                    ps3[:, tt * 128:(tt + 1) * 128],
                                    lhsT=zt[:, tt * 128:(tt + 1) * 128], rhs=dftC[:, 128:256], start=True, stop=True),
                                    reads=[T_zt, T_dftC], writes=[T_ps3], sig=(tt == 3))
                            P.op("dve", lambda e, ps2=ps2, grp=grp, g=g: e.tensor_copy(
                                out=AB[:, grp * 4:(grp + 1) * 4, g * 256:g * 256 + 128],
                                in_=ps2[:].rearrange("p (t c) -> p t c", c=128)), reads=[T_ps2], writes=[T_AB])
                            P.op("dve", lambda e, ps3=ps3, grp=grp, g=g: e.tensor_copy(
                                out=AB[:, grp * 4:(grp + 1) * 4, g * 256 + 128:g * 256 + 256],
                                in_=ps3[:].rearrange("p (t c) -> p t c", c=128)), reads=[T_ps3], writes=[T_AB])
                    if grp + 1 < 9:
                        nxt = LNG0b(grp + 1, xhl_n)
                P.emit()

            with contextlib.ExitStack() as es:
                sb = lambda n, s, d: es.enter_context(nc.sbuf_tensor("b0" + n, list(s), d))
                tabs = Ring([(sb("tab%d" % i, [128, 2, 512], BF16), Tok()) for i in range(4)])
                tabP = sb("tabP", [128, 2, 2, 256], BF16); T_tabP = Tok()
                yb = Ring([(sb("yb%d" % i, [128, 512], BF16), Tok()) for i in range(4)])
                P.dma("sp", lambda e: e.dma_start(out=tabP[:], in_=D["dftP"].rearrange("(c p) s k -> p c s k", p=128)),
                      "tabP", writes=[T_tabP])
                cvs = Ring([(sb("cvs%d" % i, [128, 8, 1024], BF16), Tok()) for i in range(2)])
                for kind in ("w1", "w2"):
                    for blk in range(4):
                        slot, T_slot = cvs.next()
                        key = "cv%d" % ((cvs.i - 1) % 2)
                        if kind == "w1":
                            src = D["w1"][0, :, blk * 1024:(blk + 1) * 1024].rearrange("(c p) n -> p c n", p=128)
                            dst = D["w1b"][0, :, blk * 1024:(blk + 1) * 1024].rearrange("(c p) n -> p c n", p=128)
                        else:
                            src = D["w2"][0, blk * 1024:(blk + 1) * 1024, :].rearrange("(c p) n -> p c n", p=128)
                            dst = D["w2b"][0, blk * 1024:(blk + 1) * 1024, :].rearrange("(c p) n -> p c n", p=128)
                        P.dma("pool", lambda e, slot=slot, src=src: e.dma_start(out=slot[:], in_=src), key, writes=[T_slot])
                        P.dma("pool", lambda e, slot=slot, dst=dst: e.dma_start(out=dst, in_=slot[:]), key + "s", reads=[T_slot])
                kblocks = [(0, 512), (512, 512), (1024, 512), (1536, 512), (2048, 256)]
                for kbi, (k0, kw) in enumerate(kblocks):
                    accs = [(pb[(kbi % 2) * 4 + g], T_pb[(kbi % 2) * 4 + g]) for g in range(4)]
                    for n in range(32):
                        tb, T_tb = tabs.next()
                        P.dma("sp", lambda e, tb=tb, n=n, k0=k0, kw=kw: e.dma_start(
                            out=tb[:, :, 0:kw], in_=D["dftS"][n * 128:(n + 1) * 128, :, k0:k0 + kw]),
                            "tab%d" % ((tabs.i - 1) % 4), writes=[T_tb])
                        for g in range(4):
                            ps, T_ps = accs[g]
                            for s in range(2):
                                P.op("pe", lambda e, ps=ps, tb=tb, n=n, g=g, s=s, kw=kw: e.matmul(
                                    ps[:, 0:kw], lhsT=AB[:, n, g * 256 + s * 128:g * 256 + (s + 1) * 128], rhs=tb[:, s, 0:kw],
                                    start=(n == 0 and s == 0), stop=(n == 31 and s == 1)),
                                    reads=[T_AB, T_tb], writes=[T_ps], sig=(s == 1 and (g == 3 or n == 31)))
                    for g in range(4):
                        ps, T_ps = accs[g]
                        yt, T_yt = yb.next()
                        P.op("act", lambda e, yt=yt, ps=ps, kw=kw: e.activation(out=yt[:, 0:kw], in_=ps[:, 0:kw], func=AF.Copy),
                             reads=[T_ps], writes=[T_yt])
                        P.dma("act", lambda e, yt=yt, g=g, k0=k0, kw=kw: e.dma_start(
                            out=D["yT"][512 + g * 128:512 + (g + 1) * 128, k0:k0 + kw], in_=yt[:, 0:kw]),
                            "yb%d" % ((yb.i - 1) % 4), reads=[T_yt])
                for s_ in range(2):
                    ps, T_ps = pb[s_], T_pb[s_]
                    for g in range(4):
                        for n in range(2):
                            for s in range(2):
                                P.op("pe", lambda e, ps=ps, g=g, n=n, s=s, s_=s_: e.matmul(
                                    ps[:, 0:256],
                                    lhsT=AB[:, 32 + s_ * 2 + n, g * 256 + s * 128:g * 256 + (s + 1) * 128],
                                    rhs=tabP[:, n, s, :], start=(n == 0 and s == 0), stop=(n == 1 and s == 1)),
                                    reads=[T_AB, T_tabP], writes=[T_ps], sig=(n == 1 and s == 1))
                        yt, T_yt = yb.next()
                        P.op("act", lambda e, yt=yt, ps=ps: e.activation(out=yt[:, 0:256], in_=ps[:, 0:256], func=AF.Copy),
                             reads=[T_ps], writes=[T_yt])
                        P.dma("act", lambda e, yt=yt, g=g, s_=s_: e.dma_start(
                            out=D["yT"][512 + g * 128:512 + (g + 1) * 128, SLAB + s_ * 256:SLAB + (s_ + 1) * 256],
                            in_=yt[:, 0:256]), "yb%d" % ((yb.i - 1) % 4), reads=[T_yt])
                P.emit()

        with contextlib.ExitStack() as es:
            sb = lambda n, s, d: es.enter_context(nc.sbuf_tensor("c0" + n, list(s), d))
            NT = NTOK0
            NA_ = 6 * 512
            Xs = [sb("X%d" % i, [128, NT], F32) for i in range(2)]; T_Xs = [Tok(), Tok()]
            XCs = [sb("XC%d" % i, [128, NT], F32) for i in range(2)]; T_XCs = [Tok(), Tok()]
            XCBs = [sb("XCB%d" % i, [128, NT], BF16) for i in range(2)]
            Rb = [sb("RA", [128, NA_], F32), sb("RB", [128, NT], F32)]
            Ib = [sb("IA", [128, NA_], F32), sb("IB", [128, NT], F32)]
            Mb = [sb("MA", [128, NA_], F32), sb("MB", [128, NT], F32)]
            T_H = [Tok(), Tok()]
            G = sb("G", [128, NSL], F32); T_G = Tok()
            YA = sb("YA", [128, NSL], BF16); T_YA = Tok()
            wg = sb("wg", [128, 16, 128], BF16); T_wg = Tok()
            bg = sb("bg", [128, 2, 2, 4], F32); T_bg = Tok()
            lam = sb("lam", [128, 2, 4], F32); T_lam = Tok()
            tq = sb("tq", [128, 2, 4], F32); T_tq = Tok()
            sp_ = sb("sp_", [128, 2, 4], F32); T_sp = Tok()
            taps = sb("taps", [128, 4, 5], F32); T_taps = Tok()
            cvb = sb("cvb", [128, 4], F32); T_cvb = Tok()
            st0 = sb("st0", [128, 8], F32); T_st0 = Tok()
            nst = sb("nst", [128, 4, 2, 2], F32); T_nst = Tok()
            P.dma("pool", lambda e: e.dma_start(out=wg[:], in_=D["wgate"]), "wg", writes=[T_wg])
            P.dma("sp", lambda e: e.dma_start(out=bg[:], in_=D["bgateF"]), "bg", writes=[T_bg])
            P.dma("sp", lambda e: e.dma_start(out=lam[:], in_=D["lamF"]), "lam", writes=[T_lam])
            P.dma("sp", lambda e: e.dma_start(out=taps[:], in_=D["taps"]), "taps", writes=[T_taps])
            P.dma("sp", lambda e: e.dma_start(out=cvb[:], in_=D["convbF"]), "cvb", writes=[T_cvb])
            P.dma("sp", lambda e: e.dma_start(out=st0[:], in_=D["st0"]), "st0", writes=[T_st0])
            P.op("act", lambda e: e.activation(out=tq[:], in_=lam[:], func=AF.Exp, scale=-1.0), reads=[T_lam], writes=[T_tq])
            P.op("dve", lambda e: e.tensor_scalar(out=sp_[:], in0=tq[:], scalar1=-1.0 / 6.0, scalar2=0.2, op0=ALU.mult, op1=ALU.add),
                 reads=[T_tq], writes=[T_sp])
            for cc in (0.25, 1.0 / 3.0, 0.5, 1.0):
                P.op("dve", lambda e: e.tensor_tensor(out=sp_[:], in0=sp_[:], in1=tq[:], op=ALU.mult), reads=[T_sp, T_tq], writes=[T_sp])
                P.op("dve", lambda e, cc=cc: e.tensor_scalar(out=sp_[:], in0=sp_[:], scalar1=-1.0, scalar2=cc, op0=ALU.mult, op1=ALU.add),
                     reads=[T_sp], writes=[T_sp])
            P.op("dve", lambda e: e.tensor_tensor(out=sp_[:], in0=sp_[:], in1=tq[:], op=ALU.mult), reads=[T_sp, T_tq], writes=[T_sp])
            P.op("dve", lambda e: e.tensor_scalar(out=sp_[:], in0=sp_[:], scalar1=-8.0, scalar2=None, op0=ALU.mult),
                 reads=[T_sp], writes=[T_sp])
            segs = [(0, NS), (NS, NS + 256), (NS + 256, NS + 512)]
            bank = Ring(list(zip(pb, T_pb)))
            NB = NT // 512
            BLK = [[0, 1, 2, 3, 4, 8], list(range(NB))]
            TB_R = [[Tok() for _ in range(NB)] for d in range(2)]
            TB_I = [[Tok() for _ in range(NB)] for d in range(2)]
            TB_M = [[Tok() for _ in range(NB)] for d in range(2)]
            TB_XCB = [[Tok() for _ in range(NB)] for i in range(2)]
            sp2 = sb("sp2", [128, 2, 4], F32); T_sp2 = Tok()
            P.op("dve", lambda e: e.tensor_scalar(out=sp2[:], in0=sp_[:], scalar1=2.0, scalar2=None, op0=ALU.mult),
                 reads=[T_sp], writes=[T_sp2])

            def cs(d, b):
                if d == 1:
                    return slice(b * 512, (b + 1) * 512)
                ci = BLK[0].index(b)
                return slice(ci * 512, (ci + 1) * 512)

            def load_conv(j):
                x = j % 2
                X, XC, XCB = Xs[x], XCs[x], XCBs[x]
                T_X, T_XC = T_Xs[x], T_XCs[x]
                P.dma("sp", lambda e, j=j, X=X: e.dma_start(out=X[:], in_=D["xrT"][j * 128:(j + 1) * 128, :]), "X%d" % x, writes=[T_X])
                P.op("dve", lambda e, j=j, X=X, XC=XC: e.tensor_scalar(out=XC[:], in0=X[:], scalar1=taps[:, j, 2:3], scalar2=cvb[:, j:j + 1],
                                                                       op0=ALU.mult, op1=ALU.add), reads=[T_X, T_taps, T_cvb], writes=[T_XC])
                for (s0, s1) in segs:
                    for o in (-2, -1, 1, 2):
                        lo = s0 + max(0, -o); hi = s1 - max(0, o)
                        P.op("dve", lambda e, j=j, o=o, lo=lo, hi=hi, X=X, XC=XC: e.scalar_tensor_tensor(
                            out=XC[:, lo:hi], in0=X[:, lo + o:hi + o], scalar=taps[:, j, o + 2:o + 3], in1=XC[:, lo:hi],
                            op0=ALU.mult, op1=ALU.add), reads=[T_X, T_taps, T_XC], writes=[T_XC])
                for b in range(NB):
                    bs = slice(b * 512, (b + 1) * 512)
                    P.op("pool", lambda e, bs=bs, XC=XC, XCB=XCB: e.tensor_copy(out=XCB[:, bs], in_=XC[:, bs]),
                         reads=[T_XC], writes=[TB_XCB[x][b]])

            load_conv(0)
            for j in range(4):
                x = j % 2
                XC, XCB, T_XC = XCs[x], XCBs[x], T_XCs[x]
                P.dma("sp", lambda e, j=j: e.dma_start(out=G[:, 0:SLAB], in_=D["gT"][j * 128:(j + 1) * 128, 0:SLAB]), "G", writes=[T_G])
                P.dma("sp", lambda e, j=j: e.dma_start(out=G[:, SLAB:NSL], in_=D["gT"][j * 128:(j + 1) * 128, NS:NS + NPR]), "G2", writes=[T_G])
                def gates_act(d):
                    R, I, M = Rb[d], Ib[d], Mb[d]
                    for gi_, (dst, TB) in enumerate(((R, TB_R[d]), (I, TB_I[d]))):
                        for b in BLK[d]:
                            bs = slice(b * 512, (b + 1) * 512)
                            ps, T_ps = bank.next()
                            P.op("pe", lambda e, ps=ps, d=d, gi_=gi_, j=j, bs=bs, XCB=XCB: e.matmul(
                                ps[:], lhsT=wg[:, (d * 2 + gi_) * 4 + j, :], rhs=XCB[:, bs], start=True, stop=True),
                                reads=[T_wg, TB_XCB[x][b]], writes=[T_ps])
                            P.op("act", lambda e, ps=ps, dst=dst, d=d, gi_=gi_, j=j, c_=cs(d, b): e.activation(
                                out=dst[:, c_], in_=ps[:], func=AF.Sigmoid,
                                bias=bg[:, d, gi_, j:j + 1], scale=1.0), reads=[T_ps, T_bg], writes=[TB[b]])
                    for b in BLK[d]:
                        P.op("act", lambda e, d=d, j=j, c_=cs(d, b), R=R, M=M: e.activation(
                            out=M[:, c_], in_=R[:, c_], func=AF.Exp, scale=sp2[:, d, j:j + 1]),
                            reads=[TB_R[d][b], T_sp2], writes=[TB_M[d][b]])
                    for b in BLK[d]:
                        P.op("act", lambda e, d=d, j=j, c_=cs(d, b), R=R: e.activation(
                            out=R[:, c_], in_=R[:, c_], func=AF.Exp, scale=sp_[:, d, j:j + 1]),
                            reads=[TB_R[d][b], T_sp], writes=[TB_R[d][b]])

                def ew(d):
                    R, I, M = Rb[d], Ib[d], Mb[d]
                    for b in BLK[d]:
                        P.op("act", lambda e, c_=cs(d, b), M=M: e.activation(out=M[:, c_], in_=M[:, c_], func=AF.Relu, bias=cst[:, 0:1], scale=-1.0),
                             reads=[TB_M[d][b], T_cst], writes=[TB_M[d][b]])
                    for b in BLK[d]:
                        bs = slice(b * 512, (b + 1) * 512)
                        P.op("dve", lambda e, c_=cs(d, b), bs=bs, I=I, XC=XC: e.tensor_tensor(out=I[:, c_], in0=I[:, c_], in1=XC[:, bs], op=ALU.mult),
                             reads=[TB_I[d][b], T_XC], writes=[TB_I[d][b]])
                    for b in BLK[d]:
                        P.op("act", lambda e, c_=cs(d, b), M=M: e.activation(out=M[:, c_], in_=M[:, c_], func=AF.Sqrt),
                             reads=[TB_M[d][b]], writes=[TB_M[d][b]])
                    for b in BLK[d]:
                        P.op("pool", lambda e, c_=cs(d, b), I=I, M=M: e.tensor_tensor(out=I[:, c_], in0=I[:, c_], in1=M[:, c_], op=ALU.mult),
                             reads=[TB_I[d][b], TB_M[d][b]], writes=[TB_I[d][b]])

                def scan(d):
                    R, I, M = Rb[d], Ib[d], Mb[d]
                    if d == 0:
                        sg = [(0, SLAB, 0), (2560, 2816, 8), (2816, 3072, 8)]
                    else:
                        sg = [(0, NS, None), (NS, NS + 256, 8), (NS + 256, NS + 512, 8)]
                    for si, (s0, s1, bb) in enumerate(sg):
                        init = st0[:, j * 2 + d:j * 2 + d + 1] if si == 0 else 0.0
                        if si == 0:
                            rb = [0, 1, 2, 3, 4] if d == 0 else list(range(8))
                        else:
                            rb = [8]
                        rd = [TB_R[d][b] for b in rb] + [TB_I[d][b] for b in rb] + [TB_M[d][b] for b in rb] + [T_st0]
                        if d == 0:
                            P.op("dve", lambda e, s0=s0, s1=s1, init=init, R=R, I=I, M=M: e.tensor_tensor_scan(
                                out=M[:, s0:s1], data0=R[:, s0:s1], data1=I[:, s0:s1], initial=init,
                                op0=ALU.mult, op1=ALU.add), reads=rd, writes=[T_H[0]] + [TB_M[d][b] for b in rb])
                        else:
                            P.op("dve", lambda e, s0=s0, s1=s1, init=init, R=R, I=I, M=M: e.tensor_tensor_scan(
                                out=M[:, s0:s1][:, ::-1], data0=R[:, s0:s1][:, ::-1], data1=I[:, s0:s1][:, ::-1], initial=init,
                                op0=ALU.mult, op1=ALU.add), reads=rd, writes=[T_H[1]] + [TB_M[d][b] for b in rb])

                gates_act(0)
                ew(0)
                gates_act(1)
                if j + 1 < 4:
                    load_conv(j + 1)
                scan(0)
                ew(1)
                scan(1)
                HA, HB = Mb[0], Mb[1]
                HA_T = [T_H[0]] + [TB_M[0][b] for b in BLK[0]]
                HB_T = [T_H[1]] + [TB_M[1][b] for b in range(NB)]
                for s_ in range(2):
                    P.op("pool", lambda e, j=j, s_=s_: e.tensor_copy(out=nst[:, j, s_, 0:1], in_=HA[:, 2560 + s_ * 256 + 255:2560 + s_ * 256 + 256]),
                         reads=HA_T, writes=[T_nst])
                    P.op("pool", lambda e, j=j, s_=s_: e.tensor_copy(out=nst[:, j, s_, 1:2], in_=HB[:, NS + s_ * 256:NS + s_ * 256 + 1]),
                         reads=HB_T, writes=[T_nst])
                P.op("pool", lambda e: e.tensor_tensor(out=HA[:, 0:SLAB], in0=HA[:, 0:SLAB], in1=HB[:, 0:SLAB], op=ALU.add),
                     reads=HA_T + HB_T, writes=[T_H[0]] + [TB_M[0][b] for b in (0, 1, 2, 3, 4)])
                P.op("pool", lambda e: e.tensor_tensor(out=HA[:, 2560:3072], in0=HA[:, 2560:3072], in1=HB[:, NS:NT], op=ALU.add),
                     reads=HA_T + HB_T, writes=[T_H[0], TB_M[0][8]])
                P.op("dve", lambda e: e.tensor_tensor(out=YA[:, 0:SLAB], in0=HA[:, 0:SLAB], in1=G[:, 0:SLAB], op=ALU.mult),
                     reads=HA_T + [T_G], writes=[T_YA])
                P.op("dve", lambda e: e.tensor_tensor(out=YA[:, SLAB:NSL], in0=HA[:, 2560:3072], in1=G[:, SLAB:NSL], op=ALU.mult),
                     reads=HA_T + [T_G], writes=[T_YA])
                P.dma("sp", lambda e, j=j: e.dma_start(out=D["yT"][j * 128:(j + 1) * 128, :], in_=YA[:]), "YA", reads=[T_YA])
            P.dma("sp", lambda e: e.dma_start(out=D["nst"], in_=nst[:]), "nst", reads=[T_nst])
            P.emit()

        def post_phase(l, tag, wout_name, bout_row, ysrc, ntiles, xsrc_fn, dst_fn):
            with contextlib.ExitStack() as es:
                sb = lambda n, s, d: es.enter_context(nc.sbuf_tensor(tag + n, list(s), d))
                S = mk_ln_scratch(es, tag, nx=5)
                S["ptr"] = Ring([(pb[7], T_pb[7])])
                wout = sb("wout", [128, 8, 1024], BF16); T_wout = Tok()
                lnB = sb("lnB", [128, 4, 1024], F32); T_lnB = Tok()
                gts = sb("gts", [128, 4, 1024], F32); T_gts = Tok()
                brow = sb("brow", [1, 2, 1024], BF16); T_brow = Tok()
                ones = sb("ones", [1, 128], BF16); T_ones = Tok()
                b1F = sb("b1F", [128, 32], F32); T_b1F = Tok()
                wsl = Ring([(sb("wsl%d" % i, [128, 8, 512], BF16), (Tok(), Tok())) for i in range(3)])
                yTg = [sb("yTg%d" % i, [128, 8, 512], BF16) for i in range(2)]; T_yTg = [Tok(), Tok()]
                hT = [sb("hT%d" % i, [128, 8, 512], BF16) for i in range(2)]; T_hT = [Tok(), Tok()]
                uT = sb("uT", [128, 32, 512], BF16); T_uT = Tok()
                x1 = [[sb("x1_%d_%d" % (a, i), [128, 1024], F32) for i in range(4)] for a in range(2)]
                T_x1 = [[Tok() for _ in range(4)] for a in range(2)]
                tmp = Ring([(sb("tmp%d" % i, [128, 1024], F32), Tok()) for i in range(4)])
                rl = Ring([(sb("rl%d" % i, [128, 512], F32), Tok()) for i in range(2)])
                P.dma("pool", lambda e: e.dma_start(out=wout[:], in_=D[wout_name].rearrange("(c p) n -> p c n", p=128)),
                      tag + "wout", writes=[T_wout])
                P.dma("sp", lambda e: e.dma_start(out=lnB[:], in_=D["lnB"][l].rearrange("v p n -> p v n")), tag + "lnB", writes=[T_lnB])
                P.dma("sp", lambda e: e.dma_start(out=gts[:], in_=D["gsc"][l].rearrange("v p n -> p v n")), tag + "gts", writes=[T_gts])
                P.dma("pool", lambda e: e.dma_start(out=brow[:, 0, :], in_=D[bout_row]), tag + "brow", writes=[T_brow])
                P.dma("pool", lambda e: e.dma_start(out=brow[:, 1, :], in_=D["b2row"][l:l + 1, :]), tag + "brow", writes=[T_brow])
                P.dma("sp", lambda e: e.dma_start(out=b1F[:], in_=D["b1F"][:, l, :]), tag + "b1F", writes=[T_b1F])
                P.op("dve", lambda e: e.memset(ones[:], 1.0), writes=[T_ones])
                ngrp = (ntiles + 3) // 4
                print("SBUF remaining in post_phase", tag, nc.sbuf_bytes_remaining)
                bank = Ring([(pb[i], T_pb[i]) for i in (0, 1)])
                bankA = Ring([(pb[2], T_pb[2]), (pb[5], T_pb[5])])
                tb = [(pb[i], T_pb[i]) for i in (3, 4, 6, 7)]
                xhs = {}

                def gi(g):
                    nt = min(4, ntiles - g * 4)
                    return nt, nt * 128, g * 512

                def rsel(t):
                    return 0 if t < ntiles - 4 else 1

                def A1(g):
                    nt, T, c0 = gi(g)
                    a = g % 2
                    P.dma("sp", lambda e, c0=c0, T=T, a=a: e.dma_start(
                        out=yTg[a][:, :, 0:T], in_=ysrc[:, c0:c0 + T].rearrange("(c p) t -> p c t", p=128)),
                        tag + "yTg%d" % a, writes=[T_yTg[a]])
                    for tt in range(nt):
                        P.dma("sp", lambda e, a=a, tt=tt, t=g * 4 + tt: e.dma_start(out=x1[a][tt][:], in_=xsrc_fn(t)),
                              tag + "x1_%d_%d" % (a, tt), writes=[T_x1[a][tt]])
                    tms = []
                    for tt in range(nt):
                        r = rsel(g * 4 + tt)
                        tm, T_tm = tmp.next()
                        tms.append((tm, T_tm))
                        for half in range(2):
                            ps, T_ps = bankA.next()
                            for c in range(8):
                                P.op("pe", lambda e, ps=ps, c=c, tt=tt, half=half, a=a: e.matmul(
                                    ps[:], lhsT=yTg[a][:, c, tt * 128:(tt + 1) * 128], rhs=wout[:, c, half * 512:(half + 1) * 512],
                                    start=(c == 0), stop=False), reads=[T_yTg[a], T_wout], writes=[T_ps], sig=False)
                            P.op("pe", lambda e, ps=ps, half=half: e.matmul(
                                ps[:], lhsT=ones[0:1, :], rhs=brow[0:1, 0, half * 512:(half + 1) * 512], start=False, stop=True),
                                reads=[T_ones, T_brow], writes=[T_ps])
                            P.op("dve", lambda e, ps=ps, tm=tm, half=half, r=r: e.tensor_tensor(
                                out=tm[:, half * 512:(half + 1) * 512], in0=ps[:], in1=gts[:, r * 2 + 0, half * 512:(half + 1) * 512],
                                op=ALU.mult), reads=[T_ps, T_gts], writes=[T_tm])
                    for tt in range(nt):
                        tm, T_tm = tms[tt]
                        P.op("dve", lambda e, a=a, tt=tt, tm=tm: e.scalar_tensor_tensor(
                            out=tm[:], in0=x1[a][tt][:], scalar=ALPHA, in1=tm[:], op0=ALU.mult, op1=ALU.add),
                            reads=[T_x1[a][tt], T_tm], writes=[T_tm])
                    x1s = [(x1[a][tt], T_x1[a][tt]) for tt in range(nt)]
                    ln_affine_multi(S, tms, lnB[:, 0, :], lnB[:, 1, :], T_lnB, x1s)
                    mvs = ln_stats_multi(S, x1s)
                    for tt in range(nt):
                        mv, T_mv = mvs[tt]
                        xh, T_xh = S["xh"].next()
                        P.op("act", lambda e, xh=xh, mv=mv, a=a, tt=tt: e.activation(
                            out=xh[:], in_=x1[a][tt][:], func=AF.Identity, bias=mv[:, 2:3], scale=mv[:, 1:2]),
                            reads=[T_x1[a][tt], T_mv], writes=[T_xh])
                        xhs[(g, tt)] = (xh, T_xh)

                def A2(g):
                    nt, T, c0 = gi(g)
                    a = g % 2
                    rs = [rsel(g * 4 + tt) for tt in range(nt)]
                    for tt in range(nt):
                        xh, T_xh = xhs[(g, tt)]
                        for c in range(8):
                            bk, T_bk = tb[c // 2]
                            col = ((c % 2) * 4 + tt) * 128
                            P.op("pe", lambda e, c=c, xh=xh, bk=bk, col=col: e.transpose(
                                out=bk[:].bitcast(BF16)[:, col:col + 128], in_=xh[:, c * 128:(c + 1) * 128], identity=ident[:]),
                                reads=[T_xh, T_ident], writes=[T_bk], sig=(tt == nt - 1 or c == 7))
                    runs = []
                    t0 = 0
                    for tt in range(1, nt + 1):
                        if tt == nt or rs[tt] != rs[t0]:
                            runs.append((t0, tt, rs[t0]))
                            t0 = tt
                    for c in range(8):
                        bk, T_bk = tb[c // 2]
                        for (a_, b_, r) in runs:
                            col = ((c % 2) * 4 + a_) * 128
                            P.op("act", lambda e, c=c, bk=bk, col=col, a_=a_, b_=b_, r=r, a=a: e.activation(
                                out=hT[a][:, c, a_ * 128:b_ * 128], in_=bk[:].bitcast(BF16)[:, col:col + (b_ - a_) * 128],
                                func=AF.Identity, bias=modF[l][:, 3 * 8 + c, r:r + 1], scale=modF[l][:, 4 * 8 + c, r:r + 1]),
                                reads=[T_bk, T_modF[l]], writes=[T_hT[a]])

                W1L = {}

                def load_w1(g, blk):
                    if (g, blk) in W1L or blk >= 8 or g >= ngrp:
                        return
                    ws, T_ws = wsl.next()
                    P.dma("sp", lambda e, ws=ws, blk=blk: e.dma_start(
                        out=ws[:], in_=D["w1b"][l, :, blk * 512:(blk + 1) * 512].rearrange("(c p) n -> p c n", p=128)),
                        tag + "wsl%d" % ((wsl.i - 1) % 3), writes=list(T_ws))
                    W1L[(g, blk)] = (ws, T_ws)

                def P1(g, blks):
                    nt, T, c0 = gi(g)
                    a = g % 2
                    for blk in blks:
                        load_w1(g, blk)
                        load_w1(g, blk + 1)
                        ws, T_ws = W1L[(g, blk)]
                        for f in range(4):
                            ps, T_ps = bank.next()
                            for c in range(8):
                                P.op("pe", lambda e, ps=ps, ws=ws, f=f, c=c, T=T, a=a: e.matmul(
                                    ps[:, 0:T], lhsT=ws[:, c, f * 128:(f + 1) * 128], rhs=hT[a][:, c, 0:T],
                                    start=(c == 0), stop=(c == 7)), reads=[T_ws[0], T_ws[1], T_hT[a]], writes=[T_ps], sig=(c == 7))
                            rt, T_rt = rl.next()
                            fi = blk * 4 + f
                            P.op("act", lambda e, ps=ps, rt=rt, fi=fi, T=T: e.activation(
                                out=rt[:, 0:T], in_=ps[:, 0:T], func=AF.Relu, bias=b1F[:, fi:fi + 1], scale=1.0),
                                reads=[T_ps, T_b1F], writes=[T_rt])
                            P.op("pool", lambda e, rt=rt, fi=fi, T=T: e.tensor_tensor(
                                out=uT[:, fi, 0:T], in0=rt[:, 0:T], in1=rt[:, 0:T], op=ALU.mult),
                                reads=[T_rt], writes=[T_uT])

                W2L = {}

                def load_w2(g, blk):
                    if (g, blk) in W2L or blk >= 8:
                        return
                    ws, T_ws = wsl.next()
                    P.dma("sp", lambda e, ws=ws, blk=blk: e.dma_start(
                        out=ws[:, 0:4, :],
                        in_=D["w2b"][l, blk * 512:(blk + 1) * 512, 0:512].rearrange("(c p) n -> p c n", p=128)),
                        tag + "wsl%d" % ((wsl.i - 1) % 3), writes=[T_ws[0]])
                    P.dma("sp", lambda e, ws=ws, blk=blk: e.dma_start(
                        out=ws[:, 4:8, :],
                        in_=D["w2b"][l, blk * 512:(blk + 1) * 512, 512:1024].rearrange("(c p) n -> p c n", p=128)),
                        tag + "wslh%d" % ((wsl.i - 1) % 3), writes=[T_ws[1]])
                    W2L[(g, blk)] = (ws, T_ws)

                def P2(g):
                    nt, T, c0 = gi(g)
                    for blk in range(8):
                        load_w2(g, blk)
                        load_w2(g, blk + 1)
                        ws, T_ws = W2L[(g, blk)]
                        for tt in range(nt):
                            for half in range(2):
                                ps, T_ps = pb[tt * 2 + half], T_pb[tt * 2 + half]
                                for c in range(4):
                                    P.op("pe", lambda e, ps=ps, ws=ws, blk=blk, c=c, tt=tt, half=half: e.matmul(
                                        ps[:], lhsT=uT[:, blk * 4 + c, tt * 128:(tt + 1) * 128],
                                        rhs=ws[:, half * 4 + c, :],
                                        start=(blk == 0 and c == 0), stop=False),
                                        reads=[T_uT, T_ws[half]], writes=[T_ps], sig=(c == 3))
                    for tt in range(nt):
                        for half in range(2):
                            ps, T_ps = pb[tt * 2 + half], T_pb[tt * 2 + half]
                            P.op("pe", lambda e, ps=ps, half=half: e.matmul(
                                ps[:], lhsT=ones[0:1, :], rhs=brow[0:1, 1, half * 512:(half + 1) * 512], start=False, stop=True),
                                reads=[T_ones, T_brow], writes=[T_ps])

                TAILS = {}

                def tail(g):
                    nt, T, c0 = gi(g)
                    a = g % 2
                    tms = []
                    for tt in range(nt):
                        r = rsel(g * 4 + tt)
                        tm, T_tm = tmp.next()
                        tms.append((tm, T_tm))
                        for half in range(2):
                            ps, T_ps = pb[tt * 2 + half], T_pb[tt * 2 + half]
                            P.op("dve", lambda e, ps=ps, tm=tm, half=half, r=r: e.tensor_tensor(
                                out=tm[:, half * 512:(half + 1) * 512], in0=ps[:], in1=gts[:, r * 2 + 1, half * 512:(half + 1) * 512],
                                op=ALU.mult), reads=[T_ps, T_gts], writes=[T_tm])
                    for tt in range(nt):
                        tm, T_tm = tms[tt]
                        P.op("dve", lambda e, tm=tm, tt=tt, a=a: e.scalar_tensor_tensor(
                            out=tm[:], in0=x1[a][tt][:], scalar=ALPHA, in1=tm[:], op0=ALU.mult, op1=ALU.add),
                            reads=[T_x1[a][tt], T_tm], writes=[T_tm])
                    TAILS[g] = (tms, tmp.i)

                def tail_rest(g):
                    nt, T, c0 = gi(g)
                    tms, tmp_i = TAILS[g]
                    ln_affine_multi(S, tms, lnB[:, 2, :], lnB[:, 3, :], T_lnB, tms)
                    for tt in range(nt):
                        t = g * 4 + tt
                        tm, T_tm = tms[tt]
                        P.dma("pool", lambda e, tm=tm, t=t: e.dma_start(out=dst_fn(t), in_=tm[:]),
                              tag + "tmp%d" % ((tmp_i - nt + tt) % 4), reads=[T_tm])

                A1(0)
                A2(0)
                for g in range(ngrp):
                    load_w1(g, 0)
                    load_w1(g, 1)
                    P1(g, range(0, 8))
                    load_w2(g, 0)
                    if g + 1 < ngrp:
                        A1(g + 1)
                    P2(g)
                    tail(g)
                    if g + 1 < ngrp:
                        A2(g + 1)
                    tail_rest(g)
                P.emit()

        def xsrc0(t):
            if t < 18:
                return D["xs"][t * 128:(t + 1) * 128, :]
            return D["xp"][(t - 18) * 128:(t - 17) * 128, :]

        post_phase(0, "d0", "e_w_out", "ebrow", D["yT"], 22, xsrc0, lambda t: D["x1"][t * 128:(t + 1) * 128, :])

        with contextlib.ExitStack() as es:
            sb = lambda n, s, d: es.enter_context(nc.sbuf_tensor("a1" + n, list(s), d))
            S = mk_ln_scratch(es, "a1", nx=5)
            S["tbanks"] = [(pb[i], T_pb[i]) for i in (4, 5, 6, 7)]
            wq = sb("wq", [128, 8, 3072], BF16); T_wq = Tok()
            bqk = sb("bqk", [128, 16], F32); T_bqk = Tok()
            bq8 = sb("bq8", [128, 8], F32); T_bq8 = Tok()
            bvB = sb("bvB", [128, 1024], F32); T_bvB = Tok()
            xin = Ring([(sb("xin%d" % i, [128, 1024], F32), Tok()) for i in range(6)])
            hTr = Ring([(sb("hT%d" % i, [128, 8, 512], BF16), Tok()) for i in range(2)])
            qm = Ring([(sb("qm%d" % i, [128, 2, 512], BF16), Tok()) for i in range(2)])
            kf = Ring([(sb("kf%d" % i, [128, 512], F32), Tok()) for i in range(2)])
            kb = Ring([(sb("kb%d" % i, [128, 512], BF16), Tok()) for i in range(2)])
            vf = Ring([(sb("vf%d" % i, [128, 1024], F32), Tok()) for i in range(2)])
            vE = Ring([(sb("vE%d" % i, [128, 16, 128], BF16), Tok()) for i in range(2)])
            T_wqp = [Tok(), Tok(), Tok()]
            for part in range(3):
                P.dma("pool", lambda e, part=part: e.dma_start(
                    out=wq[:, :, part * 1024:(part + 1) * 1024],
                    in_=D["o_w_qkv"][:, part * 1024:(part + 1) * 1024].rearrange("(c p) n -> p c n", p=128)),
                    "wq%d" % part, writes=[T_wqp[part]])
            P.dma("sp", lambda e: e.dma_start(out=bqk[:], in_=D["bqkF"]), "bqk", writes=[T_bqk])
            P.dma("sp", lambda e: e.dma_start(out=bvB[:], in_=D["bvB"]), "bvB", writes=[T_bvB])
            P.op("dve", lambda e: e.tensor_scalar(out=bq8[:], in0=bqk[:, 0:8], scalar1=0.125, scalar2=None, op0=ALU.mult),
                 reads=[T_bqk], writes=[T_bq8])
            for (q_, T_q) in qm.items:
                P.op("pool", lambda e, q_=q_: e.memset(q_[:], 0.0), writes=[T_q])
            for (v_, T_v) in vE.items:
                P.op("pool", lambda e, v_=v_: e.memset(v_[:], 1.0), writes=[T_v])
            bank = Ring([(pb[i], T_pb[i]) for i in range(4)])
            S["mt"] = Ring([(sb("mt%d" % i, [128, 8, 128], F32), Tok()) for i in range(2)])

            def LNG1a(grp):
                nt = 4 if grp < 5 else 2
                ts = [grp * 4 + tt for tt in range(nt)]
                srcs = [D["x1"][t * 128:(t + 1) * 128, :] for t in ts]
                return ln_group_1(S, srcs, xin, "a1xin")

            def LNG1b(grp, xhl):
                nt = 4 if grp < 5 else 2
                ts = [grp * 4 + tt for tt in range(nt)]
                hT, T_hT = hTr.next()
                ln_group_2(S, xhl, 1, 0, [0 if t < 18 else 1 for t in ts], hT, T_hT)
                return hT, T_hT

            nxt = LNG1b(0, LNG1a(0))
            for grp in range(6):
                nt = 4 if grp < 5 else 2
                T = nt * 128
                c0 = grp * 512
                hT, T_hT = nxt
                if grp + 1 < 6:
                    xhl_n = LNG1a(grp + 1)
                for co in range(16):
                    ps, T_ps = bank.next()
                    for c in range(8):
                        P.op("pe", lambda e, ps=ps, co=co, c=c, hT=hT, T=T: e.matmul(
                            ps[:, 0:T], lhsT=wq[:, c, co * 128:(co + 1) * 128], rhs=hT[:, c, 0:T],
                            start=(c == 0), stop=(c == 7)), reads=[T_wqp[co // 8], T_hT], writes=[T_ps], sig=(c == 7))
                    if co < 8:
                        q_, T_q = qm.next()
                        for hh in range(2):
                            P.op("act", lambda e, ps=ps, q_=q_, co=co, hh=hh, T=T: e.activation(
                                out=q_[hh * 64:(hh + 1) * 64, hh, 0:T], in_=ps[hh * 64:(hh + 1) * 64, 0:T], func=AF.Identity,
                                bias=bq8[hh * 64:(hh + 1) * 64, co:co + 1], scale=0.125), reads=[T_ps, T_bq8], writes=[T_q])
                        if grp < 4:
                            qc0, qn, off = c0, T, 0
                        elif grp == 4:
                            qc0, qn, off = None, 0, 0
                        else:
                            qc0, qn, off = None, 0, 0
                        if grp < 4:
                            for hh, nm in enumerate(("qA", "qB")):
                                P.dma("act", lambda e, q_=q_, co=co, hh=hh, nm=nm, c0=c0, T=T: e.dma_start(
                                    out=D[nm][co * 128:(co + 1) * 128, c0:c0 + T], in_=q_[:, hh, 0:T]),
                                    "qm%d" % ((qm.i - 1) % 2), reads=[T_q])
                        elif grp == 4:
                            for hh, nm in enumerate(("qA", "qB")):
                                P.dma("act", lambda e, q_=q_, co=co, hh=hh, nm=nm: e.dma_start(
                                    out=D[nm][co * 128:(co + 1) * 128, 2048:2304], in_=q_[:, hh, 256:512]),
                                    "qm%d" % ((qm.i - 1) % 2), reads=[T_q])
                        else:
                            for hh, nm in enumerate(("qA", "qB")):
                                P.dma("act", lambda e, q_=q_, co=co, hh=hh, nm=nm: e.dma_start(
                                    out=D[nm][co * 128:(co + 1) * 128, 2304:2560], in_=q_[:, hh, 0:256]),
                                    "qm%d" % ((qm.i - 1) % 2), reads=[T_q])
                    else:
                        ck = co - 8
                        kt, T_kt = kf.next()
                        P.op("act", lambda e, ps=ps, kt=kt, co=co, T=T: e.activation(
                            out=kt[:, 0:T], in_=ps[:, 0:T], func=AF.Identity, bias=bqk[:, co:co + 1], scale=1.0),
                            reads=[T_ps, T_bqk], writes=[T_kt])
                        kbt, T_kbt = kb.next()
                        P.op("pool", lambda e, kt=kt, kbt=kbt, T=T: e.tensor_copy(out=kbt[:, 0:T], in_=kt[:, 0:T]),
                             reads=[T_kt], writes=[T_kbt])
                        P.dma("pool", lambda e, kbt=kbt, ck=ck, c0=c0, T=T: e.dma_start(
                            out=D["kT"][ck * 128:(ck + 1) * 128, c0:c0 + T], in_=kbt[:, 0:T]),
                            "kb%d" % ((kb.i - 1) % 2), reads=[T_kbt])
                        if grp == 4:
                            P.dma("act", lambda e, kt=kt, ck=ck: e.dma_start(
                                out=D["nkT"][ck * 128:(ck + 1) * 128, 0:256], in_=kt[:, 256:512]),
                                "kf%d" % ((kf.i - 1) % 2), reads=[T_kt])
                        elif grp == 5:
                            P.dma("act", lambda e, kt=kt, ck=ck: e.dma_start(
                                out=D["nkT"][ck * 128:(ck + 1) * 128, 256:512], in_=kt[:, 0:256]),
                                "kf%d" % ((kf.i - 1) % 2), reads=[T_kt])
                for tt in range(nt):
                    t = grp * 4 + tt
                    vt, T_vt = vf.next()
                    for half in range(2):
                        ps, T_ps = bank.next()
                        for c in range(8):
                            P.op("pe", lambda e, ps=ps, c=c, tt=tt, half=half, hT=hT: e.matmul(
                                ps[:], lhsT=hT[:, c, tt * 128:(tt + 1) * 128], rhs=wq[:, c, 2048 + half * 512:2048 + (half + 1) * 512],
                                start=(c == 0), stop=(c == 7)), reads=[T_wqp[2], T_hT], writes=[T_ps], sig=(c == 7))
                        P.op("dve", lambda e, ps=ps, vt=vt, half=half: e.tensor_tensor(
                            out=vt[:, half * 512:(half + 1) * 512], in0=ps[:], in1=bvB[:, half * 512:(half + 1) * 512], op=ALU.add),
                            reads=[T_ps, T_bvB], writes=[T_vt])
                    if t >= 18:
                        P.dma("sp", lambda e, vt=vt, t=t: e.dma_start(out=D["nv"][(t - 18) * 128:(t - 17) * 128, :], in_=vt[:]),
                              "vf%d" % ((vf.i - 1) % 2), reads=[T_vt])
                    ve_, T_ve = vE.next()
                    P.op("pool", lambda e, vt=vt, ve_=ve_: e.tensor_copy(
                        out=ve_[:, :, 0:64], in_=vt[:].rearrange("p (h d) -> p h d", d=64)), reads=[T_vt], writes=[T_ve])
                    P.dma("pool", lambda e, ve_=ve_, t=t: e.dma_start(
                        out=D["ve"][t * 128:(t + 1) * 128, :], in_=ve_[:].rearrange("p h d -> p (h d)")),
                        "vE%d" % ((vE.i - 1) % 2), reads=[T_ve])
                if grp + 1 < 6:
                    nxt = LNG1b(grp + 1, xhl_n)
            P.emit()

        with contextlib.ExitStack() as es:
            sb = lambda n, s, d: es.enter_context(nc.sbuf_tensor("b1" + n, list(s), d))
            Qh = Ring([(sb("Qh%d" % i, [128, NOUT], BF16), Tok()) for i in range(2)])
            Kh = Ring([(sb("Kh%d" % i, [128, NSL], BF16), Tok()) for i in range(2)])
            Vh = Ring([(sb("Vh%d" % i, [128, 22, 128], BF16), Tok()) for i in range(2)])
            Kc = Ring([(sb("Kc%d" % i, [128, 512], BF16), Tok()) for i in range(2)])
            Vc = Ring([(sb("Vc%d" % i, [128, 4, 128], BF16), Tok()) for i in range(2)])
            nab = Ring([(sb("nab%d" % i, [128, 7, 128], BF16), Tok()) for i in range(2)])
            Oh = Ring([(sb("Oh%d" % i, [64, NOUT], BF16), Tok()) for i in range(2)])
            pT = Ring([(sb("pT%d" % i, [128, 512], BF16), Tok()) for i in range(4)])
            rc = Ring([(sb("rc%d" % i, [64, 512], F32), Tok()) for i in range(2)])
            for (v_, T_v) in Vc.items:
                P.op("pool", lambda e, v_=v_: e.memset(v_[:], 1.0), writes=[T_v])
            sring = Ring([5, 6, 7])
            cvs = Ring([(sb("cvs%d" % i, [128, 8, 1024], BF16), Tok()) for i in range(2)])

            def conv_bg(kind, blk):
                slot, T_slot = cvs.next()
                key = "cv%d" % ((cvs.i - 1) % 2)
                if kind == "w1":
                    src = D["w1"][1, :, blk * 1024:(blk + 1) * 1024].rearrange("(c p) n -> p c n", p=128)
                    dst = D["w1b"][1, :, blk * 1024:(blk + 1) * 1024].rearrange("(c p) n -> p c n", p=128)
                else:
                    src = D["w2"][1, blk * 1024:(blk + 1) * 1024, :].rearrange("(c p) n -> p c n", p=128)
                    dst = D["w2b"][1, blk * 1024:(blk + 1) * 1024, :].rearrange("(c p) n -> p c n", p=128)
                P.dma("pool", lambda e, slot=slot, src=src: e.dma_start(out=slot[:], in_=src), key, writes=[T_slot])
                P.dma("pool", lambda e, slot=slot, dst=dst: e.dma_start(out=dst, in_=slot[:]), key + "s", reads=[T_slot])
            CVT = [(k, b) for k in ("w1", "w2") for b in range(4)]

            def ty_of(qb, kp):
                if qb == 0:
                    return {0: 2, 1: 3, 2: 5, 3: 6}[kp]
                if qb == 1:
                    return {0: 1, 1: 2, 2: 3, 3: 5}[kp]
                return kp - qb + 2

            def qbs_of(kp):
                q = [qb for qb in range(16) if qb >= 2 and abs(qb - kp) <= 2]
                if kp <= 3:
                    q = [0, 1] + q
                return sorted(q)

            HL = {}

            def load_head(h):
                if h >= 16:
                    return
                ch = h // 2
                hh = h % 2
                q_, T_q = Qh.next(); k_, T_k = Kh.next(); v_, T_v = Vh.next()
                kc_, T_kc = Kc.next(); vc_, T_vc = Vc.next(); nb_, T_nb = nab.next()
                i2 = (Qh.i - 1) % 2
                qn = "qA" if hh == 0 else "qB"
                P.dma("sp", lambda e, q_=q_, qn=qn, ch=ch: e.dma_start(out=q_[:], in_=D[qn][ch * 128:(ch + 1) * 128, :]),
                      "Qh%d" % i2, writes=[T_q])
                P.dma("sp", lambda e, k_=k_, ch=ch: e.dma_start(out=k_[:], in_=D["kT"][ch * 128:(ch + 1) * 128, :]),
                      "Kh%d" % i2, writes=[T_k])
                P.dma("sp", lambda e, v_=v_, h=h: e.dma_start(
                    out=v_[:], in_=D["ve"][:, h * 128:(h + 1) * 128].rearrange("(t p) c -> p t c", p=128)),
                    "Vh%d" % i2, writes=[T_v])
                P.dma("pool", lambda e, kc_=kc_, ch=ch: e.dma_start(out=kc_[:], in_=D["ckT"][ch * 128:(ch + 1) * 128, :]),
                      "Kc%d" % i2, writes=[T_kc])
                P.dma("pool", lambda e, vc_=vc_, h=h: e.dma_start(
                    out=vc_[:, :, 0:64], in_=D["cv"][:, h * 64:(h + 1) * 64].rearrange("(t p) d -> p t d", p=128)),
                    "Vc%d" % i2, writes=[T_vc])
                P.dma("pool", lambda e, nb_=nb_, h=h: e.dma_start(out=nb_[:], in_=D["nabias"][h]), "nab%d" % i2, writes=[T_nb])
                HL[h] = (q_, T_q, k_, T_k, v_, T_v, kc_, T_kc, vc_, T_vc, nb_, T_nb)

            load_head(0)
            for h in range(16):
                load_head(h + 1)
                if h % 2 == 0:
                    conv_bg(*CVT[h // 2])
                (q_, T_q, k_, T_k, v_, T_v, kc_, T_kc, vc_, T_vc, nb_, T_nb) = HL[h]
                o_, T_o = Oh.next()
                i2 = h % 2
                units = []
                for qg in range(4):
                    qs = slice(qg * 512, (qg + 1) * 512)
                    for cc in range(4):
                        def sc(bank, cc=cc, qs=qs, kc_=kc_, q_=q_):
                            return [((lambda e: e.matmul(pb[bank][:], lhsT=kc_[:, cc * 128:(cc + 1) * 128], rhs=q_[:, qs],
                                                         start=True, stop=True)), [T_kc, T_q], True)]

                        def pv(pt, qg=qg, cc=cc, vc_=vc_):
                            return [((lambda e: e.matmul(pb[qg][:], lhsT=vc_[:, cc, :], rhs=pt[:, 0:512],
                                                         start=(cc == 0), stop=False)), [T_vc], qg)]
                        units.append(dict(score=sc, ncol=512, pv=pv))
                for kp in range(18):
                    qbs = qbs_of(kp)
                    qlo, nq = qbs[0], len(qbs)
                    assert qbs == list(range(qlo, qlo + nq))
                    pieces = [(0, min(nq, 4))] + ([(4, nq)] if nq > 4 else [])
                    for (s0, s1) in pieces:
                        def sc(bank, s0=s0, s1=s1, kp=kp, qlo=qlo, k_=k_, q_=q_, nb_=nb_):
                            ops = [((lambda e: e.matmul(
                                pb[bank][:, 0:(s1 - s0) * 128], lhsT=k_[:, kp * 128:(kp + 1) * 128],
                                rhs=q_[:, (qlo + s0) * 128:(qlo + s1) * 128], start=True, stop=False)), [T_k, T_q], False)]
                            i = s0
                            while i < s1:
                                ti0 = 6 - ty_of(qlo + i, kp)
                                n = 1
                                while i + n < s1 and (6 - ty_of(qlo + i + n, kp)) == ti0 + n:
                                    n += 1
                                last = (i + n >= s1)
                                ops.append(((lambda e, i=i, n=n, ti0=ti0, last=last: e.matmul(
                                    pb[bank][:, (i - s0) * 128:(i - s0 + n) * 128], lhsT=ident[:],
                                    rhs=nb_[:, ti0:ti0 + n, :].rearrange("p a b -> p (a b)"), start=False, stop=last)),
                                    [T_ident, T_nb], last))
                                i += n
                            return ops

                        def pv(pt, s0=s0, s1=s1, kp=kp, qlo=qlo, v_=v_):
                            ops = []
                            i = s0
                            while i < s1:
                                qb0 = qlo + i
                                bk = qb0 // 4
                                n = min(s1 - i, 4 - (qb0 % 4))
                                ops.append(((lambda e, bk=bk, qb0=qb0, n=n, i=i: e.matmul(
                                    pb[bk][:, (qb0 % 4) * 128:(qb0 % 4 + n) * 128], lhsT=v_[:, kp, :],
                                    rhs=pt[:, (i - s0) * 128:(i - s0 + n) * 128], start=False, stop=False)), [T_v], bk))
                                i += n
                            return ops
                        units.append(dict(score=sc, ncol=(s1 - s0) * 128, pv=pv))
                for s_ in range(2):
                    qs = slice(OWN + s_ * 256, OWN + (s_ + 1) * 256)

                    def sc(bank, s_=s_, qs=qs, k_=k_, q_=q_):
                        ops = []
                        for kt in range(2):
                            tile_i = 18 + 2 * s_ + kt
                            ops.append(((lambda e, kt=kt, tile_i=tile_i: e.matmul(
                                pb[bank][:, kt * 256:(kt + 1) * 256], lhsT=k_[:, tile_i * 128:(tile_i + 1) * 128], rhs=q_[:, qs],
                                start=True, stop=True)), [T_k, T_q], kt == 1))
                        return ops

                    def pv(pt, s_=s_, v_=v_):
                        ops = []
                        for kt in range(2):
                            tile_i = 18 + 2 * s_ + kt
                            ops.append(((lambda e, kt=kt, tile_i=tile_i: e.matmul(
                                pb[4][:, s_ * 256:(s_ + 1) * 256], lhsT=v_[:, tile_i, :], rhs=pt[:, kt * 256:(kt + 1) * 256],
                                start=(kt == 0), stop=(kt == 1))), [T_v], 4))
                        return ops
                    units.append(dict(score=sc, ncol=512, pv=pv))

                def emit_score(u):
                    u["bank"] = sring.next()
                    for fn, rd, sg in u["score"](u["bank"]):
                        P.op("pe", fn, reads=rd, writes=[T_pb[u["bank"]]], sig=sg)

                def emit_exp(u):
                    pt, T_pt = pT.next()
                    u["pt"], u["T_pt"] = pt, T_pt
                    bank, ncol = u["bank"], u["ncol"]
                    P.op("act", lambda e, pt=pt, bank=bank, ncol=ncol: e.activation(
                        out=pt[:, 0:ncol], in_=pb[bank][:, 0:ncol], func=AF.Exp), reads=[T_pb[bank]], writes=[T_pt])

                def emit_pv(u):
                    ops = u["pv"](u["pt"])
                    for io, (fn, rd, bk) in enumerate(ops):
                        P.op("pe", fn, reads=rd + [u["T_pt"]], writes=[T_pb[bk]], sig=(io == len(ops) - 1))

                emit_score(units[0]); emit_score(units[1])
                for iu, u in enumerate(units):
                    emit_exp(u)
                    if iu + 2 < len(units):
                        emit_score(units[iu + 2])
                    emit_pv(u)
                for bk in range(5):
                    r_, T_r = rc.next()
                    P.op("dve", lambda e, r_=r_, bk=bk: e.reciprocal(out=r_[:], in_=pb[bk][64:128, :]),
                         reads=[T_pb[bk]], writes=[T_r])
                    P.op("dve", lambda e, r_=r_, bk=bk, o_=o_: e.tensor_tensor(
                        out=o_[:, bk * 512:(bk + 1) * 512], in0=pb[bk][0:64, :], in1=r_[:], op=ALU.mult),
                        reads=[T_pb[bk], T_r], writes=[T_o])
                P.dma("sp", lambda e, o_=o_, h=h: e.dma_start(out=D["oT"][h * 64:(h + 1) * 64, :], in_=o_[:]),
                      "Oh%d" % i2, reads=[T_o])
            P.emit()

        def dst1(t):
            if t < 16:
                return D["ys"][t * 128:(t + 1) * 128, :]
            return D["yp"][(t - 16) * 128:(t - 15) * 128, :]

        def xsrc1(t):
            if t < 16:
                return D["x1"][t * 128:(t + 1) * 128, :]
            return D["x1"][(t + 2) * 128:(t + 3) * 128, :]

        post_phase(1, "d1", "o_w_out", "obrow", D["oT"], 20, xsrc1, dst1)
    except _Stop:
        pass
    return nc


_NC_CACHE = {}


def _fm(vec, nch):
    return np.ascontiguousarray(np.asarray(vec, np.float32).reshape(nch, 128).T)


def _bc(vec):
    return np.ascontiguousarray(np.broadcast_to(np.asarray(vec, np.float32)[None, :], (128, vec.shape[-1])))


def _dft_tables():
    if "dft" in _NC_CACHE:
        return _NC_CACHE["dft"]
    bf = ml_dtypes.bfloat16
    out = {}
    for p in range(2):
        n = (np.arange(NS, dtype=np.int64) + p)[:, None]
        k = (np.arange(SLAB, dtype=np.int64) + p)[None, :]
        ang = 2.0 * np.pi * ((n * k) % NS).astype(np.float64) / NS
        tab = np.stack([np.cos(ang) / 64.0, -np.sin(ang) / 64.0], axis=1)
        out["S%d" % p] = np.ascontiguousarray(tab.astype(np.float32).astype(bf))
        n = (np.arange(256, dtype=np.int64) + p)[:, None]
        k = (np.arange(256, dtype=np.int64) + p)[None, :]
        ang = 2.0 * np.pi * ((n * k) % 256).astype(np.float64) / 256
        tab = np.stack([np.cos(ang) / 16.0, -np.sin(ang) / 16.0], axis=1)
        out["P%d" % p] = np.ascontiguousarray(tab.astype(np.float32).astype(bf))
    c = np.arange(128, dtype=np.int64)[:, None]
    l_ = np.arange(128, dtype=np.int64)[None, :]
    ang = 2.0 * np.pi * ((c * l_) % 128).astype(np.float64) / 128
    s = 1.0 / np.sqrt(128.0)
    out["C"] = np.ascontiguousarray(np.concatenate([np.cos(ang) * s, np.sin(ang) * s], axis=1).astype(np.float32).astype(bf))
    out["I"] = np.eye(128, dtype=np.float32).astype(bf)
    _NC_CACHE["dft"] = out
    return out


def _na_bias(rpb, p):
    reps = [(8, 6), (8, 7), (8, 8), (8, 9), (8, 10), (1, 3), (0, 3)]
    kk = np.arange(128)
    kr, kc = kk // 64, kk % 64
    out = np.empty((16, 128, 7, 128), np.float32)
    for ty, (qb, kp) in enumerate(reps):
        i = (2 * qb + kr)[None, :]
        qc = kc[None, :]
        a = (2 * kp + kr)[:, None]
        kcc = kc[:, None]
        if p == 0:
            gi, ga, gq, gk = i, a, qc, kcc
        else:
            gi, ga, gq, gk = 63 - i, 63 - a, 63 - qc, 63 - kcc
        rs = np.clip(gi - 4, 0, 56)
        cs = np.clip(gq - 8, 0, 48)
        valid = (ga >= rs) & (ga < rs + 8) & (gk >= cs) & (gk < cs + 16)
        ro = np.clip(ga - gi + 7, 0, 14)
        co = np.clip(gk - gq + 15, 0, 30)
        ro_b, co_b = np.broadcast_arrays(ro, co)
        vals = rpb[:, ro_b, co_b]
        out[:, :, 6 - ty, :] = np.where(valid[None], vals, np.float32(-30000.0))
    return out


def kernel(x_prompt, x_sample, c, state_lru, cache_k, cache_v, c_ctx,
           ada_w, ada_b, ln1_g, ln1_b, ln2_g, ln2_b, w1, b1, w2, b2,
           e_w_in, e_b_in, e_conv_w, e_conv_b, e_w_r, e_b_r, e_w_i, e_b_i, e_lam,
           e_w_out, e_b_out, o_w_qkv, o_b_qkv, o_rpb, o_w_out, o_b_out):
    f = lambda a: np.ascontiguousarray(np.asarray(a, np.float32))
    x_prompt, x_sample, c, state_lru, cache_k, cache_v, c_ctx = map(f, (x_prompt, x_sample, c, state_lru, cache_k, cache_v, c_ctx))
    ada_w, ada_b, w1, b1, w2, b2 = map(f, (ada_w, ada_b, w1, b1, w2, b2))
    tabs = _dft_tables()
    if "nc" not in _NC_CACHE:
        _NC_CACHE["nc"] = build()
    nc = _NC_CACHE["nc"]

    shared = {}
    shared["ada_w"] = ada_w
    shared["adabF"] = np.ascontiguousarray(np.stack([_fm(ada_b[l], 48) for l in range(2)], axis=1))
    shared["adabB"] = np.ascontiguousarray(np.stack(
        [np.stack([_bc(ada_b[l, 2048:3072]), _bc(ada_b[l, 5120:6144])]) for l in range(2)]))
    shared["lnB"] = np.ascontiguousarray(np.stack(
        [np.stack([_bc(f(ln1_g)[l]), _bc(f(ln1_b)[l]), _bc(f(ln2_g)[l]), _bc(f(ln2_b)[l])]) for l in range(2)]))
    shared["w1"] = w1
    shared["b1F"] = np.ascontiguousarray(np.stack([_fm(b1[l], 32) for l in range(2)], axis=1))
    shared["w2"] = w2
    shared["b2row"] = b2
    shared["e_w_in"] = f(e_w_in)[0]
    shared["e_b_inF"] = _fm(f(e_b_in)[0], 12)
    shared["convbF"] = _fm(f(e_conv_b)[0], 4)
    shared["e_w_out"] = f(e_w_out)[0]
    shared["ebrow"] = f(e_b_out)[0].reshape(1, 1024)
    shared["o_w_qkv"] = f(o_w_qkv)[0]
    shared["bqkF"] = _fm(f(o_b_qkv)[0][:2048], 16)
    shared["bvB"] = _bc(f(o_b_qkv)[0][2048:])
    shared["o_w_out"] = f(o_w_out)[0]
    shared["obrow"] = f(o_b_out)[0].reshape(1, 1024)
    shared["dftC"] = tabs["C"]
    shared["ident"] = tabs["I"]

    wr, wi = f(e_w_r)[0], f(e_w_i)[0]
    br, bi, lam = f(e_b_r)[0], f(e_b_i)[0], f(e_lam)[0]
    cw = f(e_conv_w)[0]
    rpb = f(o_rpb)[0]
    per_p = []
    for p in range(2):
        dirs = (0, 1) if p == 0 else (1, 0)
        wg = np.zeros((2, 2, 4, 128, 128), np.float32)
        for dl, dg in enumerate(dirs):
            for gi, wsrc in enumerate((wr, wi)):
                for j in range(4):
                    wg[dl, gi, j, 0:64, 0:64] = wsrc[dg, 2 * j]
                    wg[dl, gi, j, 64:128, 64:128] = wsrc[dg, 2 * j + 1]
        bgF = np.empty((128, 2, 2, 4), np.float32)
        lamF = np.empty((128, 2, 4), np.float32)
        for dl, dg in enumerate(dirs):
            bgF[:, dl, 0, :] = _fm(br[dg], 4)
            bgF[:, dl, 1, :] = _fm(bi[dg], 4)
            lamF[:, dl, :] = _fm(lam[dg], 4)
        taps5 = np.zeros((5, 512), np.float32)
        if p == 0:
            taps5[0:4] = cw
        else:
            taps5[1:5] = cw[::-1]
        tapsF = np.ascontiguousarray(taps5.reshape(5, 4, 128).transpose(2, 1, 0))
        per_p.append(dict(wgate=np.ascontiguousarray(wg.reshape(16, 128, 128).transpose(1, 0, 2)), bgateF=bgF, lamF=lamF, taps=tapsF, nabias=_na_bias(rpb, p),
                          dftS=tabs["S%d" % p], dftP=tabs["P%d" % p]))

    in_maps = []
    for core in range(8):
        b, p = core // 2, core % 2
        m = dict(shared)
        m.update(per_p[p])
        xs = x_sample[b]
        xp = x_prompt[2 * core:2 * core + 2]
        if p == 1:
            xs = xs[::-1]
            xp = xp[:, ::-1]
        m["xs"] = np.ascontiguousarray(xs)
        m["xp"] = np.ascontiguousarray(xp.reshape(512, 1024))
        cond = np.stack([c[b], c_ctx])
        m["condT"] = np.ascontiguousarray(cond.T.reshape(8, 128, 2).transpose(1, 0, 2))
        dirs = (0, 1) if p == 0 else (1, 0)
        st = np.empty((128, 8), np.float32)
        for dl, dg in enumerate(dirs):
            st[:, dl::2] = _fm(state_lru[b, 0, dg], 4)
        m["st0"] = st
        m["ckT"] = np.ascontiguousarray(cache_k[b, 0].transpose(0, 2, 1).reshape(1024, 512))
        m["cv"] = np.ascontiguousarray(cache_v[b, 0].transpose(1, 0, 2).reshape(512, 1024))
        in_maps.append(m)

    res = run_bass_kernel_spmd(nc, in_maps, core_ids=list(range(8)))
    R = res.results

    y_prompt = np.empty((16, 256, 1024), np.float32)
    y_sample = np.empty((4, 4096, 1024), np.float32)
    new_state = np.empty((16, 1, 2, 512), np.float32)
    new_k = np.empty((16, 1, 16, 256, 64), np.float32)
    new_v = np.empty((16, 1, 16, 256, 64), np.float32)
    for core in range(8):
        b, p = core // 2, core % 2
        r = R[core]
        ys = np.asarray(r["ys"]); yp = np.asarray(r["yp"]).reshape(2, 256, 1024)
        nkT = np.asarray(r["nkT"]).reshape(16, 64, 2, 256)
        nv = np.asarray(r["nv"]).reshape(2, 256, 16, 64)
        nst = np.asarray(r["nst"])
        nst = nst.transpose(2, 3, 1, 0).reshape(2, 2, 512)
        kk = nkT.transpose(2, 0, 3, 1)
        vv = nv.transpose(0, 2, 1, 3)
        if p == 0:
            y_sample[b, 0:2048] = ys
        else:
            y_sample[b, 2048:4096] = ys[::-1]
            yp = yp[:, ::-1]
            kk = kk[:, :, ::-1]
            vv = vv[:, :, ::-1]
            nst = nst[:, ::-1]
        y_prompt[2 * core:2 * core + 2] = yp
        new_k[2 * core:2 * core + 2, 0] = kk
        new_v[2 * core:2 * core + 2, 0] = vv
        new_state[2 * core:2 * core + 2, 0] = nst
    return (y_prompt, y_sample, new_state, new_k, new_v)
```

```python
import contextlib
import numpy as np
import ml_dtypes
import concourse.bass as bass
import concourse.mybir as mybir
from concourse.bass_utils import run_bass_kernel_spmd

F32 = mybir.dt.float32
BF16 = mybir.dt.bfloat16
AF = mybir.ActivationFunctionType
ALU = mybir.AluOpType
ENGS = ("pe", "act", "dve", "pool", "sp")

D_MODEL = 1024
ALPHA = 4.0 ** 0.25
LN_EPS = 1e-5
NS = 4096
SLAB = 2304
OWN = 2048
NPR = 512
NTOK0 = NS + NPR
NSL = SLAB + NPR
NOUT = OWN + NPR
GELU_C = 0.7978845608028654


import os
MAXSTAGE = int(os.environ.get("MAXSTAGE", "99"))


class _Stop(Exception):
    pass


class Tok:
    __slots__ = ("name", "w", "r")

    def __init__(self, name=""):
        self.name = name
        self.w = None
        self.r = []


class Prog:
    def __init__(self, nc, es, ndma=48):
        self.nc = nc
        self.ops = {e: [] for e in ENGS}
        self.emitted = {e: 0 for e in ENGS}
        self.esem = {e: es.enter_context(nc.semaphore("s_" + e)) for e in ENGS}
        self.dpool = [es.enter_context(nc.semaphore("d_%d" % i)) for i in range(ndma)]
        self.dsem = {}
        self.dma_cnt = {}
        self.free = [(s_, 0) for s_ in self.dpool]
        self.stage = 0
        self.seen = {e: {} for e in ENGS}

    def _deps(self, eng, reads, writes):
        deps = []
        for t in reads:
            if t.w is not None:
                deps.append(t.w)
        for t in writes:
            if t.w is not None and not (t.w[0] == 'e' and t.w[1] == eng and eng != 'pool'):
                deps.append(t.w)
            for r in t.r:
                if not (r[0] == 'e' and r[1] == eng and eng != 'pool'):
                    deps.append(r)
        return deps

    def op(self, eng, fn, reads=(), writes=(), sig=True):
        deps = self._deps(eng, reads, writes)
        idx = len(self.ops[eng])
        self.ops[eng].append(dict(fn=fn, deps=deps, sig=sig, dma=None))
        me = ('e', eng, idx)
        for t in reads:
            t.r.append(me)
        for t in writes:
            t.w = me
            t.r = []
        return me

    def dma(self, eng, fn, key, reads=(), writes=()):
        deps = self._deps(None, reads, writes)
        key = (self.stage, key)
        if key not in self.dsem:
            sem_, cnt_ = self.free.pop(0)
            self.dsem[key] = sem_
            self.dma_cnt[key] = cnt_
        self.dma_cnt[key] += 16
        me = ('d', key, self.dma_cnt[key])
        self.ops[eng].append(dict(fn=fn, deps=deps, sig=False, dma=key))
        for t in reads:
            t.r.append(me)
        for t in writes:
            t.w = me
            t.r = []
        return me

    def emit(self):
        nc = self.nc
        need = {}
        for e in ENGS:
            ops = self.ops[e]
            for o in reversed(ops[self.emitted[e]:]):
                if o["dma"] is None:
                    o["sig"] = True
                    break
            c = 0
            arr = []
            for o in ops:
                if o["sig"]:
                    c += 1
                arr.append(c)
            nd = [None] * len(ops)
            nxt = None
            for i in range(len(ops) - 1, -1, -1):
                if ops[i]["sig"]:
                    nxt = arr[i]
                nd[i] = nxt
            need[e] = nd
        with nc.Block() as block:
            engobj = {"pe": block.tensor, "act": block.scalar, "dve": block.vector,
                      "pool": block.gpsimd, "sp": block.sync}

            def make(e):
                def body(eng):
                    seen = self.seen[e]
                    for o in self.ops[e][self.emitted[e]:]:
                        w = {}
                        for d in o["deps"]:
                            if d[0] == 'e':
                                k = ('e', d[1])
                                v = need[d[1]][d[2]]
                                assert v is not None
                            else:
                                if d[1] not in self.dsem:
                                    continue
                                k = ('d', d[1])
                                v = d[2]
                            if v > w.get(k, 0):
                                w[k] = v
                        for k, v in w.items():
                            if seen.get(k, 0) >= v:
                                continue
                            seen[k] = v
                            s = self.esem[k[1]] if k[0] == 'e' else self.dsem[k[1]]
                            eng.wait_ge(s, v)
                        ins = o["fn"](eng)
                        if o["dma"] is not None:
                            ins.then_inc(self.dsem[o["dma"]], 16)
                        elif o["sig"]:
                            ins.then_inc(self.esem[e], 1)
                    if e == "sp":
                        for k, s in self.dsem.items():
                            if seen.get(('d', k), 0) < self.dma_cnt[k]:
                                seen[('d', k)] = self.dma_cnt[k]
                                eng.wait_ge(s, self.dma_cnt[k])
                    self.emitted[e] = len(self.ops[e])
                return body

            for e in ENGS:
                engobj[e](make(e))
        for k in list(self.dsem.keys()):
            self.free.append((self.dsem[k], self.dma_cnt[k]))
            for e in ENGS:
                self.seen[e].pop(('d', k), None)
        self.dsem = {}
        self.dma_cnt = {}
        self.stage += 1
        if self.stage >= MAXSTAGE:
            raise _Stop()


class Ring:
    def __init__(self, items):
        self.items = items
        self.i = 0

    def next(self):
        it = self.items[self.i % len(self.items)]
        self.i += 1
        return it


def build():
    nc = bass.Bass("TRN2", target_bir_lowering=False)
    D = {}

    def din(name, shape, dt=F32):
        D[name] = nc.dram_tensor(name, list(shape), dt, kind="ExternalInput").ap()

    def dout(name, shape, dt=F32):
        D[name] = nc.dram_tensor(name, list(shape), dt, kind="ExternalOutput").ap()

    def dscr(name, shape, dt=F32):
        D[name] = nc.dram_tensor(name, list(shape), dt).ap()

    din("xs", [NS, 1024]); din("xp", [NPR, 1024])
    din("condT", [128, 8, 2]); din("st0", [128, 8])
    din("ckT", [1024, 512]); din("cv", [512, 1024])
    din("ada_w", [2, 1024, 6144]); din("adabF", [128, 2, 48]); din("adabB", [2, 2, 128, 1024])
    din("lnB", [2, 4, 128, 1024])
    din("w1", [2, 1024, 4096]); din("b1F", [128, 2, 32]); din("w2", [2, 4096, 1024])
    din("e_w_in", [1024, 1536]); din("e_b_inF", [128, 12]); din("taps", [128, 4, 5]); din("convbF", [128, 4])
    din("wgate", [128, 16, 128]); din("bgateF", [128, 2, 2, 4]); din("lamF", [128, 2, 4])
    din("e_w_out", [1024, 1024]); din("ebrow", [1, 1024]); din("obrow", [1, 1024]); din("b2row", [2, 1024])
    din("o_w_qkv", [1024, 3072]); din("bqkF", [128, 16]); din("bvB", [128, 1024])
    din("nabias", [16, 128, 7, 128]); din("o_w_out", [1024, 1024])
    din("dftS", [NS, 2, SLAB], BF16); din("dftP", [256, 2, 256], BF16); din("dftC", [128, 256], BF16)
    din("ident", [128, 128], BF16)
    dout("ys", [OWN, 1024]); dout("yp", [NPR, 1024]); dout("nst", [128, 4, 2, 2])
    dout("nkT", [1024, NPR]); dout("nv", [NPR, 1024])
    dscr("w1b", [2, 1024, 4096], BF16); dscr("w2b", [2, 4096, 1024], BF16)
    dscr("xrT", [512, NTOK0]); dscr("gT", [512, NTOK0]); dscr("yT", [1024, NSL], BF16)
    dscr("x1", [NSL, 1024]); dscr("gsc", [2, 4, 128, 1024])
    dscr("qA", [1024, NOUT], BF16); dscr("qB", [1024, NOUT], BF16); dscr("kT", [1024, NSL], BF16)
    dscr("ve", [NSL, 2048], BF16); dscr("oT", [1024, NOUT], BF16)

    try:
      with contextlib.ExitStack() as ges:
        P = Prog(nc, ges)
        gsb = lambda n, s, d: ges.enter_context(nc.sbuf_tensor("g_" + n, list(s), d))
        ident = gsb("ident", [128, 128], BF16); T_ident = Tok()
        modF = [gsb("modF%d" % l, [128, 48, 2], F32) for l in range(2)]
        T_modF = [Tok(), Tok()]
        cst = gsb("cst", [128, 4], F32); T_cst = Tok()
        pb = [ges.enter_context(nc.psum_tensor("pb%d" % i, [128, 512], F32)) for i in range(8)]
        T_pb = [Tok() for _ in range(8)]

        P.dma("sp", lambda e: e.dma_start(out=ident[:], in_=D["ident"]), "ident", writes=[T_ident])
        P.op("dve", lambda e: e.memset(cst[:, 0:1], 1.0), writes=[T_cst])
        P.op("dve", lambda e: e.memset(cst[:, 1:2], LN_EPS), writes=[T_cst])

        def ln_stats(S, xt, T_x):
            st, T_st = S["stat"].next()
            mv, T_mv = S["mv"].next()
            P.op("dve", lambda e: e.bn_stats(out=st[:, 0:6], in_=xt[:, 0:512]), reads=[T_x], writes=[T_st])
            P.op("dve", lambda e: e.bn_stats(out=st[:, 6:12], in_=xt[:, 512:1024]), reads=[T_x], writes=[T_st])
            P.op("dve", lambda e: e.bn_aggr(out=mv[:, 0:2], in_=st[:]), reads=[T_st], writes=[T_mv])
            P.op("act", lambda e: e.activation(out=mv[:, 1:2], in_=mv[:, 1:2], func=AF.Sqrt, bias=cst[:, 1:2], scale=1.0),
                 reads=[T_mv, T_cst], writes=[T_mv])
            P.op("dve", lambda e: e.reciprocal(out=mv[:, 1:2], in_=mv[:, 1:2]), reads=[T_mv], writes=[T_mv])
            return mv, T_mv

        def ln_mod_T(S, xt, T_x, l, vshift, r, hT, T_hT, col0):
            mv, T_mv = ln_stats(S, xt, T_x)
            xh, T_xh = S["xh"].next()
            P.op("dve", lambda e: e.tensor_scalar(out=xh[:], in0=xt[:], scalar1=mv[:, 0:1], scalar2=mv[:, 1:2],
                                                  op0=ALU.subtract, op1=ALU.mult), reads=[T_x, T_mv], writes=[T_xh])
            (pt, T_pt) = S["ptr"].next()
            ptb = pt[:].bitcast(BF16)
            for c in range(8):
                P.op("pe", lambda e, c=c: e.transpose(out=ptb[:, c * 128:(c + 1) * 128], in_=xh[:, c * 128:(c + 1) * 128],
                                                      identity=ident[:]),
                     reads=[T_xh, T_ident], writes=[T_pt], sig=(c == 7))
            for c in range(8):
                P.op("act", lambda e, c=c: e.activation(out=hT[:, c, col0:col0 + 128], in_=ptb[:, c * 128:(c + 1) * 128],
                                                        func=AF.Identity,
                                                        bias=modF[l][:, vshift * 8 + c, r:r + 1],
                                                        scale=modF[l][:, (vshift + 1) * 8 + c, r:r + 1]),
                     reads=[T_pt, T_modF[l]], writes=[T_hT])

        def ln_affine(S, xt, T_x, gB, bB, T_gb, out, T_out):
            mv, T_mv = ln_stats(S, xt, T_x)
            P.op("dve", lambda e: e.tensor_scalar(out=xt[:], in0=xt[:], scalar1=mv[:, 0:1], scalar2=mv[:, 1:2],
                                                  op0=ALU.subtract, op1=ALU.mult), reads=[T_x, T_mv], writes=[T_x])
            P.op("pool", lambda e: e.tensor_tensor(out=xt[:], in0=xt[:], in1=gB, op=ALU.mult), reads=[T_x, T_gb], writes=[T_x])
            P.op("pool", lambda e: e.tensor_tensor(out=out[:], in0=xt[:], in1=bB, op=ALU.add), reads=[T_x, T_gb], writes=[T_out])

        def ln_stats_multi(S, xs):
            sts = [S["stat"].next() for _ in xs]
            mvs = [S["mv"].next() for _ in xs]
            for (xt, T_x), (st, T_st) in zip(xs, sts):
                P.op("dve", lambda e, st=st, xt=xt: e.bn_stats(out=st[:, 0:6], in_=xt[:, 0:512]), reads=[T_x], writes=[T_st])
                P.op("dve", lambda e, st=st, xt=xt: e.bn_stats(out=st[:, 6:12], in_=xt[:, 512:1024]), reads=[T_x], writes=[T_st])
            for (st, T_st), (mv, T_mv) in zip(sts, mvs):
                P.op("dve", lambda e, st=st, mv=mv: e.bn_aggr(out=mv[:, 0:2], in_=st[:]), reads=[T_st], writes=[T_mv])
            for (mv, T_mv) in mvs:
                P.op("act", lambda e, mv=mv: e.activation(out=mv[:, 1:2], in_=mv[:, 1:2], func=AF.Sqrt, bias=cst[:, 1:2], scale=1.0),
                     reads=[T_mv, T_cst], writes=[T_mv])
            for (mv, T_mv) in mvs:
                P.op("dve", lambda e, mv=mv: e.reciprocal(out=mv[:, 1:2], in_=mv[:, 1:2]), reads=[T_mv], writes=[T_mv])
                P.op("dve", lambda e, mv=mv: e.scalar_tensor_tensor(out=mv[:, 2:3], in0=mv[:, 0:1], scalar=-1.0, in1=mv[:, 1:2],
                                                                     op0=ALU.mult, op1=ALU.mult), reads=[T_mv], writes=[T_mv])
            return mvs

        def ln_affine_multi(S, xs, gB, bB, T_gb, outs):
            mvs = ln_stats_multi(S, xs)
            for (xt, T_x), (mv, T_mv) in zip(xs, mvs):
                P.op("act", lambda e, xt=xt, mv=mv: e.activation(out=xt[:], in_=xt[:], func=AF.Identity,
                                                                 bias=mv[:, 2:3], scale=mv[:, 1:2]), reads=[T_x, T_mv], writes=[T_x])
            for (xt, T_x) in xs:
                P.op("dve", lambda e, xt=xt: e.tensor_tensor(out=xt[:], in0=xt[:], in1=gB, op=ALU.mult), reads=[T_x, T_gb], writes=[T_x])
            for (xt, T_x), (out, T_out) in zip(xs, outs):
                P.op("pool", lambda e, xt=xt, out=out: e.tensor_tensor(out=out[:], in0=xt[:], in1=bB, op=ALU.add),
                     reads=[T_x, T_gb], writes=[T_out])

        def ln_group_1(S, srcs, xin, keyp):
            xs = []
            for src in srcs:
                xt, T_x = xin.next()
                P.dma("sp", lambda e, xt=xt, src=src: e.dma_start(out=xt[:], in_=src),
                      keyp + "%d" % ((xin.i - 1) % len(xin.items)), writes=[T_x])
                xs.append((xt, T_x))
            mvs = ln_stats_multi(S, xs)
            xhl = []
            for (xt, T_x), (mv, T_mv) in zip(xs, mvs):
                xh, T_xh = S["xh"].next()
                P.op("act", lambda e, xh=xh, xt=xt, mv=mv: e.activation(out=xh[:], in_=xt[:], func=AF.Identity,
                                                                        bias=mv[:, 2:3], scale=mv[:, 1:2]),
                     reads=[T_x, T_mv], writes=[T_xh])
                xhl.append((xh, T_xh))
            return xhl

        def ln_group_2(S, xhl, l, vshift, rs, hT, T_hT, all_act=False):
            tb = S["tbanks"]
            nt = len(xhl)
            for tt, (xh, T_xh) in enumerate(xhl):
                for c in range(8):
                    bk, T_bk = tb[c // 2]
                    col = ((c % 2) * 4 + tt) * 128
                    P.op("pe", lambda e, c=c, xh=xh, bk=bk, col=col: e.transpose(
                        out=bk[:].bitcast(BF16)[:, col:col + 128], in_=xh[:, c * 128:(c + 1) * 128], identity=ident[:]),
                        reads=[T_xh, T_ident], writes=[T_bk], sig=(tt == nt - 1 or c == 7))
            runs = []
            t0 = 0
            for tt in range(1, nt + 1):
                if tt == nt or rs[tt] != rs[t0]:
                    runs.append((t0, tt, rs[t0]))
                    t0 = tt
            for c in range(8):
                bk, T_bk = tb[c // 2]
                for (a, b, r) in runs:
                    col = ((c % 2) * 4 + a) * 128
                    P.op("act", lambda e, c=c, bk=bk, col=col, a=a, b=b, r=r: e.activation(
                        out=hT[:, c, a * 128:b * 128], in_=bk[:].bitcast(BF16)[:, col:col + (b - a) * 128], func=AF.Identity,
                        bias=modF[l][:, vshift * 8 + c, r:r + 1], scale=modF[l][:, (vshift + 1) * 8 + c, r:r + 1]),
                        reads=[T_bk, T_modF[l]], writes=[T_hT])

        def mk_ln_scratch(es, tag, nx=2):
            sb = lambda n, s, d: es.enter_context(nc.sbuf_tensor(tag + n, list(s), d))
            S = {}
            S["stat"] = Ring([(sb("st%d" % i, [128, 12], F32), Tok()) for i in range(4)])
            S["mv"] = Ring([(sb("mv%d" % i, [128, 4], F32), Tok()) for i in range(8)])
            S["xh"] = Ring([(sb("xh%d" % i, [128, 1024], BF16), Tok()) for i in range(nx)])
            return S

        with contextlib.ExitStack() as es:
            sb = lambda n, s, d: es.enter_context(nc.sbuf_tensor("s0" + n, list(s), d))
            condT = sb("condT", [128, 8, 2], F32); T_condT = Tok()
            sT = sb("sT", [128, 8, 2], BF16); T_sT = Tok()
            sB = [sb("sB%d" % r, [128, 8, 128], BF16) for r in range(2)]; T_sB = [Tok(), Tok()]
            adabF = sb("adabF", [128, 2, 48], F32); T_adabF = Tok()
            adabB = Ring([(sb("adabB%d" % i, [128, 1024], F32), Tok()) for i in range(2)])
            gstage = Ring([(sb("gst%d" % i, [128, 1024], F32), Tok()) for i in range(2)])
            slots = Ring([(sb("aw%d" % i, [128, 8, 1024], BF16), Tok()) for i in range(3)])
            P.dma("sp", lambda e: e.dma_start(out=condT[:], in_=D["condT"]), "condT", writes=[T_condT])
            P.dma("sp", lambda e: e.dma_start(out=adabF[:], in_=D["adabF"]), "adabF", writes=[T_adabF])
            P.op("act", lambda e: e.activation(out=sT[:], in_=condT[:], func=AF.Silu), reads=[T_condT], writes=[T_sT])
            for r in range(2):
                for c in range(8):
                    P.op("dve", lambda e, r=r, c=c: e.tensor_copy(out=sB[r][:, c, :],
                                                                  in_=sT[:, c, r:r + 1].to_broadcast([128, 128])),
                         reads=[T_sT], writes=[T_sB[r]])
            bank = Ring(list(zip(pb, T_pb)))
            for l in range(2):
                for v in range(6):
                    slot, T_slot = slots.next()
                    key = "aw%d" % ((slots.i - 1) % 3)
                    P.dma("pool", lambda e, slot=slot, l=l, v=v: e.dma_start(
                        out=slot[:], in_=D["ada_w"][l, :, v * 1024:(v + 1) * 1024].rearrange("(c p) n -> p c n", p=128)),
                        key, writes=[T_slot])
                    if v in (2, 5):
                        g = 0 if v == 2 else 1
                        bt, T_bt = adabB.next()
                        P.dma("sp", lambda e, bt=bt, l=l, g=g: e.dma_start(out=bt[:], in_=D["adabB"][l, g]),
                              "adabB%d" % ((adabB.i - 1) % 2), writes=[T_bt])
                        for r in range(2):
                            for half in range(2):
                                ps, T_ps = bank.next()
                                for c in range(8):
                                    P.op("pe", lambda e, ps=ps, slot=slot, r=r, c=c, half=half: e.matmul(
                                        ps[:], lhsT=sB[r][:, c, :], rhs=slot[:, c, half * 512:(half + 1) * 512],
                                        start=(c == 0), stop=(c == 7)),
                                        reads=[T_sB[r], T_slot], writes=[T_ps], sig=(c == 7))
                                if half == 0:
                                    gst, T_gst = gstage.next()
                                P.op("dve", lambda e, ps=ps, bt=bt, gst=gst, half=half: e.tensor_tensor(
                                    out=gst[:, half * 512:(half + 1) * 512], in0=ps[:],
                                    in1=bt[:, half * 512:(half + 1) * 512], op=ALU.add),
                                    reads=[T_ps, T_bt], writes=[T_gst])
                                if half == 1:
                                    P.dma("sp", lambda e, gst=gst, l=l, r=r, g=g: e.dma_start(out=D["gsc"][l, r * 2 + g], in_=gst[:]),
                                          "gst%d" % ((gstage.i - 1) % 2), reads=[T_gst])
                    else:
                        ps, T_ps = bank.next()
                        for co in range(8):
                            for c in range(8):
                                P.op("pe", lambda e, ps=ps, slot=slot, co=co, c=c: e.matmul(
                                    ps[:, co * 2:co * 2 + 2], lhsT=slot[:, c, co * 128:(co + 1) * 128], rhs=sT[:, c, :],
                                    start=(c == 0), stop=(c == 7)),
                                    reads=[T_sT, T_slot], writes=[T_ps], sig=(co == 7 and c == 7))
                        P.op("dve", lambda e, ps=ps, l=l, v=v: e.tensor_tensor(
                            out=modF[l][:, v * 8:(v + 1) * 8, :],
                            in0=ps[:, 0:16].rearrange("p (c r) -> p c r", r=2),
                            in1=adabF[:, l, v * 8:(v + 1) * 8].unsqueeze(2).to_broadcast([128, 8, 2]), op=ALU.add),
                            reads=[T_ps, T_adabF], writes=[T_modF[l]])
                        if v in (1, 4):
                            P.op("dve", lambda e, l=l, v=v: e.tensor_scalar(
                                out=modF[l][:, v * 8:(v + 1) * 8, :], in0=modF[l][:, v * 8:(v + 1) * 8, :],
                                scalar1=1.0, scalar2=None, op0=ALU.add), reads=[T_modF[l]], writes=[T_modF[l]])
            P.emit()

        with contextlib.ExitStack() as esAB:
            AB = esAB.enter_context(nc.sbuf_tensor("AB", [128, 36, 1024], BF16)); T_AB = Tok()
            with contextlib.ExitStack() as es:
                sb = lambda n, s, d: es.enter_context(nc.sbuf_tensor("a0" + n, list(s), d))
                S = mk_ln_scratch(es, "a0", nx=5)
                S["tbanks"] = [(pb[i], T_pb[i]) for i in (4, 5, 6, 7)]
                win = sb("win", [128, 8, 1536], BF16); T_win = Tok()
                binF = sb("binF", [128, 12], F32); T_binF = Tok()
                dftC = sb("dftC", [128, 256], BF16); T_dftC = Tok()
                xin = Ring([(sb("xin%d" % i, [128, 1024], F32), Tok()) for i in range(6)])
                hTr = Ring([(sb("hT%d" % i, [128, 8, 512], BF16), Tok()) for i in range(2)])
                ev = Ring([(sb("ev%d" % i, [128, 512], F32), Tok()) for i in range(3)])
                gl = Ring([(sb("gl%d" % i, [128, 512], F32), Tok()) for i in range(4)])
                zT = Ring([(sb("zT%d" % i, [128, 512], BF16), Tok()) for i in range(5)])
                T_winp = [Tok(), Tok(), Tok()]
                for part in (2, 0, 1):
                    P.dma("pool", lambda e, part=part: e.dma_start(
                        out=win[:, :, part * 512:(part + 1) * 512],
                        in_=D["e_w_in"][:, part * 512:(part + 1) * 512].rearrange("(c p) n -> p c n", p=128)),
                        "win%d" % part, writes=[T_winp[part]])
                P.dma("sp", lambda e: e.dma_start(out=binF[:], in_=D["e_b_inF"]), "binF", writes=[T_binF])
                P.dma("sp", lambda e: e.dma_start(out=dftC[:], in_=D["dftC"]), "dftC", writes=[T_dftC])
                bank = Ring([(pb[i], T_pb[i]) for i in range(4)])
                S["mt"] = Ring([(sb("mt%d" % i, [128, 8, 128], F32), Tok()) for i in range(2)])

                def LNG0a(grp):
                    srcs = [(D["xs"][(grp * 4 + tt) * 128:(grp * 4 + tt + 1) * 128, :] if grp < 8
                             else D["xp"][tt * 128:(tt + 1) * 128, :]) for tt in range(4)]
                    return ln_group_1(S, srcs, xin, "xin")

                def LNG0b(grp, xhl):
                    r = 0 if grp < 8 else 1
                    hT, T_hT = hTr.next()
                    ln_group_2(S, xhl, 0, 0, [r] * 4, hT, T_hT, all_act=True)
                    return hT, T_hT

                nxt = LNG0b(0, LNG0a(0))
                for grp in range(9):
                    hT, T_hT = nxt
                    if grp + 1 < 9:
                        xhl_n = LNG0a(grp + 1)
                    dft_q = []
                    for co in (8, 9, 10, 11, 0, 1, 2, 3, 4, 5, 6, 7):
                        if 4 <= co < 8 and grp in (5, 6, 7):
                            continue
                        ps, T_ps = bank.next()
                        for c in range(8):
                            P.op("pe", lambda e, ps=ps, co=co, c=c, hT=hT: e.matmul(
                                ps[:], lhsT=win[:, c, co * 128:(co + 1) * 128], rhs=hT[:, c, :],
                                start=(c == 0), stop=(c == 7)), reads=[T_winp[co // 4], T_hT], writes=[T_ps], sig=(c == 7))
                        cols = slice(grp * 512, (grp + 1) * 512)
                        if co < 4:
                            et, T_et = ev.next()
                            P.op("act", lambda e, et=et, ps=ps, co=co: e.activation(
                                out=et[:], in_=ps[:], func=AF.Identity, bias=binF[:, co:co + 1], scale=1.0),
                                reads=[T_ps, T_binF], writes=[T_et])
                            P.dma("act", lambda e, et=et, co=co, cols=cols: e.dma_start(
                                out=D["xrT"][co * 128:(co + 1) * 128, cols], in_=et[:]),
                                "ev%d" % ((ev.i - 1) % 3), reads=[T_et])
                        elif co < 8:
                            j = co - 4
                            x0, T_x0 = gl.next(); u0, T_u0 = gl.next()
                            P.op("act", lambda e, x0=x0, ps=ps, co=co: e.activation(
                                out=x0[:], in_=ps[:], func=AF.Identity, bias=binF[:, co:co + 1], scale=1.0),
                                reads=[T_ps, T_binF], writes=[T_x0])
                            P.op("dve", lambda e, x0=x0, u0=u0: e.tensor_tensor(out=u0[:], in0=x0[:], in1=x0[:], op=ALU.mult),
                                 reads=[T_x0], writes=[T_u0])
                            P.op("dve", lambda e, u0=u0: e.tensor_scalar(out=u0[:], in0=u0[:], scalar1=0.044715, scalar2=1.0,
                                                                         op0=ALU.mult, op1=ALU.add), reads=[T_u0], writes=[T_u0])
                            P.op("dve", lambda e, x0=x0, u0=u0: e.tensor_tensor(out=u0[:], in0=u0[:], in1=x0[:], op=ALU.mult),
                                 reads=[T_x0, T_u0], writes=[T_u0])
                            P.op("act", lambda e, u0=u0: e.activation(out=u0[:], in_=u0[:], func=AF.Sigmoid, scale=2.0 * GELU_C),
                                 reads=[T_u0], writes=[T_u0])
                            P.op("pool", lambda e, x0=x0, u0=u0: e.tensor_tensor(out=u0[:], in0=u0[:], in1=x0[:], op=ALU.mult),
                                 reads=[T_x0, T_u0], writes=[T_u0])
                            P.dma("pool", lambda e, u0=u0, j=j, cols=cols: e.dma_start(
                                out=D["gT"][j * 128:(j + 1) * 128, cols], in_=u0[:]),
                                "gl%d" % ((gl.i - 1) % 4), reads=[T_u0])
                        else:
                            g = co - 8
                            zt, T_zt = zT.next()
                            P.op("act", lambda e, zt=zt, ps=ps, co=co: e.activation(
                                out=zt[:], in_=ps[:], func=AF.Identity, bias=binF[:, co:co + 1], scale=1.0),
                                reads=[T_ps, T_binF], writes=[T_zt])
                            dft_q.append((g, zt, T_zt))
                            continue
                    for (g, zt, T_zt) in dft_q:
                        if True:
                            ps2, T_ps2 = bank.next()
                            for tt in range(4):
                                P.op("pe", lambda e, ps2=ps2, zt=zt, tt=tt: e.matmul(
                                    ps2[:, tt * 128:(tt + 1) * 128],
                                    lhsT=zt[:, tt * 128:(tt + 1) * 128], rhs=dftC[:, 0:128], start=True, stop=True),
                                    reads=[T_zt, T_dftC], writes=[T_ps2], sig=False)
                            ps3, T_ps3 = bank.next()
                            for tt in range(4):
                                P.op("pe", lambda e, ps3=ps3, zt=zt, tt=tt: e.matmul(
                                    ps3[:, tt * 128:(tt + 1) * 128],
                                    lhsT=zt[:, tt * 128:(tt + 1) * 128], rhs=dftC[:, 128:256], start=True, stop=True),
                                    reads=[T_zt, T_dftC], writes=[T_ps3], sig=(tt == 3))
                            P.op("dve", lambda e, ps2=ps2, grp=grp, g=g: e.tensor_copy(
                                out=AB[:, grp * 4:(grp + 1) * 4, g * 256:g * 256 + 128],
                                in_=ps2[:].rearrange("p (t c) -> p t c", c=128)), reads=[T_ps2], writes=[T_AB])
                            P.op("dve", lambda e, ps3=ps3, grp=grp, g=g: e.tensor_copy(
                                out=AB[:, grp * 4:(grp + 1) * 4, g * 256 + 128:g * 256 + 256],
                                in_=ps3[:].rearrange("p (t c) -> p t c", c=128)), reads=[T_ps3], writes=[T_AB])
                    if grp + 1 < 9:
                        nxt = LNG0b(grp + 1, xhl_n)
                P.emit()

            with contextlib.ExitStack() as es:
                sb = lambda n, s, d: es.enter_context(nc.sbuf_tensor("b0" + n, list(s), d))
                tabs = Ring([(sb("tab%d" % i, [128, 2, 512], BF16), Tok()) for i in range(4)])
                tabP = sb("tabP", [128, 2, 2, 256], BF16); T_tabP = Tok()
                yb = Ring([(sb("yb%d" % i, [128, 512], BF16), Tok()) for i in range(4)])
                P.dma("sp", lambda e: e.dma_start(out=tabP[:], in_=D["dftP"].rearrange("(c p) s k -> p c s k", p=128)),
                      "tabP", writes=[T_tabP])
                cvs = Ring([(sb("cvs%d" % i, [128, 8, 1024], BF16), Tok()) for i in range(2)])
                for kind in ("w1", "w2"):
                    for blk in range(4):
                        slot, T_slot = cvs.next()
                        key = "cv%d" % ((cvs.i - 1) % 2)
                        if kind == "w1":
                            src = D["w1"][0, :, blk * 1024:(blk + 1) * 1024].rearrange("(c p) n -> p c n", p=128)
                            dst = D["w1b"][0, :, blk * 1024:(blk + 1) * 1024].rearrange("(c p) n -> p c n", p=128)
                        else:
                            src = D["w2"][0, blk * 1024:(blk + 1) * 1024, :].rearrange("(c p) n -> p c n", p=128)
                            dst = D["w2b"][0, blk * 1024:(blk + 1) * 1024, :].rearrange("(c p) n -> p c n", p=128)
                        P.dma("pool", lambda e, slot=slot, src=src: e.dma_start(out=slot[:], in_=src), key, writes=[T_slot])
                        P.dma("pool", lambda e, slot=slot, dst=dst: e.dma_start(out=dst, in_=slot[:]), key + "s", reads=[T_slot])
                kblocks = [(0, 512), (512, 512), (1024, 512), (1536, 512), (2048, 256)]
                for kbi, (k0, kw) in enumerate(kblocks):
                    accs = [(pb[(kbi % 2) * 4 + g], T_pb[(kbi % 2) * 4 + g]) for g in range(4)]
                    for n in range(32):
                        tb, T_tb = tabs.next()
                        P.dma("sp", lambda e, tb=tb, n=n, k0=k0, kw=kw: e.dma_start(
                            out=tb[:, :, 0:kw], in_=D["dftS"][n * 128:(n + 1) * 128, :, k0:k0 + kw]),
                            "tab%d" % ((tabs.i - 1) % 4), writes=[T_tb])
                        for g in range(4):
                            ps, T_ps = accs[g]
                            for s in range(2):
                                P.op("pe", lambda e, ps=ps, tb=tb, n=n, g=g, s=s, kw=kw: e.matmul(
                                    ps[:, 0:kw], lhsT=AB[:, n, g * 256 + s * 128:g * 256 + (s + 1) * 128], rhs=tb[:, s, 0:kw],
                                    start=(n == 0 and s == 0), stop=(n == 31 and s == 1)),
                                    reads=[T_AB, T_tb], writes=[T_ps], sig=(s == 1 and (g == 3 or n == 31)))
                    for g in range(4):
                        ps, T_ps = accs[g]
                        yt, T_yt = yb.next()
                        P.op("act", lambda e, yt=yt, ps=ps, kw=kw: e.activation(out=yt[:, 0:kw], in_=ps[:, 0:kw], func=AF.Copy),
                             reads=[T_ps], writes=[T_yt])
                        P.dma("act", lambda e, yt=yt, g=g, k0=k0, kw=kw: e.dma_start(
                            out=D["yT"][512 + g * 128:512 + (g + 1) * 128, k0:k0 + kw], in_=yt[:, 0:kw]),
                            "yb%d" % ((yb.i - 1) % 4), reads=[T_yt])
                for s_ in range(2):
                    ps, T_ps = pb[s_], T_pb[s_]
                    for g in range(4):
                        for n in range(2):
                            for s in range(2):
                                P.op("pe", lambda e, ps=ps, g=g, n=n, s=s, s_=s_: e.matmul(
                                    ps[:, 0:256],
                                    lhsT=AB[:, 32 + s_ * 2 + n, g * 256 + s * 128:g * 256 + (s + 1) * 128],
                                    rhs=tabP[:, n, s, :], start=(n == 0 and s == 0), stop=(n == 1 and s == 1)),
                                    reads=[T_AB, T_tabP], writes=[T_ps], sig=(n == 1 and s == 1))
                        yt, T_yt = yb.next()
                        P.op("act", lambda e, yt=yt, ps=ps: e.activation(out=yt[:, 0:256], in_=ps[:, 0:256], func=AF.Copy),
                             reads=[T_ps], writes=[T_yt])
                        P.dma("act", lambda e, yt=yt, g=g, s_=s_: e.dma_start(
                            out=D["yT"][512 + g * 128:512 + (g + 1) * 128, SLAB + s_ * 256:SLAB + (s_ + 1) * 256],
                            in_=yt[:, 0:256]), "yb%d" % ((yb.i - 1) % 4), reads=[T_yt])
                P.emit()

        with contextlib.ExitStack() as es:
            sb = lambda n, s, d: es.enter_context(nc.sbuf_tensor("c0" + n, list(s), d))
            NT = NTOK0
            NA_ = 6 * 512
            Xs = [sb("X%d" % i, [128, NT], F32) for i in range(2)]; T_Xs = [Tok(), Tok()]
            XCs = [sb("XC%d" % i, [128, NT], F32) for i in range(2)]; T_XCs = [Tok(), Tok()]
            XCBs = [sb("XCB%d" % i, [128, NT], BF16) for i in range(2)]
            Rb = [sb("RA", [128, NA_], F32), sb("RB", [128, NT], F32)]
            Ib = [sb("IA", [128, NA_], F32), sb("IB", [128, NT], F32)]
            Mb = [sb("MA", [128, NA_], F32), sb("MB", [128, NT], F32)]
            T_H = [Tok(), Tok()]
            G = sb("G", [128, NSL], F32); T_G = Tok()
            YA = sb("YA", [128, NSL], BF16); T_YA = Tok()
            wg = sb("wg", [128, 16, 128], BF16); T_wg = Tok()
            bg = sb("bg", [128, 2, 2, 4], F32); T_bg = Tok()
            lam = sb("lam", [128, 2, 4], F32); T_lam = Tok()
            tq = sb("tq", [128, 2, 4], F32); T_tq = Tok()
            sp_ = sb("sp_", [128, 2, 4], F32); T_sp = Tok()
            taps = sb("taps", [128, 4, 5], F32); T_taps = Tok()
            cvb = sb("cvb", [128, 4], F32); T_cvb = Tok()
            st0 = sb("st0", [128, 8], F32); T_st0 = Tok()
            nst = sb("nst", [128, 4, 2, 2], F32); T_nst = Tok()
            P.dma("pool", lambda e: e.dma_start(out=wg[:], in_=D["wgate"]), "wg", writes=[T_wg])
            P.dma("sp", lambda e: e.dma_start(out=bg[:], in_=D["bgateF"]), "bg", writes=[T_bg])
            P.dma("sp", lambda e: e.dma_start(out=lam[:], in_=D["lamF"]), "lam", writes=[T_lam])
            P.dma("sp", lambda e: e.dma_start(out=taps[:], in_=D["taps"]), "taps", writes=[T_taps])
            P.dma("sp", lambda e: e.dma_start(out=cvb[:], in_=D["convbF"]), "cvb", writes=[T_cvb])
            P.dma("sp", lambda e: e.dma_start(out=st0[:], in_=D["st0"]), "st0", writes=[T_st0])
            P.op("act", lambda e: e.activation(out=tq[:], in_=lam[:], func=AF.Exp, scale=-1.0), reads=[T_lam], writes=[T_tq])
            P.op("dve", lambda e: e.tensor_scalar(out=sp_[:], in0=tq[:], scalar1=-1.0 / 6.0, scalar2=0.2, op0=ALU.mult, op1=ALU.add),
                 reads=[T_tq], writes=[T_sp])
            for cc in (0.25, 1.0 / 3.0, 0.5, 1.0):
                P.op("dve", lambda e: e.tensor_tensor(out=sp_[:], in0=sp_[:], in1=tq[:], op=ALU.mult), reads=[T_sp, T_tq], writes=[T_sp])
                P.op("dve", lambda e, cc=cc: e.tensor_scalar(out=sp_[:], in0=sp_[:], scalar1=-1.0, scalar2=cc, op0=ALU.mult, op1=ALU.add),
                     reads=[T_sp], writes=[T_sp])
            P.op("dve", lambda e: e.tensor_tensor(out=sp_[:], in0=sp_[:], in1=tq[:], op=ALU.mult), reads=[T_sp, T_tq], writes=[T_sp])
            P.op("dve", lambda e: e.tensor_scalar(out=sp_[:], in0=sp_[:], scalar1=-8.0, scalar2=None, op0=ALU.mult),
                 reads=[T_sp], writes=[T_sp])
            segs = [(0, NS), (NS, NS + 256), (NS + 256, NS + 512)]
            bank = Ring(list(zip(pb, T_pb)))
            NB = NT // 512
            BLK = [[0, 1, 2, 3, 4, 8], list(range(NB))]
            TB_R = [[Tok() for _ in range(NB)] for d in range(2)]
            TB_I = [[Tok() for _ in range(NB)] for d in range(2)]
            TB_M = [[Tok() for _ in range(NB)] for d in range(2)]
            TB_XCB = [[Tok() for _ in range(NB)] for i in range(2)]
            sp2 = sb("sp2", [128, 2, 4], F32); T_sp2 = Tok()
            P.op("dve", lambda e: e.tensor_scalar(out=sp2[:], in0=sp_[:], scalar1=2.0, scalar2=None, op0=ALU.mult),
                 reads=[T_sp], writes=[T_sp2])

            def cs(d, b):
                if d == 1:
                    return slice(b * 512, (b + 1) * 512)
                ci = BLK[0].index(b)
                return slice(ci * 512, (ci + 1) * 512)

            def load_conv(j):
                x = j % 2
                X, XC, XCB = Xs[x], XCs[x], XCBs[x]
                T_X, T_XC = T_Xs[x], T_XCs[x]
                P.dma("sp", lambda e, j=j, X=X: e.dma_start(out=X[:], in_=D["xrT"][j * 128:(j + 1) * 128, :]), "X%d" % x, writes=[T_X])
                P.op("dve", lambda e, j=j, X=X, XC=XC: e.tensor_scalar(out=XC[:], in0=X[:], scalar1=taps[:, j, 2:3], scalar2=cvb[:, j:j + 1],
                                                                       op0=ALU.mult, op1=ALU.add), reads=[T_X, T_taps, T_cvb], writes=[T_XC])
                for (s0, s1) in segs:
                    for o in (-2, -1, 1, 2):
                        lo = s0 + max(0, -o); hi = s1 - max(0, o)
                        P.op("dve", lambda e, j=j, o=o, lo=lo, hi=hi, X=X, XC=XC: e.scalar_tensor_tensor(
                            out=XC[:, lo:hi], in0=X[:, lo + o:hi + o], scalar=taps[:, j, o + 2:o + 3], in1=XC[:, lo:hi],
                            op0=ALU.mult, op1=ALU.add), reads=[T_X, T_taps, T_XC], writes=[T_XC])
                for b in range(NB):
                    bs = slice(b * 512, (b + 1) * 512)
                    P.op("pool", lambda e, bs=bs, XC=XC, XCB=XCB: e.tensor_copy(out=XCB[:, bs], in_=XC[:, bs]),
                         reads=[T_XC], writes=[TB_XCB[x][b]])

            load_conv(0)
            for j in range(4):
                x = j % 2
                XC, XCB, T_XC = XCs[x], XCBs[x], T_XCs[x]
                P.dma("sp", lambda e, j=j: e.dma_start(out=G[:, 0:SLAB], in_=D["gT"][j * 128:(j + 1) * 128, 0:SLAB]), "G", writes=[T_G])
                P.dma("sp", lambda e, j=j: e.dma_start(out=G[:, SLAB:NSL], in_=D["gT"][j * 128:(j + 1) * 128, NS:NS + NPR]), "G2", writes=[T_G])
                def gates_act(d):
                    R, I, M = Rb[d], Ib[d], Mb[d]
                    for gi_, (dst, TB) in enumerate(((R, TB_R[d]), (I, TB_I[d]))):
                        for b in BLK[d]:
                            bs = slice(b * 512, (b + 1) * 512)
                            ps, T_ps = bank.next()
                            P.op("pe", lambda e, ps=ps, d=d, gi_=gi_, j=j, bs=bs, XCB=XCB: e.matmul(
                                ps[:], lhsT=wg[:, (d * 2 + gi_) * 4 + j, :], rhs=XCB[:, bs], start=True, stop=True),
                                reads=[T_wg, TB_XCB[x][b]], writes=[T_ps])
                            P.op("act", lambda e, ps=ps, dst=dst, d=d, gi_=gi_, j=j, c_=cs(d, b): e.activation(
                                out=dst[:, c_], in_=ps[:], func=AF.Sigmoid,
                                bias=bg[:, d, gi_, j:j + 1], scale=1.0), reads=[T_ps, T_bg], writes=[TB[b]])
                    for b in BLK[d]:
                        P.op("act", lambda e, d=d, j=j, c_=cs(d, b), R=R, M=M: e.activation(
                            out=M[:, c_], in_=R[:, c_], func=AF.Exp, scale=sp2[:, d, j:j + 1]),
                            reads=[TB_R[d][b], T_sp2], writes=[TB_M[d][b]])
                    for b in BLK[d]:
                        P.op("act", lambda e, d=d, j=j, c_=cs(d, b), R=R: e.activation(
                            out=R[:, c_], in_=R[:, c_], func=AF.Exp, scale=sp_[:, d, j:j + 1]),
                            reads=[TB_R[d][b], T_sp], writes=[TB_R[d][b]])

                def ew(d):
                    R, I, M = Rb[d], Ib[d], Mb[d]
                    for b in BLK[d]:
                        P.op("act", lambda e, c_=cs(d, b), M=M: e.activation(out=M[:, c_], in_=M[:, c_], func=AF.Relu, bias=cst[:, 0:1], scale=-1.0),
                             reads=[TB_M[d][b], T_cst], writes=[TB_M[d][b]])
                    runs = [([0, 1, 2, 3, 4], 0, 0, 2560), ([8], 2560, NS, 512)] if d == 0 else [(list(range(NB)), 0, 0, NT)]
                    for (bl, c0_, x0_, n_) in runs:
                        tk = [TB_I[d][b] for b in bl]
                        P.op("dve", lambda e, c0_=c0_, x0_=x0_, n_=n_, I=I, XC=XC: e.tensor_tensor(
                            out=I[:, c0_:c0_ + n_], in0=I[:, c0_:c0_ + n_], in1=XC[:, x0_:x0_ + n_], op=ALU.mult),
                            reads=tk + [T_XC], writes=tk)
                    for b in BLK[d]:
                        P.op("act", lambda e, c_=cs(d, b), M=M: e.activation(out=M[:, c_], in_=M[:, c_], func=AF.Sqrt),
                             reads=[TB_M[d][b]], writes=[TB_M[d][b]])
                    for b in BLK[d]:
                        P.op("pool", lambda e, c_=cs(d, b), I=I, M=M: e.tensor_tensor(out=I[:, c_], in0=I[:, c_], in1=M[:, c_], op=ALU.mult),
                             reads=[TB_I[d][b], TB_M[d][b]], writes=[TB_I[d][b]])

                def scan(d):
                    R, I, M = Rb[d], Ib[d], Mb[d]
                    if d == 0:
                        sg = [(0, SLAB, 0), (2560, 2816, 8), (2816, 3072, 8)]
                    else:
                        sg = [(0, NS, None), (NS, NS + 256, 8), (NS + 256, NS + 512, 8)]
                    for si, (s0, s1, bb) in enumerate(sg):
                        init = st0[:, j * 2 + d:j * 2 + d + 1] if si == 0 else 0.0
                        if si == 0:
                            rb = [0, 1, 2, 3, 4] if d == 0 else list(range(8))
                        else:
                            rb = [8]
                        rd = [TB_R[d][b] for b in rb] + [TB_I[d][b] for b in rb] + [TB_M[d][b] for b in rb] + [T_st0]
                        if d == 0:
                            P.op("dve", lambda e, s0=s0, s1=s1, init=init, R=R, I=I, M=M: e.tensor_tensor_scan(
                                out=M[:, s0:s1], data0=R[:, s0:s1], data1=I[:, s0:s1], initial=init,
                                op0=ALU.mult, op1=ALU.add), reads=rd, writes=[T_H[0]] + [TB_M[d][b] for b in rb])
                        else:
                            P.op("dve", lambda e, s0=s0, s1=s1, init=init, R=R, I=I, M=M: e.tensor_tensor_scan(
                                out=M[:, s0:s1][:, ::-1], data0=R[:, s0:s1][:, ::-1], data1=I[:, s0:s1][:, ::-1], initial=init,
                                op0=ALU.mult, op1=ALU.add), reads=rd, writes=[T_H[1]] + [TB_M[d][b] for b in rb])

                gates_act(0)
                ew(0)
                gates_act(1)
                if j + 1 < 4:
                    load_conv(j + 1)
                scan(0)
                ew(1)
                scan(1)
                HA, HB = Mb[0], Mb[1]
                HA_T = [T_H[0]] + [TB_M[0][b] for b in BLK[0]]
                HB_T = [T_H[1]] + [TB_M[1][b] for b in range(NB)]
                for s_ in range(2):
                    P.op("pool", lambda e, j=j, s_=s_: e.tensor_copy(out=nst[:, j, s_, 0:1], in_=HA[:, 2560 + s_ * 256 + 255:2560 + s_ * 256 + 256]),
                         reads=HA_T, writes=[T_nst])
                    P.op("pool", lambda e, j=j, s_=s_: e.tensor_copy(out=nst[:, j, s_, 1:2], in_=HB[:, NS + s_ * 256:NS + s_ * 256 + 1]),
                         reads=HB_T, writes=[T_nst])
                P.op("pool", lambda e: e.tensor_tensor(out=HA[:, 0:SLAB], in0=HA[:, 0:SLAB], in1=HB[:, 0:SLAB], op=ALU.add),
                     reads=HA_T + HB_T, writes=[T_H[0]] + [TB_M[0][b] for b in (0, 1, 2, 3, 4)])
                P.op("pool", lambda e: e.tensor_tensor(out=HA[:, 2560:3072], in0=HA[:, 2560:3072], in1=HB[:, NS:NT], op=ALU.add),
                     reads=HA_T + HB_T, writes=[T_H[0], TB_M[0][8]])
                P.op("dve", lambda e: e.tensor_tensor(out=YA[:, 0:SLAB], in0=HA[:, 0:SLAB], in1=G[:, 0:SLAB], op=ALU.mult),
                     reads=HA_T + [T_G], writes=[T_YA])
                P.op("dve", lambda e: e.tensor_tensor(out=YA[:, SLAB:NSL], in0=HA[:, 2560:3072], in1=G[:, SLAB:NSL], op=ALU.mult),
                     reads=HA_T + [T_G], writes=[T_YA])
                P.dma("sp", lambda e, j=j: e.dma_start(out=D["yT"][j * 128:(j + 1) * 128, :], in_=YA[:]), "YA", reads=[T_YA])
            P.dma("sp", lambda e: e.dma_start(out=D["nst"], in_=nst[:]), "nst", reads=[T_nst])
            P.emit()

        def post_phase(l, tag, wout_name, bout_row, ysrc, ntiles, xsrc_fn, dst_fn):
            with contextlib.ExitStack() as es:
                sb = lambda n, s, d: es.enter_context(nc.sbuf_tensor(tag + n, list(s), d))
                S = mk_ln_scratch(es, tag, nx=5)
                S["ptr"] = Ring([(pb[7], T_pb[7])])
                wout = sb("wout", [128, 8, 1024], BF16); T_wout = Tok()
                lnB = sb("lnB", [128, 4, 1024], F32); T_lnB = Tok()
                gts = sb("gts", [128, 4, 1024], F32); T_gts = Tok()
                brow = sb("brow", [1, 2, 1024], BF16); T_brow = Tok()
                ones = sb("ones", [1, 128], BF16); T_ones = Tok()
                b1F = sb("b1F", [128, 32], F32); T_b1F = Tok()
                wsl = Ring([(sb("wsl%d" % i, [128, 8, 512], BF16), (Tok(), Tok())) for i in range(3)])
                yTg = [sb("yTg%d" % i, [128, 8, 512], BF16) for i in range(2)]; T_yTg = [Tok(), Tok()]
                hT = [sb("hT%d" % i, [128, 8, 512], BF16) for i in range(2)]; T_hT = [Tok(), Tok()]
                uT = sb("uT", [128, 32, 512], BF16); T_uT = Tok()
                x1 = [[sb("x1_%d_%d" % (a, i), [128, 1024], F32) for i in range(4)] for a in range(2)]
                T_x1 = [[Tok() for _ in range(4)] for a in range(2)]
                tmp = Ring([(sb("tmp%d" % i, [128, 1024], F32), Tok()) for i in range(4)])
                rl = Ring([(sb("rl%d" % i, [128, 512], F32), Tok()) for i in range(2)])
                P.dma("pool", lambda e: e.dma_start(out=wout[:], in_=D[wout_name].rearrange("(c p) n -> p c n", p=128)),
                      tag + "wout", writes=[T_wout])
                P.dma("sp", lambda e: e.dma_start(out=lnB[:], in_=D["lnB"][l].rearrange("v p n -> p v n")), tag + "lnB", writes=[T_lnB])
                P.dma("sp", lambda e: e.dma_start(out=gts[:], in_=D["gsc"][l].rearrange("v p n -> p v n")), tag + "gts", writes=[T_gts])
                P.dma("pool", lambda e: e.dma_start(out=brow[:, 0, :], in_=D[bout_row]), tag + "brow", writes=[T_brow])
                P.dma("pool", lambda e: e.dma_start(out=brow[:, 1, :], in_=D["b2row"][l:l + 1, :]), tag + "brow", writes=[T_brow])
                P.dma("sp", lambda e: e.dma_start(out=b1F[:], in_=D["b1F"][:, l, :]), tag + "b1F", writes=[T_b1F])
                P.op("dve", lambda e: e.memset(ones[:], 1.0), writes=[T_ones])
                ngrp = (ntiles + 3) // 4
                print("SBUF remaining in post_phase", tag, nc.sbuf_bytes_remaining)
                bank = Ring([(pb[i], T_pb[i]) for i in (0, 1)])
                bankA = Ring([(pb[2], T_pb[2]), (pb[5], T_pb[5])])
                tb = [(pb[i], T_pb[i]) for i in (3, 4, 6, 7)]
                xhs = {}

                def gi(g):
                    nt = min(4, ntiles - g * 4)
                    return nt, nt * 128, g * 512

                def rsel(t):
                    return 0 if t < ntiles - 4 else 1

                def A1(g):
                    nt, T, c0 = gi(g)
                    a = g % 2
                    P.dma("sp", lambda e, c0=c0, T=T, a=a: e.dma_start(
                        out=yTg[a][:, :, 0:T], in_=ysrc[:, c0:c0 + T].rearrange("(c p) t -> p c t", p=128)),
                        tag + "yTg%d" % a, writes=[T_yTg[a]])
                    for tt in range(nt):
                        P.dma("sp", lambda e, a=a, tt=tt, t=g * 4 + tt: e.dma_start(out=x1[a][tt][:], in_=xsrc_fn(t)),
                              tag + "x1_%d_%d" % (a, tt), writes=[T_x1[a][tt]])
                    tms = []
                    for tt in range(nt):
                        r = rsel(g * 4 + tt)
                        tm, T_tm = tmp.next()
                        tms.append((tm, T_tm))
                        for half in range(2):
                            ps, T_ps = bankA.next()
                            for c in range(8):
                                P.op("pe", lambda e, ps=ps, c=c, tt=tt, half=half, a=a: e.matmul(
                                    ps[:], lhsT=yTg[a][:, c, tt * 128:(tt + 1) * 128], rhs=wout[:, c, half * 512:(half + 1) * 512],
                                    start=(c == 0), stop=False), reads=[T_yTg[a], T_wout], writes=[T_ps], sig=False)
                            P.op("pe", lambda e, ps=ps, half=half: e.matmul(
                                ps[:], lhsT=ones[0:1, :], rhs=brow[0:1, 0, half * 512:(half + 1) * 512], start=False, stop=True),
                                reads=[T_ones, T_brow], writes=[T_ps])
                            P.op("dve", lambda e, ps=ps, tm=tm, half=half, r=r: e.tensor_tensor(
                                out=tm[:, half * 512:(half + 1) * 512], in0=ps[:], in1=gts[:, r * 2 + 0, half * 512:(half + 1) * 512],
                                op=ALU.mult), reads=[T_ps, T_gts], writes=[T_tm])
                    for tt in range(nt):
                        tm, T_tm = tms[tt]
                        P.op("dve", lambda e, a=a, tt=tt, tm=tm: e.scalar_tensor_tensor(
                            out=tm[:], in0=x1[a][tt][:], scalar=ALPHA, in1=tm[:], op0=ALU.mult, op1=ALU.add),
                            reads=[T_x1[a][tt], T_tm], writes=[T_tm])
                    x1s = [(x1[a][tt], T_x1[a][tt]) for tt in range(nt)]
                    ln_affine_multi(S, tms, lnB[:, 0, :], lnB[:, 1, :], T_lnB, x1s)
                    mvs = ln_stats_multi(S, x1s)
                    for tt in range(nt):
                        mv, T_mv = mvs[tt]
                        xh, T_xh = S["xh"].next()
                        P.op("act", lambda e, xh=xh, mv=mv, a=a, tt=tt: e.activation(
                            out=xh[:], in_=x1[a][tt][:], func=AF.Identity, bias=mv[:, 2:3], scale=mv[:, 1:2]),
                            reads=[T_x1[a][tt], T_mv], writes=[T_xh])
                        xhs[(g, tt)] = (xh, T_xh)

                def A2(g):
                    nt, T, c0 = gi(g)
                    a = g % 2
                    rs = [rsel(g * 4 + tt) for tt in range(nt)]
                    for tt in range(nt):
                        xh, T_xh = xhs[(g, tt)]
                        for c in range(8):
                            bk, T_bk = tb[c // 2]
                            col = ((c % 2) * 4 + tt) * 128
                            P.op("pe", lambda e, c=c, xh=xh, bk=bk, col=col: e.transpose(
                                out=bk[:].bitcast(BF16)[:, col:col + 128], in_=xh[:, c * 128:(c + 1) * 128], identity=ident[:]),
                                reads=[T_xh, T_ident], writes=[T_bk], sig=(tt == nt - 1 or c == 7))
                    runs = []
                    t0 = 0
                    for tt in range(1, nt + 1):
                        if tt == nt or rs[tt] != rs[t0]:
                            runs.append((t0, tt, rs[t0]))
                            t0 = tt
                    for c in range(8):
                        bk, T_bk = tb[c // 2]
                        for (a_, b_, r) in runs:
                            col = ((c % 2) * 4 + a_) * 128
                            P.op("act", lambda e, c=c, bk=bk, col=col, a_=a_, b_=b_, r=r, a=a: e.activation(
                                out=hT[a][:, c, a_ * 128:b_ * 128], in_=bk[:].bitcast(BF16)[:, col:col + (b_ - a_) * 128],
                                func=AF.Identity, bias=modF[l][:, 3 * 8 + c, r:r + 1], scale=modF[l][:, 4 * 8 + c, r:r + 1]),
                                reads=[T_bk, T_modF[l]], writes=[T_hT[a]])

                W1L = {}

                def load_w1(g, blk):
                    if (g, blk) in W1L or blk >= 8 or g >= ngrp:
                        return
                    ws, T_ws = wsl.next()
                    P.dma("sp", lambda e, ws=ws, blk=blk: e.dma_start(
                        out=ws[:], in_=D["w1b"][l, :, blk * 512:(blk + 1) * 512].rearrange("(c p) n -> p c n", p=128)),
                        tag + "wsl%d" % ((wsl.i - 1) % 3), writes=list(T_ws))
                    W1L[(g, blk)] = (ws, T_ws)

                def P1(g, blks):
                    nt, T, c0 = gi(g)
                    a = g % 2
                    for blk in blks:
                        load_w1(g, blk)
                        load_w1(g, blk + 1)
                        ws, T_ws = W1L[(g, blk)]
                        for f in range(4):
                            ps, T_ps = bank.next()
                            for c in range(8):
                                P.op("pe", lambda e, ps=ps, ws=ws, f=f, c=c, T=T, a=a: e.matmul(
                                    ps[:, 0:T], lhsT=ws[:, c, f * 128:(f + 1) * 128], rhs=hT[a][:, c, 0:T],
                                    start=(c == 0), stop=(c == 7)), reads=[T_ws[0], T_ws[1], T_hT[a]], writes=[T_ps], sig=(c == 7))
                            rt, T_rt = rl.next()
                            fi = blk * 4 + f
                            P.op("act", lambda e, ps=ps, rt=rt, fi=fi, T=T: e.activation(
                                out=rt[:, 0:T], in_=ps[:, 0:T], func=AF.Relu, bias=b1F[:, fi:fi + 1], scale=1.0),
                                reads=[T_ps, T_b1F], writes=[T_rt])
                            P.op("pool", lambda e, rt=rt, fi=fi, T=T: e.tensor_tensor(
                                out=uT[:, fi, 0:T], in0=rt[:, 0:T], in1=rt[:, 0:T], op=ALU.mult),
                                reads=[T_rt], writes=[T_uT])

                W2L = {}

                def load_w2(g, blk):
                    if (g, blk) in W2L or blk >= 8:
                        return
                    ws, T_ws = wsl.next()
                    P.dma("sp", lambda e, ws=ws, blk=blk: e.dma_start(
                        out=ws[:, 0:4, :],
                        in_=D["w2b"][l, blk * 512:(blk + 1) * 512, 0:512].rearrange("(c p) n -> p c n", p=128)),
                        tag + "wsl%d" % ((wsl.i - 1) % 3), writes=[T_ws[0]])
                    P.dma("sp", lambda e, ws=ws, blk=blk: e.dma_start(
                        out=ws[:, 4:8, :],
                        in_=D["w2b"][l, blk * 512:(blk + 1) * 512, 512:1024].rearrange("(c p) n -> p c n", p=128)),
                        tag + "wslh%d" % ((wsl.i - 1) % 3), writes=[T_ws[1]])
                    W2L[(g, blk)] = (ws, T_ws)

                def P2(g):
                    nt, T, c0 = gi(g)
                    for blk in range(8):
                        load_w2(g, blk)
                        load_w2(g, blk + 1)
                        ws, T_ws = W2L[(g, blk)]
                        for tt in range(nt):
                            for half in range(2):
                                ps, T_ps = pb[tt * 2 + half], T_pb[tt * 2 + half]
                                for c in range(4):
                                    P.op("pe", lambda e, ps=ps, ws=ws, blk=blk, c=c, tt=tt, half=half: e.matmul(
                                        ps[:], lhsT=uT[:, blk * 4 + c, tt * 128:(tt + 1) * 128],
                                        rhs=ws[:, half * 4 + c, :],
                                        start=(blk == 0 and c == 0), stop=False),
                                        reads=[T_uT, T_ws[half]], writes=[T_ps], sig=(c == 3))
                    for tt in range(nt):
                        for half in range(2):
                            ps, T_ps = pb[tt * 2 + half], T_pb[tt * 2 + half]
                            P.op("pe", lambda e, ps=ps, half=half: e.matmul(
                                ps[:], lhsT=ones[0:1, :], rhs=brow[0:1, 1, half * 512:(half + 1) * 512], start=False, stop=True),
                                reads=[T_ones, T_brow], writes=[T_ps])

                TAILS = {}

                def tail(g):
                    nt, T, c0 = gi(g)
                    a = g % 2
                    tms = []
                    for tt in range(nt):
                        r = rsel(g * 4 + tt)
                        tm, T_tm = tmp.next()
                        tms.append((tm, T_tm))
                        for half in range(2):
                            ps, T_ps = pb[tt * 2 + half], T_pb[tt * 2 + half]
                            P.op("dve", lambda e, ps=ps, tm=tm, half=half, r=r: e.tensor_tensor(
                                out=tm[:, half * 512:(half + 1) * 512], in0=ps[:], in1=gts[:, r * 2 + 1, half * 512:(half + 1) * 512],
                                op=ALU.mult), reads=[T_ps, T_gts], writes=[T_tm])
                    for tt in range(nt):
                        tm, T_tm = tms[tt]
                        P.op("dve", lambda e, tm=tm, tt=tt, a=a: e.scalar_tensor_tensor(
                            out=tm[:], in0=x1[a][tt][:], scalar=ALPHA, in1=tm[:], op0=ALU.mult, op1=ALU.add),
                            reads=[T_x1[a][tt], T_tm], writes=[T_tm])
                    TAILS[g] = (tms, tmp.i)

                def tail_rest(g):
                    nt, T, c0 = gi(g)
                    tms, tmp_i = TAILS[g]
                    ln_affine_multi(S, tms, lnB[:, 2, :], lnB[:, 3, :], T_lnB, tms)
                    for tt in range(nt):
                        t = g * 4 + tt
                        tm, T_tm = tms[tt]
                        P.dma("pool", lambda e, tm=tm, t=t: e.dma_start(out=dst_fn(t), in_=tm[:]),
                              tag + "tmp%d" % ((tmp_i - nt + tt) % 4), reads=[T_tm])

                A1(0)
                A2(0)
                for g in range(ngrp):
                    load_w1(g, 0)
                    load_w1(g, 1)
                    P1(g, range(0, 8))
                    load_w2(g, 0)
                    if g + 1 < ngrp:
                        A1(g + 1)
                    P2(g)
                    tail(g)
                    if g + 1 < ngrp:
                        A2(g + 1)
                    tail_rest(g)
                P.emit()

        def xsrc0(t):
            if t < 18:
                return D["xs"][t * 128:(t + 1) * 128, :]
            return D["xp"][(t - 18) * 128:(t - 17) * 128, :]

        post_phase(0, "d0", "e_w_out", "ebrow", D["yT"], 22, xsrc0, lambda t: D["x1"][t * 128:(t + 1) * 128, :])

        with contextlib.ExitStack() as es:
            sb = lambda n, s, d: es.enter_context(nc.sbuf_tensor("a1" + n, list(s), d))
            S = mk_ln_scratch(es, "a1", nx=5)
            S["tbanks"] = [(pb[i], T_pb[i]) for i in (4, 5, 6, 7)]
            wq = sb("wq", [128, 8, 3072], BF16); T_wq = Tok()
            bqk = sb("bqk", [128, 16], F32); T_bqk = Tok()
            bq8 = sb("bq8", [128, 8], F32); T_bq8 = Tok()
            bvB = sb("bvB", [128, 1024], F32); T_bvB = Tok()
            xin = Ring([(sb("xin%d" % i, [128, 1024], F32), Tok()) for i in range(6)])
            hTr = Ring([(sb("hT%d" % i, [128, 8, 512], BF16), Tok()) for i in range(2)])
            qm = Ring([(sb("qm%d" % i, [128, 2, 512], BF16), Tok()) for i in range(2)])
            kf = Ring([(sb("kf%d" % i, [128, 512], F32), Tok()) for i in range(2)])
            kb = Ring([(sb("kb%d" % i, [128, 512], BF16), Tok()) for i in range(2)])
            vf = Ring([(sb("vf%d" % i, [128, 1024], F32), Tok()) for i in range(2)])
            vE = Ring([(sb("vE%d" % i, [128, 16, 128], BF16), Tok()) for i in range(2)])
            T_wqp = [Tok(), Tok(), Tok()]
            for part in range(3):
                P.dma("pool", lambda e, part=part: e.dma_start(
                    out=wq[:, :, part * 1024:(part + 1) * 1024],
                    in_=D["o_w_qkv"][:, part * 1024:(part + 1) * 1024].rearrange("(c p) n -> p c n", p=128)),
                    "wq%d" % part, writes=[T_wqp[part]])
            P.dma("sp", lambda e: e.dma_start(out=bqk[:], in_=D["bqkF"]), "bqk", writes=[T_bqk])
            P.dma("sp", lambda e: e.dma_start(out=bvB[:], in_=D["bvB"]), "bvB", writes=[T_bvB])
            P.op("dve", lambda e: e.tensor_scalar(out=bq8[:], in0=bqk[:, 0:8], scalar1=0.125, scalar2=None, op0=ALU.mult),
                 reads=[T_bqk], writes=[T_bq8])
            for (q_, T_q) in qm.items:
                P.op("pool", lambda e, q_=q_: e.memset(q_[:], 0.0), writes=[T_q])
            for (v_, T_v) in vE.items:
                P.op("pool", lambda e, v_=v_: e.memset(v_[:], 1.0), writes=[T_v])
            bank = Ring([(pb[i], T_pb[i]) for i in range(4)])
            S["mt"] = Ring([(sb("mt%d" % i, [128, 8, 128], F32), Tok()) for i in range(2)])

            def LNG1a(grp):
                nt = 4 if grp < 5 else 2
                ts = [grp * 4 + tt for tt in range(nt)]
                srcs = [D["x1"][t * 128:(t + 1) * 128, :] for t in ts]
                return ln_group_1(S, srcs, xin, "a1xin")

            def LNG1b(grp, xhl):
                nt = 4 if grp < 5 else 2
                ts = [grp * 4 + tt for tt in range(nt)]
                hT, T_hT = hTr.next()
                ln_group_2(S, xhl, 1, 0, [0 if t < 18 else 1 for t in ts], hT, T_hT)
                return hT, T_hT

            nxt = LNG1b(0, LNG1a(0))
            for grp in range(6):
                nt = 4 if grp < 5 else 2
                T = nt * 128
                c0 = grp * 512
                hT, T_hT = nxt
                if grp + 1 < 6:
                    xhl_n = LNG1a(grp + 1)
                for co in range(16):
                    ps, T_ps = bank.next()
                    for c in range(8):
                        P.op("pe", lambda e, ps=ps, co=co, c=c, hT=hT, T=T: e.matmul(
                            ps[:, 0:T], lhsT=wq[:, c, co * 128:(co + 1) * 128], rhs=hT[:, c, 0:T],
                            start=(c == 0), stop=(c == 7)), reads=[T_wqp[co // 8], T_hT], writes=[T_ps], sig=(c == 7))
                    if co < 8:
                        q_, T_q = qm.next()
                        for hh in range(2):
                            P.op("act", lambda e, ps=ps, q_=q_, co=co, hh=hh, T=T: e.activation(
                                out=q_[hh * 64:(hh + 1) * 64, hh, 0:T], in_=ps[hh * 64:(hh + 1) * 64, 0:T], func=AF.Identity,
                                bias=bq8[hh * 64:(hh + 1) * 64, co:co + 1], scale=0.125), reads=[T_ps, T_bq8], writes=[T_q])
                        if grp < 4:
                            qc0, qn, off = c0, T, 0
                        elif grp == 4:
                            qc0, qn, off = None, 0, 0
                        else:
                            qc0, qn, off = None, 0, 0
                        if grp < 4:
                            for hh, nm in enumerate(("qA", "qB")):
                                P.dma("act", lambda e, q_=q_, co=co, hh=hh, nm=nm, c0=c0, T=T: e.dma_start(
                                    out=D[nm][co * 128:(co + 1) * 128, c0:c0 + T], in_=q_[:, hh, 0:T]),
                                    "qm%d" % ((qm.i - 1) % 2), reads=[T_q])
                        elif grp == 4:
                            for hh, nm in enumerate(("qA", "qB")):
                                P.dma("act", lambda e, q_=q_, co=co, hh=hh, nm=nm: e.dma_start(
                                    out=D[nm][co * 128:(co + 1) * 128, 2048:2304], in_=q_[:, hh, 256:512]),
                                    "qm%d" % ((qm.i - 1) % 2), reads=[T_q])
                        else:
                            for hh, nm in enumerate(("qA", "qB")):
                                P.dma("act", lambda e, q_=q_, co=co, hh=hh, nm=nm: e.dma_start(
                                    out=D[nm][co * 128:(co + 1) * 128, 2304:2560], in_=q_[:, hh, 0:256]),
                                    "qm%d" % ((qm.i - 1) % 2), reads=[T_q])
                    else:
                        ck = co - 8
                        kt, T_kt = kf.next()
                        P.op("act", lambda e, ps=ps, kt=kt, co=co, T=T: e.activation(
                            out=kt[:, 0:T], in_=ps[:, 0:T], func=AF.Identity, bias=bqk[:, co:co + 1], scale=1.0),
                            reads=[T_ps, T_bqk], writes=[T_kt])
                        kbt, T_kbt = kb.next()
                        P.op("pool", lambda e, kt=kt, kbt=kbt, T=T: e.tensor_copy(out=kbt[:, 0:T], in_=kt[:, 0:T]),
                             reads=[T_kt], writes=[T_kbt])
                        P.dma("pool", lambda e, kbt=kbt, ck=ck, c0=c0, T=T: e.dma_start(
                            out=D["kT"][ck * 128:(ck + 1) * 128, c0:c0 + T], in_=kbt[:, 0:T]),
                            "kb%d" % ((kb.i - 1) % 2), reads=[T_kbt])
                        if grp == 4:
                            P.dma("act", lambda e, kt=kt, ck=ck: e.dma_start(
                                out=D["nkT"][ck * 128:(ck + 1) * 128, 0:256], in_=kt[:, 256:512]),
                                "kf%d" % ((kf.i - 1) % 2), reads=[T_kt])
                        elif grp == 5:
                            P.dma("act", lambda e, kt=kt, ck=ck: e.dma_start(
                                out=D["nkT"][ck * 128:(ck + 1) * 128, 256:512], in_=kt[:, 0:256]),
                                "kf%d" % ((kf.i - 1) % 2), reads=[T_kt])
                for tt in range(nt):
                    t = grp * 4 + tt
                    vt, T_vt = vf.next()
                    for half in range(2):
                        ps, T_ps = bank.next()
                        for c in range(8):
                            P.op("pe", lambda e, ps=ps, c=c, tt=tt, half=half, hT=hT: e.matmul(
                                ps[:], lhsT=hT[:, c, tt * 128:(tt + 1) * 128], rhs=wq[:, c, 2048 + half * 512:2048 + (half + 1) * 512],
                                start=(c == 0), stop=(c == 7)), reads=[T_wqp[2], T_hT], writes=[T_ps], sig=(c == 7))
                        P.op("dve", lambda e, ps=ps, vt=vt, half=half: e.tensor_tensor(
                            out=vt[:, half * 512:(half + 1) * 512], in0=ps[:], in1=bvB[:, half * 512:(half + 1) * 512], op=ALU.add),
                            reads=[T_ps, T_bvB], writes=[T_vt])
                    if t >= 18:
                        P.dma("sp", lambda e, vt=vt, t=t: e.dma_start(out=D["nv"][(t - 18) * 128:(t - 17) * 128, :], in_=vt[:]),
                              "vf%d" % ((vf.i - 1) % 2), reads=[T_vt])
                    ve_, T_ve = vE.next()
                    P.op("pool", lambda e, vt=vt, ve_=ve_: e.tensor_copy(
                        out=ve_[:, :, 0:64], in_=vt[:].rearrange("p (h d) -> p h d", d=64)), reads=[T_vt], writes=[T_ve])
                    P.dma("pool", lambda e, ve_=ve_, t=t: e.dma_start(
                        out=D["ve"][t * 128:(t + 1) * 128, :], in_=ve_[:].rearrange("p h d -> p (h d)")),
                        "vE%d" % ((vE.i - 1) % 2), reads=[T_ve])
                if grp + 1 < 6:
                    nxt = LNG1b(grp + 1, xhl_n)
            P.emit()

        with contextlib.ExitStack() as es:
            sb = lambda n, s, d: es.enter_context(nc.sbuf_tensor("b1" + n, list(s), d))
            Qh = Ring([(sb("Qh%d" % i, [128, NOUT], BF16), Tok()) for i in range(2)])
            Kh = Ring([(sb("Kh%d" % i, [128, NSL], BF16), Tok()) for i in range(2)])
            Vh = Ring([(sb("Vh%d" % i, [128, 22, 128], BF16), Tok()) for i in range(2)])
            Kc = Ring([(sb("Kc%d" % i, [128, 512], BF16), Tok()) for i in range(2)])
            Vc = Ring([(sb("Vc%d" % i, [128, 4, 128], BF16), Tok()) for i in range(2)])
            nab = Ring([(sb("nab%d" % i, [128, 7, 128], BF16), Tok()) for i in range(2)])
            Oh = Ring([(sb("Oh%d" % i, [64, NOUT], BF16), Tok()) for i in range(2)])
            pT = Ring([(sb("pT%d" % i, [128, 512], BF16), Tok()) for i in range(4)])
            rc = Ring([(sb("rc%d" % i, [64, 512], F32), Tok()) for i in range(2)])
            for (v_, T_v) in Vc.items:
                P.op("pool", lambda e, v_=v_: e.memset(v_[:], 1.0), writes=[T_v])
            sring = Ring([5, 6, 7])
            cvs = Ring([(sb("cvs%d" % i, [128, 8, 1024], BF16), Tok()) for i in range(2)])

            def conv_bg(kind, blk):
                slot, T_slot = cvs.next()
                key = "cv%d" % ((cvs.i - 1) % 2)
                if kind == "w1":
                    src = D["w1"][1, :, blk * 1024:(blk + 1) * 1024].rearrange("(c p) n -> p c n", p=128)
                    dst = D["w1b"][1, :, blk * 1024:(blk + 1) * 1024].rearrange("(c p) n -> p c n", p=128)
                else:
                    src = D["w2"][1, blk * 1024:(blk + 1) * 1024, :].rearrange("(c p) n -> p c n", p=128)
                    dst = D["w2b"][1, blk * 1024:(blk + 1) * 1024, :].rearrange("(c p) n -> p c n", p=128)
                P.dma("pool", lambda e, slot=slot, src=src: e.dma_start(out=slot[:], in_=src), key, writes=[T_slot])
                P.dma("pool", lambda e, slot=slot, dst=dst: e.dma_start(out=dst, in_=slot[:]), key + "s", reads=[T_slot])
            CVT = [(k, b) for k in ("w1", "w2") for b in range(4)]

            def ty_of(qb, kp):
                if qb == 0:
                    return {0: 2, 1: 3, 2: 5, 3: 6}[kp]
                if qb == 1:
                    return {0: 1, 1: 2, 2: 3, 3: 5}[kp]
                return kp - qb + 2

            def qbs_of(kp):
                q = [qb for qb in range(16) if qb >= 2 and abs(qb - kp) <= 2]
                if kp <= 3:
                    q = [0, 1] + q
                return sorted(q)

            HL = {}

            def load_head(h):
                if h >= 16:
                    return
                ch = h // 2
                hh = h % 2
                q_, T_q = Qh.next(); k_, T_k = Kh.next(); v_, T_v = Vh.next()
                kc_, T_kc = Kc.next(); vc_, T_vc = Vc.next(); nb_, T_nb = nab.next()
                i2 = (Qh.i - 1) % 2
                qn = "qA" if hh == 0 else "qB"
                P.dma("sp", lambda e, q_=q_, qn=qn, ch=ch: e.dma_start(out=q_[:], in_=D[qn][ch * 128:(ch + 1) * 128, :]),
                      "Qh%d" % i2, writes=[T_q])
                P.dma("sp", lambda e, k_=k_, ch=ch: e.dma_start(out=k_[:], in_=D["kT"][ch * 128:(ch + 1) * 128, :]),
                      "Kh%d" % i2, writes=[T_k])
                P.dma("sp", lambda e, v_=v_, h=h: e.dma_start(
                    out=v_[:], in_=D["ve"][:, h * 128:(h + 1) * 128].rearrange("(t p) c -> p t c", p=128)),
                    "Vh%d" % i2, writes=[T_v])
                P.dma("pool", lambda e, kc_=kc_, ch=ch: e.dma_start(out=kc_[:], in_=D["ckT"][ch * 128:(ch + 1) * 128, :]),
                      "Kc%d" % i2, writes=[T_kc])
                P.dma("pool", lambda e, vc_=vc_, h=h: e.dma_start(
                    out=vc_[:, :, 0:64], in_=D["cv"][:, h * 64:(h + 1) * 64].rearrange("(t p) d -> p t d", p=128)),
                    "Vc%d" % i2, writes=[T_vc])
                P.dma("pool", lambda e, nb_=nb_, h=h: e.dma_start(out=nb_[:], in_=D["nabias"][h]), "nab%d" % i2, writes=[T_nb])
                HL[h] = (q_, T_q, k_, T_k, v_, T_v, kc_, T_kc, vc_, T_vc, nb_, T_nb)

            load_head(0)
            for h in range(16):
                load_head(h + 1)
                if h % 2 == 0:
                    conv_bg(*CVT[h // 2])
                (q_, T_q, k_, T_k, v_, T_v, kc_, T_kc, vc_, T_vc, nb_, T_nb) = HL[h]
                o_, T_o = Oh.next()
                i2 = h % 2
                units = []
                for qg in range(4):
                    qs = slice(qg * 512, (qg + 1) * 512)
                    for cc in range(4):
                        def sc(bank, cc=cc, qs=qs, kc_=kc_, q_=q_):
                            return [((lambda e: e.matmul(pb[bank][:], lhsT=kc_[:, cc * 128:(cc + 1) * 128], rhs=q_[:, qs],
                                                         start=True, stop=True)), [T_kc, T_q], True)]

                        def pv(pt, qg=qg, cc=cc, vc_=vc_):
                            return [((lambda e: e.matmul(pb[qg][:], lhsT=vc_[:, cc, :], rhs=pt[:, 0:512],
                                                         start=(cc == 0), stop=False)), [T_vc], qg)]
                        units.append(dict(score=sc, ncol=512, pv=pv))
                for kp in range(18):
                    qbs = qbs_of(kp)
                    qlo, nq = qbs[0], len(qbs)
                    assert qbs == list(range(qlo, qlo + nq))
                    pieces = [(0, min(nq, 4))] + ([(4, nq)] if nq > 4 else [])
                    for (s0, s1) in pieces:
                        def sc(bank, s0=s0, s1=s1, kp=kp, qlo=qlo, k_=k_, q_=q_, nb_=nb_):
                            ops = [((lambda e: e.matmul(
                                pb[bank][:, 0:(s1 - s0) * 128], lhsT=k_[:, kp * 128:(kp + 1) * 128],
                                rhs=q_[:, (qlo + s0) * 128:(qlo + s1) * 128], start=True, stop=False)), [T_k, T_q], False)]
                            i = s0
                            while i < s1:
                                ti0 = 6 - ty_of(qlo + i, kp)
                                n = 1
                                while i + n < s1 and (6 - ty_of(qlo + i + n, kp)) == ti0 + n:
                                    n += 1
                                last = (i + n >= s1)
                                ops.append(((lambda e, i=i, n=n, ti0=ti0, last=last: e.matmul(
                                    pb[bank][:, (i - s0) * 128:(i - s0 + n) * 128], lhsT=ident[:],
                                    rhs=nb_[:, ti0:ti0 + n, :].rearrange("p a b -> p (a b)"), start=False, stop=last)),
                                    [T_ident, T_nb], last))
                                i += n
                            return ops

                        def pv(pt, s0=s0, s1=s1, kp=kp, qlo=qlo, v_=v_):
                            ops = []
                            i = s0
                            while i < s1:
                                qb0 = qlo + i
                                bk = qb0 // 4
                                n = min(s1 - i, 4 - (qb0 % 4))
                                ops.append(((lambda e, bk=bk, qb0=qb0, n=n, i=i: e.matmul(
                                    pb[bk][:, (qb0 % 4) * 128:(qb0 % 4 + n) * 128], lhsT=v_[:, kp, :],
                                    rhs=pt[:, (i - s0) * 128:(i - s0 + n) * 128], start=False, stop=False)), [T_v], bk))
                                i += n
                            return ops
                        units.append(dict(score=sc, ncol=(s1 - s0) * 128, pv=pv))
                for s_ in range(2):
                    qs = slice(OWN + s_ * 256, OWN + (s_ + 1) * 256)

                    def sc(bank, s_=s_, qs=qs, k_=k_, q_=q_):
                        ops = []
                        for kt in range(2):
                            tile_i = 18 + 2 * s_ + kt
                            ops.append(((lambda e, kt=kt, tile_i=tile_i: e.matmul(
                                pb[bank][:, kt * 256:(kt + 1) * 256], lhsT=k_[:, tile_i * 128:(tile_i + 1) * 128], rhs=q_[:, qs],
                                start=True, stop=True)), [T_k, T_q], kt == 1))
                        return ops

                    def pv(pt, s_=s_, v_=v_):
                        ops = []
                        for kt in range(2):
                            tile_i = 18 + 2 * s_ + kt
                            ops.append(((lambda e, kt=kt, tile_i=tile_i: e.matmul(
                                pb[4][:, s_ * 256:(s_ + 1) * 256], lhsT=v_[:, tile_i, :], rhs=pt[:, kt * 256:(kt + 1) * 256],
                                start=(kt == 0), stop=(kt == 1))), [T_v], 4))
                        return ops
                    units.append(dict(score=sc, ncol=512, pv=pv))

                def emit_score(u):
                    u["bank"] = sring.next()
                    for fn, rd, sg in u["score"](u["bank"]):
                        P.op("pe", fn, reads=rd, writes=[T_pb[u["bank"]]], sig=sg)

                def emit_exp(u):
                    pt, T_pt = pT.next()
                    u["pt"], u["T_pt"] = pt, T_pt
                    bank, ncol = u["bank"], u["ncol"]
                    P.op("act", lambda e, pt=pt, bank=bank, ncol=ncol: e.activation(
                        out=pt[:, 0:ncol], in_=pb[bank][:, 0:ncol], func=AF.Exp), reads=[T_pb[bank]], writes=[T_pt])

                def emit_pv(u):
                    for fn, rd, bk in u["pv"](u["pt"]):
                        P.op("pe", fn, reads=rd + [u["T_pt"]], writes=[T_pb[bk]], sig=True)

                emit_score(units[0]); emit_score(units[1])
                for iu, u in enumerate(units):
                    emit_exp(u)
                    if iu + 2 < len(units):
                        emit_score(units[iu + 2])
                    emit_pv(u)
                for bk in range(5):
                    r_, T_r = rc.next()
                    P.op("dve", lambda e, r_=r_, bk=bk: e.reciprocal(out=r_[:], in_=pb[bk][64:128, :]),
                         reads=[T_pb[bk]], writes=[T_r])
                    P.op("dve", lambda e, r_=r_, bk=bk, o_=o_: e.tensor_tensor(
                        out=o_[:, bk * 512:(bk + 1) * 512], in0=pb[bk][0:64, :], in1=r_[:], op=ALU.mult),
                        reads=[T_pb[bk], T_r], writes=[T_o])
                P.dma("sp", lambda e, o_=o_, h=h: e.dma_start(out=D["oT"][h * 64:(h + 1) * 64, :], in_=o_[:]),
                      "Oh%d" % i2, reads=[T_o])
            P.emit()

        def dst1(t):
            if t < 16:
                return D["ys"][t * 128:(t + 1) * 128, :]
            return D["yp"][(t - 16) * 128:(t - 15) * 128, :]

        def xsrc1(t):
            if t < 16:
                return D["x1"][t * 128:(t + 1) * 128, :]
            return D["x1"][(t + 2) * 128:(t + 3) * 128, :]

        post_phase(1, "d1", "o_w_out", "obrow", D["oT"], 20, xsrc1, dst1)
    except _Stop:
        pass
    return nc


_NC_CACHE = {}


def _fm(vec, nch):
    return np.ascontiguousarray(np.asarray(vec, np.float32).reshape(nch, 128).T)


def _bc(vec):
    return np.ascontiguousarray(np.broadcast_to(np.asarray(vec, np.float32)[None, :], (128, vec.shape[-1])))


def _dft_tables():
    if "dft" in _NC_CACHE:
        return _NC_CACHE["dft"]
    bf = ml_dtypes.bfloat16
    out = {}
    for p in range(2):
        n = (np.arange(NS, dtype=np.int64) + p)[:, None]
        k = (np.arange(SLAB, dtype=np.int64) + p)[None, :]
        ang = 2.0 * np.pi * ((n * k) % NS).astype(np.float64) / NS
        tab = np.stack([np.cos(ang) / 64.0, -np.sin(ang) / 64.0], axis=1)
        out["S%d" % p] = np.ascontiguousarray(tab.astype(np.float32).astype(bf))
        n = (np.arange(256, dtype=np.int64) + p)[:, None]
        k = (np.arange(256, dtype=np.int64) + p)[None, :]
        ang = 2.0 * np.pi * ((n * k) % 256).astype(np.float64) / 256
        tab = np.stack([np.cos(ang) / 16.0, -np.sin(ang) / 16.0], axis=1)
        out["P%d" % p] = np.ascontiguousarray(tab.astype(np.float32).astype(bf))
    c = np.arange(128, dtype=np.int64)[:, None]
    l_ = np.arange(128, dtype=np.int64)[None, :]
    ang = 2.0 * np.pi * ((c * l_) % 128).astype(np.float64) / 128
    s = 1.0 / np.sqrt(128.0)
    out["C"] = np.ascontiguousarray(np.concatenate([np.cos(ang) * s, np.sin(ang) * s], axis=1).astype(np.float32).astype(bf))
    out["I"] = np.eye(128, dtype=np.float32).astype(bf)
    _NC_CACHE["dft"] = out
    return out


def _na_bias(rpb, p):
    reps = [(8, 6), (8, 7), (8, 8), (8, 9), (8, 10), (1, 3), (0, 3)]
    kk = np.arange(128)
    kr, kc = kk // 64, kk % 64
    out = np.empty((16, 128, 7, 128), np.float32)
    for ty, (qb, kp) in enumerate(reps):
        i = (2 * qb + kr)[None, :]
        qc = kc[None, :]
        a = (2 * kp + kr)[:, None]
        kcc = kc[:, None]
        if p == 0:
            gi, ga, gq, gk = i, a, qc, kcc
        else:
            gi, ga, gq, gk = 63 - i, 63 - a, 63 - qc, 63 - kcc
        rs = np.clip(gi - 4, 0, 56)
        cs = np.clip(gq - 8, 0, 48)
        valid = (ga >= rs) & (ga < rs + 8) & (gk >= cs) & (gk < cs + 16)
        ro = np.clip(ga - gi + 7, 0, 14)
        co = np.clip(gk - gq + 15, 0, 30)
        ro_b, co_b = np.broadcast_arrays(ro, co)
        vals = rpb[:, ro_b, co_b]
        out[:, :, 6 - ty, :] = np.where(valid[None], vals, np.float32(-30000.0))
    return out


def kernel(x_prompt, x_sample, c, state_lru, cache_k, cache_v, c_ctx,
           ada_w, ada_b, ln1_g, ln1_b, ln2_g, ln2_b, w1, b1, w2, b2,
           e_w_in, e_b_in, e_conv_w, e_conv_b, e_w_r, e_b_r, e_w_i, e_b_i, e_lam,
           e_w_out, e_b_out, o_w_qkv, o_b_qkv, o_rpb, o_w_out, o_b_out):
    f = lambda a: np.ascontiguousarray(np.asarray(a, np.float32))
    x_prompt, x_sample, c, state_lru, cache_k, cache_v, c_ctx = map(f, (x_prompt, x_sample, c, state_lru, cache_k, cache_v, c_ctx))
    ada_w, ada_b, w1, b1, w2, b2 = map(f, (ada_w, ada_b, w1, b1, w2, b2))
    tabs = _dft_tables()
    if "nc" not in _NC_CACHE:
        _NC_CACHE["nc"] = build()
    nc = _NC_CACHE["nc"]

    shared = {}
    shared["ada_w"] = ada_w
    shared["adabF"] = np.ascontiguousarray(np.stack([_fm(ada_b[l], 48) for l in range(2)], axis=1))
    shared["adabB"] = np.ascontiguousarray(np.stack(
        [np.stack([_bc(ada_b[l, 2048:3072]), _bc(ada_b[l, 5120:6144])]) for l in range(2)]))
    shared["lnB"] = np.ascontiguousarray(np.stack(
        [np.stack([_bc(f(ln1_g)[l]), _bc(f(ln1_b)[l]), _bc(f(ln2_g)[l]), _bc(f(ln2_b)[l])]) for l in range(2)]))
    shared["w1"] = w1
    shared["b1F"] = np.ascontiguousarray(np.stack([_fm(b1[l], 32) for l in range(2)], axis=1))
    shared["w2"] = w2
    shared["b2row"] = b2
    shared["e_w_in"] = f(e_w_in)[0]
    shared["e_b_inF"] = _fm(f(e_b_in)[0], 12)
    shared["convbF"] = _fm(f(e_conv_b)[0], 4)
    shared["e_w_out"] = f(e_w_out)[0]
    shared["ebrow"] = f(e_b_out)[0].reshape(1, 1024)
    shared["o_w_qkv"] = f(o_w_qkv)[0]
    shared["bqkF"] = _fm(f(o_b_qkv)[0][:2048], 16)
    shared["bvB"] = _bc(f(o_b_qkv)[0][2048:])
    shared["o_w_out"] = f(o_w_out)[0]
    shared["obrow"] = f(o_b_out)[0].reshape(1, 1024)
    shared["dftC"] = tabs["C"]
    shared["ident"] = tabs["I"]

    wr, wi = f(e_w_r)[0], f(e_w_i)[0]
    br, bi, lam = f(e_b_r)[0], f(e_b_i)[0], f(e_lam)[0]
    cw = f(e_conv_w)[0]
    rpb = f(o_rpb)[0]
    per_p = []
    for p in range(2):
        dirs = (0, 1) if p == 0 else (1, 0)
        wg = np.zeros((2, 2, 4, 128, 128), np.float32)
        for dl, dg in enumerate(dirs):
            for gi, wsrc in enumerate((wr, wi)):
                for j in range(4):
                    wg[dl, gi, j, 0:64, 0:64] = wsrc[dg, 2 * j]
                    wg[dl, gi, j, 64:128, 64:128] = wsrc[dg, 2 * j + 1]
        bgF = np.empty((128, 2, 2, 4), np.float32)
        lamF = np.empty((128, 2, 4), np.float32)
        for dl, dg in enumerate(dirs):
            bgF[:, dl, 0, :] = _fm(br[dg], 4)
            bgF[:, dl, 1, :] = _fm(bi[dg], 4)
            lamF[:, dl, :] = _fm(lam[dg], 4)
        taps5 = np.zeros((5, 512), np.float32)
        if p == 0:
            taps5[0:4] = cw
        else:
            taps5[1:5] = cw[::-1]
        tapsF = np.ascontiguousarray(taps5.reshape(5, 4, 128).transpose(2, 1, 0))
        per_p.append(dict(wgate=np.ascontiguousarray(wg.reshape(16, 128, 128).transpose(1, 0, 2)), bgateF=bgF, lamF=lamF, taps=tapsF, nabias=_na_bias(rpb, p),
                          dftS=tabs["S%d" % p], dftP=tabs["P%d" % p]))

    in_maps = []
    for core in range(8):
        b, p = core // 2, core % 2
        m = dict(shared)
        m.update(per_p[p])
        xs = x_sample[b]
        xp = x_prompt[2 * core:2 * core + 2]
        if p == 1:
            xs = xs[::-1]
            xp = xp[:, ::-1]
        m["xs"] = np.ascontiguousarray(xs)
        m["xp"] = np.ascontiguousarray(xp.reshape(512, 1024))
        cond = np.stack([c[b], c_ctx])
        m["condT"] = np.ascontiguousarray(cond.T.reshape(8, 128, 2).transpose(1, 0, 2))
        dirs = (0, 1) if p == 0 else (1, 0)
        st = np.empty((128, 8), np.float32)
        for dl, dg in enumerate(dirs):
            st[:, dl::2] = _fm(state_lru[b, 0, dg], 4)
        m["st0"] = st
        m["ckT"] = np.ascontiguousarray(cache_k[b, 0].transpose(0, 2, 1).reshape(1024, 512))
        m["cv"] = np.ascontiguousarray(cache_v[b, 0].transpose(1, 0, 2).reshape(512, 1024))
        in_maps.append(m)

    res = run_bass_kernel_spmd(nc, in_maps, core_ids=list(range(8)))
    R = res.results

    y_prompt = np.empty((16, 256, 1024), np.float32)
    y_sample = np.empty((4, 4096, 1024), np.float32)
    new_state = np.empty((16, 1, 2, 512), np.float32)
    new_k = np.empty((16, 1, 16, 256, 64), np.float32)
    new_v = np.empty((16, 1, 16, 256, 64), np.float32)
    for core in range(8):
        b, p = core // 2, core % 2
        r = R[core]
        ys = np.asarray(r["ys"]); yp = np.asarray(r["yp"]).reshape(2, 256, 1024)
        nkT = np.asarray(r["nkT"]).reshape(16, 64, 2, 256)
        nv = np.asarray(r["nv"]).reshape(2, 256, 16, 64)
        nst = np.asarray(r["nst"])
        nst = nst.transpose(2, 3, 1, 0).reshape(2, 2, 512)
        kk = nkT.transpose(2, 0, 3, 1)
        vv = nv.transpose(0, 2, 1, 3)
        if p == 0:
            y_sample[b, 0:2048] = ys
        else:
            y_sample[b, 2048:4096] = ys[::-1]
            yp = yp[:, ::-1]
            kk = kk[:, :, ::-1]
            vv = vv[:, :, ::-1]
            nst = nst[:, ::-1]
        y_prompt[2 * core:2 * core + 2] = yp
        new_k[2 * core:2 * core + 2, 0] = kk
        new_v[2 * core:2 * core + 2, 0] = vv
        new_state[2 * core:2 * core + 2, 0] = nst
    return (y_prompt, y_sample, new_state, new_k, new_v)
```
